# Optimizing a Trainium2 kernel written in Bass

```python
import jax, jax.numpy as jnp
from jax import lax
import numpy as np

D_MODEL = 1024
BATCH = 4
SEQ = 4096
DEPTH = 1

N_MEM = 256
CONV_WIDTH = 512
CONV_KERNEL = 31
MLA_HEADS = 8
QK_NOPE = 64
QK_ROPE = 32
V_HEAD = 64
Q_LORA = 384
KV_LORA = 256
MLA_WIDTH = MLA_HEADS * V_HEAD
X_HEADS = 4
X_HEAD_DIM = 128
X_WIDTH = X_HEADS * X_HEAD_DIM
N_BRANCH = 3
ROPE_THETA = 10000.0
BLOCK_Q = 128
EPS = 1e-6
IN_SPLITS = (CONV_WIDTH, CONV_WIDTH, CONV_WIDTH, Q_LORA, KV_LORA, QK_ROPE, MLA_WIDTH, X_WIDTH, X_WIDTH, N_BRANCH * D_MODEL)
D_IN = sum(IN_SPLITS)

kernel_name = 'hybrid_conv_mla_xattn_gated_block'


def rms_norm(x, g):
    xf = x.astype(jnp.float32)
    y = xf * lax.rsqrt(jnp.mean(xf * xf, axis=-1, keepdims=True) + EPS)
    return (y * g.astype(jnp.float32)).astype(x.dtype)


def layer_norm(x, g, b):
    xf = x.astype(jnp.float32)
    mu = jnp.mean(xf, axis=-1, keepdims=True)
    var = jnp.mean(jnp.square(xf - mu), axis=-1, keepdims=True)
    y = (xf - mu) * lax.rsqrt(var + EPS)
    return (y * g.astype(jnp.float32) + b.astype(jnp.float32)).astype(x.dtype)


def apply_rope(x, cos, sin):
    xf = x.astype(jnp.float32)
    x1, x2 = jnp.split(xf, 2, axis=-1)
    out = jnp.concatenate([x1 * cos - x2 * sin, x1 * sin + x2 * cos], axis=-1)
    return out.astype(x.dtype)


def conv_branch(val, glu, gate, conv_w, conv_b, ln_g, ln_b, w_o):
    u = val * jax.nn.sigmoid(glu)
    u = lax.conv_general_dilated(
        u, conv_w[:, None, :], window_strides=(1,),
        padding=[(CONV_KERNEL - 1, 0)],
        dimension_numbers=('NWC', 'WIO', 'NWC'),
        feature_group_count=CONV_WIDTH) + conv_b
    u = jax.nn.silu(layer_norm(u, ln_g, ln_b))
    u = u * jax.nn.silu(gate)
    return u @ w_o


def mla_branch(q_down, kv_down, k_rope_raw, gate, cos, sin,
               q_norm_g, w_uq, kv_norm_g, w_ukv, w_o):
    B, S, _ = q_down.shape
    cq = rms_norm(q_down, q_norm_g)
    q = (cq @ w_uq).reshape(B, S, MLA_HEADS, QK_NOPE + QK_ROPE)
    q_nope = q[..., :QK_NOPE]
    q_rope = apply_rope(q[..., QK_NOPE:], cos[:, :, None, :], sin[:, :, None, :])
    ckv = rms_norm(kv_down, kv_norm_g)
    kv = (ckv @ w_ukv).reshape(B, S, MLA_HEADS, QK_NOPE + V_HEAD)
    k_nope = kv[..., :QK_NOPE]
    v = kv[..., QK_NOPE:]
    k_rope = apply_rope(k_rope_raw, cos, sin)
    scale = (QK_NOPE + QK_ROPE) ** -0.5
    n_blk = S // BLOCK_Q
    qn_b = (q_nope * scale).reshape(B, n_blk, BLOCK_Q, MLA_HEADS, QK_NOPE).transpose(1, 0, 2, 3, 4)
    qr_b = (q_rope * scale).reshape(B, n_blk, BLOCK_Q, MLA_HEADS, QK_ROPE).transpose(1, 0, 2, 3, 4)
    k_pos = jnp.arange(S)
    neg = jnp.finfo(jnp.float32).min

    def attend(args):
        qn, qr, blk = args
        s = (jnp.einsum('bqhd,bkhd->bhqk', qn, k_nope)
             + jnp.einsum('bqhr,bkr->bhqk', qr, k_rope)).astype(jnp.float32)
        q_pos = blk * BLOCK_Q + jnp.arange(BLOCK_Q)
        causal = k_pos[None, :] <= q_pos[:, None]
        p = jax.nn.softmax(jnp.where(causal, s, neg), axis=-1).astype(v.dtype)
        return jnp.einsum('bhqk,bkhd->bqhd', p, v)

    o = lax.map(attend, (qn_b, qr_b, jnp.arange(n_blk)))
    o = o.transpose(1, 0, 2, 3, 4).reshape(B, S, MLA_WIDTH)
    return (o * jax.nn.silu(gate)) @ w_o


def cross_branch(xq, gate, mem, mem_norm_g, w_mem_kv, w_o):
    B, S, _ = xq.shape
    M = mem.shape[1]
    q = xq.reshape(B, S, X_HEADS, X_HEAD_DIM)
    mkv = rms_norm(mem, mem_norm_g) @ w_mem_kv
    k = mkv[..., :X_WIDTH].reshape(B, M, X_HEADS, X_HEAD_DIM)
    v = mkv[..., X_WIDTH:].reshape(B, M, X_HEADS, X_HEAD_DIM)
    s = jnp.einsum('bshd,bmhd->bhsm', q, k).astype(jnp.float32) * (X_HEAD_DIM ** -0.5)
    p = jax.nn.softmax(s, axis=-1).astype(v.dtype)
    o = jnp.einsum('bhsm,bmhd->bshd', p, v).reshape(B, S, X_WIDTH)
    return (o * jax.nn.silu(gate)) @ w_o


def setup_inputs(seed: int = 0) -> dict:
    key = jax.random.key(seed)
    ks = jax.random.split(key, 24)
    f32 = jnp.float32

    def nrm(k, shape, fan_in):
        return jax.random.normal(k, shape, f32) * (fan_in ** -0.5)

    def gain(k, shape):
        return 1.0 + 0.02 * jax.random.normal(k, shape, f32)

    x = jax.random.normal(ks[0], (BATCH, SEQ, D_MODEL), f32)
    mem = jax.random.normal(ks[1], (BATCH, N_MEM, D_MODEL), f32)
    offset = jax.random.randint(ks[2], (BATCH, 1), 0, 1024, dtype=jnp.int32)
    positions = (offset + jnp.arange(SEQ, dtype=jnp.int32)[None, :]).astype(jnp.int32)
    return {
        'x': x,
        'mem': mem,
        'positions': positions,
        'norm_g': gain(ks[3], (DEPTH, D_MODEL)),
        'w_in': nrm(ks[4], (DEPTH, D_MODEL, D_IN), D_MODEL),
        'b_gate': 0.01 * jax.random.normal(ks[5], (DEPTH, N_BRANCH * D_MODEL), f32),
        'conv_w': nrm(ks[6], (DEPTH, CONV_KERNEL, CONV_WIDTH), CONV_KERNEL),
        'conv_b': 0.01 * jax.random.normal(ks[7], (DEPTH, CONV_WIDTH), f32),
        'conv_ln_g': gain(ks[8], (DEPTH, CONV_WIDTH)),
        'conv_ln_b': 0.01 * jax.random.normal(ks[9], (DEPTH, CONV_WIDTH), f32),
        'w_conv_o': nrm(ks[10], (DEPTH, CONV_WIDTH, D_MODEL), CONV_WIDTH),
        'q_norm_g': gain(ks[11], (DEPTH, Q_LORA)),
        'w_uq': nrm(ks[12], (DEPTH, Q_LORA, MLA_HEADS * (QK_NOPE + QK_ROPE)), Q_LORA),
        'kv_norm_g': gain(ks[13], (DEPTH, KV_LORA)),
        'w_ukv': nrm(ks[14], (DEPTH, KV_LORA, MLA_HEADS * (QK_NOPE + V_HEAD)), KV_LORA),
        'w_mla_o': nrm(ks[15], (DEPTH, MLA_WIDTH, D_MODEL), MLA_WIDTH),
        'mem_norm_g': gain(ks[16], (DEPTH, D_MODEL)),
        'w_mem_kv': nrm(ks[17], (DEPTH, D_MODEL, 2 * X_WIDTH), D_MODEL),
        'w_x_o': nrm(ks[18], (DEPTH, X_WIDTH, D_MODEL), X_WIDTH),
        'w_out': nrm(ks[19], (DEPTH, D_MODEL, D_MODEL), D_MODEL),
        'final_norm_g': gain(ks[20], (D_MODEL,)),
    }


def reference(x, mem, positions, norm_g, w_in, b_gate, conv_w, conv_b, conv_ln_g,
              conv_ln_b, w_conv_o, q_norm_g, w_uq, kv_norm_g, w_ukv, w_mla_o,
              mem_norm_g, w_mem_kv, w_x_o, w_out, final_norm_g):
    B, S, D = x.shape
    split_idx = np.cumsum(np.array(IN_SPLITS))[:-1].tolist()
    inv_freq = ROPE_THETA ** (-jnp.arange(0, QK_ROPE, 2, dtype=jnp.float32) / QK_ROPE)
    angles = positions.astype(jnp.float32)[..., None] * inv_freq
    cos, sin = jnp.cos(angles), jnp.sin(angles)
    for l in range(DEPTH):
        h = rms_norm(x, norm_g[l])
        z = h @ w_in[l]
        (c_val, c_glu, c_gate, q_down, kv_down, k_rope_raw, m_gate,
         x_q, x_gate, g_logits) = jnp.split(z, split_idx, axis=-1)
        y_conv = conv_branch(c_val, c_glu, c_gate, conv_w[l], conv_b[l],
                             conv_ln_g[l], conv_ln_b[l], w_conv_o[l])
        y_mla = mla_branch(q_down, kv_down, k_rope_raw, m_gate, cos, sin,
                           q_norm_g[l], w_uq[l], kv_norm_g[l], w_ukv[l], w_mla_o[l])
        y_x = cross_branch(x_q, x_gate, mem, mem_norm_g[l], w_mem_kv[l], w_x_o[l])
        g = jax.nn.sigmoid((g_logits + b_gate[l]).reshape(B, S, N_BRANCH, D))
        merged = g[:, :, 0] * y_conv + g[:, :, 1] * y_mla + g[:, :, 2] * y_x
        x = x + merged @ w_out[l]
    return rms_norm(x, final_norm_g)
```

```python
import contextlib
from functools import partial as P
import numpy as np
import concourse.bass as bass
import concourse.mybir as mybir
from concourse.bass_utils import run_bass_kernel_spmd

F32 = mybir.dt.float32
BF16 = mybir.dt.bfloat16
I32 = mybir.dt.int32
AF = mybir.ActivationFunctionType
ALU = mybir.AluOpType

NCORES = 8
D = 1024
SEQ = 4096
CH = 256
OWN = {0: [0, 3, 4, 7, 8, 11, 12, 15], 1: [1, 2, 5, 6, 9, 10, 13, 14]}
OTH = {0: OWN[1], 1: OWN[0]}
NEG = -30000.0
EPS = 1e-6
SCALE_MLA = 96.0 ** -0.5
SCALE_X = 128.0 ** -0.5
SB_BASE = 16512
TWO_PI = float(2 * np.pi)
CW1 = 6.28125
CW2 = float(2 * np.pi - 6.28125)
PI_LO = 3.1415925

VC_BG = 0
VC_CB = 24
VC_LG = 28
VC_LB = 32
VC_QG = 36
VC_KG = 39
VC_IF = 41
VC_PC = 42
VC_PS = 43
VC_CW = 44
VC_NG = 168
VC_MG = 176
VC_WH = 184
NV = 184


STRICT_SAME_ENGINE = True


class Buf:
    __slots__ = ("name", "writers", "readers", "sem", "dcount")

    def __init__(self, name):
        self.name = name
        self.writers = []
        self.readers = []
        self.sem = None
        self.dcount = 0


class Op:
    __slots__ = ("eng", "fn", "deps", "is_dma", "chan", "chan_count", "signal", "count")

    def __init__(self, eng, fn, is_dma=False):
        self.eng = eng
        self.fn = fn
        self.deps = []
        self.is_dma = is_dma
        self.chan = None
        self.chan_count = 0
        self.signal = False
        self.count = 0


class Sched:
    def __init__(self):
        self.ops = []
        self.last = {}
        self.barrier_deps = []
        self.chans = []
        self.chanmap = {}
        self.out_chans = []

    def _dep(self, x, y, kind):
        if y is x:
            return
        if (not y.is_dma) and (not x.is_dma) and y.eng == x.eng:
            if x.eng == "pe":
                return
            if kind != "RAW" and not STRICT_SAME_ENGINE:
                return
        x.deps.append(y)
        if not y.is_dma:
            y.signal = True

    def _add(self, x, reads, writes, shared):
        for y in self.barrier_deps:
            self._dep(x, y, "RAW")
        for b in reads:
            for w in b.writers:
                self._dep(x, w, "RAW")
        for b in writes:
            for w in b.writers:
                self._dep(x, w, "WAW")
            for r in b.readers:
                self._dep(x, r, "WAR")
        for b in shared:
            if b.readers:
                for w in b.writers:
                    self._dep(x, w, "WAW")
                for r in b.readers:
                    self._dep(x, r, "WAR")
        for b in reads:
            b.readers.append(x)
        for b in writes:
            b.writers = [x]
            b.readers = []
        for b in shared:
            if b.readers:
                b.writers = [x]
                b.readers = []
            else:
                b.writers.append(x)
        self.ops.append(x)
        self.last[x.eng if not x.is_dma else ("dma", id(x.chan))] = x

    def op(self, eng, fn, reads=(), writes=(), shared=()):
        x = Op(eng, fn)
        self._add(x, reads, writes, shared)
        return x

    def dma(self, eng, fn, reads=(), writes=(), shared=(), chan=None, is_out=False):
        x = Op(eng, fn, is_dma=True)
        if chan is None:
            chan = (list(writes) + list(shared) + list(reads))[0]
        key = (id(chan), eng)
        if key not in self.chanmap:
            self.chanmap[key] = Buf("chan_%s_%s" % (chan.name, eng))
            self.chans.append(self.chanmap[key])
        chan = self.chanmap[key]
        x.chan = chan
        chan.dcount += 1
        x.chan_count = chan.dcount
        if is_out and chan not in self.out_chans:
            self.out_chans.append(chan)
        self._add(x, reads, writes, shared)
        return x

    def barrier(self):
        self.barrier_deps = list(self.last.values())
        for y in self.barrier_deps:
            if not y.is_dma:
                y.signal = True

    def emit(self, nc, es):
        engs = {"pe": nc.tensor, "act": nc.scalar, "dve": nc.vector, "pool": nc.gpsimd, "sp": nc.sync}
        sems = {}
        for e in ("pe", "act", "dve", "pool"):
            sems[e] = es.enter_context(nc.semaphore("s_" + e))
        for i, c in enumerate(self.chans):
            c.sem = es.enter_context(nc.semaphore("d%d" % i))
        cnt = {e: 0 for e in sems}
        for x in self.ops:
            if not x.is_dma and x.signal:
                cnt[x.eng] += 1
                x.count = cnt[x.eng]
        known = {e: {} for e in engs}
        nwait = 0
        for x in self.ops:
            e = engs[x.eng]
            need = {}
            for y in x.deps:
                if y.is_dma:
                    key, val = y.chan.sem, 16 * y.chan_count
                else:
                    key, val = sems[y.eng], y.count
                k = id(key)
                if k not in need or need[k][1] < val:
                    need[k] = (key, val)
            kn = known[x.eng]
            for k, (key, val) in need.items():
                if kn.get(k, 0) < val:
                    e.wait_ge(key, val)
                    kn[k] = val
                    nwait += 1
            ins = x.fn()
            if x.is_dma:
                ins.then_inc(x.chan.sem, 16)
            elif x.signal:
                ins.then_inc(sems[x.eng], 1)
        for c in self.out_chans:
            nc.sync.wait_ge(c.sem, 16 * c.dcount)
        return nwait


class Arena:
    def __init__(self, nc, lo, hi, tag):
        self.nc, self.lo, self.hi, self.tag = nc, lo, hi, tag
        self.off = lo
        self.n = 0

    def alloc(self, shape, dtype, name="t"):
        nbytes = int(np.prod(shape[1:])) * (2 if dtype == BF16 else 4)
        nbytes = (nbytes + 63) // 64 * 64
        assert self.off + nbytes <= self.hi, (self.tag, name, self.off, nbytes, self.hi)
        t = self.nc.alloc_sbuf_tensor_at("%s_%s%d" % (self.tag, name, self.n), list(shape), dtype,
                                         offset=SB_BASE + self.off)
        self.off += nbytes
        self.n += 1
        return t


class PsumRing:
    def __init__(self, ps, banks):
        self.ps = ps
        self.banks = list(banks)
        self.bufs = {b: Buf("ps%d" % b) for b in self.banks}
        self.i = 0

    def next(self):
        b = self.banks[self.i % len(self.banks)]
        self.i += 1
        return self.ps[:, b, :], self.bufs[b]


CSTOP = 9


def build_program(debug=None, phases="ABC"):
    nc = bass.Bass("TRN2", target_bir_lowering=False)
    S = Sched()

    def din(name, shape, dt=F32):
        return nc.dram_tensor(name, list(shape), dt, kind="ExternalInput").ap()

    x_all = din("x_all", [4096, D])
    x_ext = din("x_ext", [4, 576, D])
    pos_all = din("pos_all", [4096], I32)
    mem_in = din("mem", [256, D])
    vecs_in = din("vecs", [128, NV])
    gb_in = din("norm_g", [D])
    fg_in = din("final_g", [D])
    mg_in = din("mem_g", [D])
    ident_in = din("ident", [128, 128])
    maskd_in = din("maskd", [128, 2, 256])
    maskf_in = din("maskf", [128, 8, 256])
    w_a_in = din("w_a", [D, 832])
    w_uqa_in = din("w_uqa", [384, 768])
    w_uqb_in = din("w_uqb", [384, 768])
    w_uk_in = din("w_uk", [256, 512])
    w_uv_in = din("w_uv", [256, 512])
    w_c_in = din("w_c", [D, 7168])
    w_conv_o_in = din("w_conv_o", [512, D])
    w_mla_o_in = din("w_mla_o", [512, D])
    w_x_o_in = din("w_x_o", [512, D])
    w_out_in = din("w_out", [D, D])
    w_mkv_in = din("w_mkv", [D, D])
    out = nc.dram_tensor("out", [2048, D], F32, kind="ExternalOutput").ap()
    dbg = {}
    if debug:
        for nm, shp in debug.items():
            dbg[nm] = nc.dram_tensor("dbg_" + nm, list(shp), F32, kind="ExternalOutput").ap()

    ps = nc.alloc_psum_tensor("ps", [128, 8, 512], F32)

    P_ = Arena(nc, 0, 164352, "P")
    KT = P_.alloc([128, 8, 4096], BF16, "KT")
    VA = P_.alloc([128, 32, 8, 65], BF16, "VA")
    QT = P_.alloc([128, 8, 2048], BF16, "QT")
    OG = P_.alloc([128, 4, 8, 512], BF16, "OG")
    CST = Arena(nc, 208768, 212864, "C")
    ident = CST.alloc([128, 128], BF16, "ident")
    ones_f = CST.alloc([128, 128], F32, "ones_f")
    ones_b = CST.alloc([128, 128], BF16, "ones_b")
    vecs = CST.alloc([128, NV], F32, "vecs")
    hb = CST.alloc([128, 24], F32, "hb")
    zero_c = CST.alloc([128, 1], F32, "zero")
    eps_c = CST.alloc([128, 1], F32, "eps")
    b_const = Buf("consts")
    b_KT = [Buf("KT%d" % t) for t in range(8)]
    b_VA = [Buf("VA%d" % t) for t in range(8)]
    b_QT = [Buf("QT%d" % t) for t in range(4)]
    b_OG = [Buf("OG%d" % t) for t in range(4)]

    def dump(name, ap_sb, bufs, dst=None, cast=False):
        if name in dbg and cast:
            S.dma("pool", P(nc.gpsimd.dma_start, out=(dst if dst is not None else dbg[name]), in_=ap_sb),
                  reads=bufs, chan=Buf("dbg_" + name), is_out=True)
        elif name in dbg:
            S.dma("sp", P(nc.sync.dma_start, out=(dst if dst is not None else dbg[name]), in_=ap_sb),
                  reads=bufs, chan=Buf("dbg_" + name), is_out=True)

    S.dma("pool", P(nc.gpsimd.dma_start, out=ident[:], in_=ident_in[:, :]), shared=[b_const])
    S.dma("sp", P(nc.sync.dma_start, out=vecs[:], in_=vecs_in[:, :]), shared=[b_const])
    S.op("dve", P(nc.vector.memset, ones_f[:], 1.0), shared=[b_const])
    S.op("dve", P(nc.vector.memset, ones_b[:], 1.0), shared=[b_const])
    S.op("dve", P(nc.vector.memset, zero_c[:], 0.0), shared=[b_const])
    S.op("dve", P(nc.vector.memset, eps_c[:], EPS), shared=[b_const])
    b_hb = Buf("hb")
    S.op("dve", P(nc.vector.tensor_scalar, out=hb[:], in0=vecs[:, VC_BG:VC_BG + 24], scalar1=0.5,
                                                scalar2=None, op0=ALU.mult), reads=[b_const], writes=[b_hb])

    wc_bf = nc.dram_tensor("wc_bf", [D, 7168], BF16, kind="Internal").ap()
    wo_bf = nc.dram_tensor("wo_bf", [3, 512, D], BF16, kind="Internal").ap()
    wmkv_bf = nc.dram_tensor("wmkv_bf", [D, D], BF16, kind="Internal").ap()
    b_wcv = {("c", ck): Buf("wcv_c%d" % ck) for ck in range(14)}
    b_wcv.update({("o", i): Buf("wcv_o%d" % i) for i in range(3)})
    b_wcv.update({("m", ck): Buf("wcv_m%d" % ck) for ck in range(2)})

    def convert_weights():
        def cv_c(ck):
            S.dma("pool", P(nc.gpsimd.dma_start, out=wc_bf[:, ck * 512:(ck + 1) * 512],
                            in_=w_c_in[:, ck * 512:(ck + 1) * 512]), writes=[b_wcv[("c", ck)]])

        def cv_o(i):
            win_ = (w_x_o_in, w_mla_o_in, w_conv_o_in)[i]
            S.dma("pool", P(nc.gpsimd.dma_start, out=wo_bf[i], in_=win_[:, :]), writes=[b_wcv[("o", i)]])
        for ck in range(2):
            S.dma("pool", P(nc.gpsimd.dma_start, out=wmkv_bf[:, ck * 512:(ck + 1) * 512],
                            in_=w_mkv_in[:, ck * 512:(ck + 1) * 512]), writes=[b_wcv[("m", ck)]])
        for ck in (0, 1, 5, 6, 2, 9):
            cv_c(ck)
        cv_o(0)
        cv_c(7)
        cv_c(8)
        cv_o(1)
        cv_c(10)
        cv_c(11)
        cv_o(2)
        for ck in (3, 4, 12, 13):
            cv_c(ck)

    if "A" in phases:
        A_ = Arena(nc, 131584, 208768, "A")
        ckv = A_.alloc([128, 2, 4096], BF16, "ckv")
        cq = A_.alloc([128, 3, 2048], BF16, "cq")
        cs = A_.alloc([128, 4096], BF16, "cs")
        sn = A_.alloc([128, 4096], BF16, "sn")
        w_uqa = A_.alloc([128, 3, 768], BF16, "w_uqa")
        w_uqb = A_.alloc([128, 3, 768], BF16, "w_uqb")
        w_uk = A_.alloc([128, 2, 512], BF16, "w_uk")
        w_uv = A_.alloc([128, 2, 512], BF16, "w_uv")
        t1q = [A_.alloc([128, 512], F32, "t1q") for _ in range(2)]
        t2q = [A_.alloc([128, 512], F32, "t2q") for _ in range(2)]
        A1_ = Arena(nc, 65536, 131584, "A1")
        w_a = A1_.alloc([128, 8, 832], BF16, "w_a")
        xb = [A1_.alloc([128, D], F32, "xb") for _ in range(4)]
        xn = [A1_.alloc([128, D], BF16, "xn") for _ in range(2)]
        xnT = [A1_.alloc([128, 8, 512], BF16, "xnT") for _ in range(2)]
        a1_mark = A1_.off
        sqb = [A1_.alloc([128, 3, 512], BF16, "sqb") for _ in range(2)]
        rsb = [A1_.alloc([128, 512], F32, "rsb") for _ in range(2)]
        t1 = A1_.alloc([128, 512], F32, "t1")
        t2 = A1_.alloc([128, 512], F32, "t2")
        kr = [A_.alloc([128, 512], BF16, "kr") for _ in range(2)]
        st = A_.alloc([128, 16], F32, "st")
        A2_ = Arena(nc, a1_mark, 131584, "A2")
        rp_i = A2_.alloc([128, 1024], I32, "rp_i")
        rp_a = A2_.alloc([128, 1024], F32, "rp_a")
        rp_n = A2_.alloc([128, 1024], F32, "rp_n")
        rp_o = A2_.alloc([128, 1024], BF16, "rp_o")
        b_w, b_w0, b_w2 = Buf("wA"), Buf("w_a_raw"), Buf("wA2")
        b_cs = Buf("cs")
        b_xb = [Buf("xb%d" % i) for i in range(4)]
        b_xn = [Buf("xn0"), Buf("xn1")]
        b_xnT = [Buf("xnT0"), Buf("xnT1")]
        b_sq, b_rs = [Buf("sq0"), Buf("sq1")], [Buf("rs0"), Buf("rs1")]
        b_t1, b_t2, b_kr = Buf("t1"), Buf("t2"), [Buf("kr0"), Buf("kr1")]
        b_stl = [Buf("st%d" % i) for i in range(8)]
        b_rpi, b_rpa, b_rpn, b_rpo = Buf("rpi"), Buf("rpa"), Buf("rpn"), Buf("rpo")
        b_ckv = [Buf("ckv%d" % t) for t in range(8)]
        b_cq = [Buf("cq%d" % t) for t in range(4)]
        b_t1q, b_t2q = [Buf("t1q0"), Buf("t1q1")], [Buf("t2q0"), Buf("t2q1")]

        S.dma("pool", P(nc.gpsimd.dma_start, out=w_a[:], in_=w_a_in.rearrange("(kc p) n -> p kc n", p=128)),
              writes=[b_w0])
        for wt_, win_ in ((w_uk, w_uk_in), (w_uv, w_uv_in), (w_uqa, w_uqa_in), (w_uqb, w_uqb_in)):
            S.dma("pool", P(nc.gpsimd.dma_start, out=wt_[:], in_=win_.rearrange("(kc p) n -> p kc n", p=128)),
                  shared=[b_w2])
        for g in range(4):
            S.dma("sp", P(nc.sync.dma_start, out=rp_i[32 * g:32 * g + 32, :],
                          in_=pos_all[g * 1024:(g + 1) * 1024].partition_broadcast(32)), shared=[b_rpi])
        for tbl, pcol in ((cs, VC_PC), (sn, VC_PS)):
            S.op("dve", P(nc.vector.tensor_copy, rp_a[:], rp_i[:]), reads=[b_rpi], writes=[b_rpa])
            S.op("dve", P(nc.vector.tensor_scalar, out=rp_a[:], in0=rp_a[:], scalar1=vecs[:, VC_IF:VC_IF + 1],
                          scalar2=vecs[:, pcol:pcol + 1], op0=ALU.mult, op1=ALU.add),
                 reads=[b_rpa, b_const], writes=[b_rpa])
            S.op("dve", P(nc.vector.tensor_scalar, out=rp_n[:], in0=rp_a[:], scalar1=1.0 / TWO_PI,
                          scalar2=None, op0=ALU.mult), reads=[b_rpa], writes=[b_rpn])
            S.op("dve", P(nc.vector.tensor_copy, rp_i[:], rp_n[:]), reads=[b_rpn], writes=[b_rpi])
            S.op("dve", P(nc.vector.tensor_copy, rp_n[:], rp_i[:]), reads=[b_rpi], writes=[b_rpn])
            if tbl is cs:
                for g in range(4):
                    S.dma("sp", P(nc.sync.dma_start, out=rp_i[32 * g:32 * g + 32, :],
                                  in_=pos_all[g * 1024:(g + 1) * 1024].partition_broadcast(32)), shared=[b_rpi])
            S.op("dve", P(nc.vector.scalar_tensor_tensor, out=rp_a[:], in0=rp_n[:], scalar=-CW1,
                          in1=rp_a[:], op0=ALU.mult, op1=ALU.add), reads=[b_rpn, b_rpa], writes=[b_rpa])
            S.op("dve", P(nc.vector.scalar_tensor_tensor, out=rp_a[:], in0=rp_n[:], scalar=-CW2,
                          in1=rp_a[:], op0=ALU.mult, op1=ALU.add), reads=[b_rpn, b_rpa], writes=[b_rpa])
            S.op("dve", P(nc.vector.tensor_scalar, out=rp_a[:], in0=rp_a[:], scalar1=-PI_LO, scalar2=PI_LO,
                          op0=ALU.max, op1=ALU.min), reads=[b_rpa], writes=[b_rpa])
            S.op("act", P(nc.scalar.activation, out=rp_o[:], in_=rp_a[:], func=AF.Sin, bias=zero_c[:, :], scale=1.0),
                 reads=[b_rpa, b_const], writes=[b_rpo])
            for g in range(4):
                S.dma("sp", P(nc.sync.dma_start, out=tbl[64:96, g * 1024:(g + 1) * 1024], in_=rp_o[32 * g:32 * g + 32, :]),
                      reads=[b_rpo], shared=[b_cs], chan=b_cs)
        for kc in range(8):
            S.op("dve", P(nc.vector.tensor_scalar, out=w_a[:, kc, :], in0=w_a[:, kc, :],
                          scalar1=vecs[:, VC_NG + kc:VC_NG + kc + 1], scalar2=None, op0=ALU.mult),
                 reads=[b_w0, b_const], shared=[b_w])
        S.barrier()

        R = slice(64, 96)
        ringT = PsumRing(ps, [0, 1])
        ringP = PsumRing(ps, [2, 3, 4, 5, 6, 7])
        evac_i = [0]

        def evac_copy(out_ap, in_ap, reads, writes=(), shared=(), scale=None):
            evac_i[0] += 1
            if evac_i[0] % 2 == 0:
                if scale is None:
                    S.op("act", P(nc.scalar.copy, out=out_ap, in_=in_ap), reads=reads, writes=writes, shared=shared)
                else:
                    S.op("act", P(nc.scalar.mul, out=out_ap, in_=in_ap, mul=scale), reads=reads, writes=writes,
                         shared=shared)
            else:
                if scale is None:
                    S.op("dve", P(nc.vector.tensor_copy, out_ap, in_ap), reads=reads, writes=writes, shared=shared)
                else:
                    S.op("dve", P(nc.vector.tensor_scalar, out=out_ap, in0=in_ap, scalar1=scale, scalar2=None,
                                  op0=ALU.mult), reads=reads, writes=writes, shared=shared)

        ngrp = [0]

        def front1(t):
            tok0 = t * 512
            b_st = b_stl[t % 2]
            c0_ = (t % 2) * 4
            for blk in range(4):
                r0 = tok0 + blk * 128
                S.dma("sp", P(nc.sync.dma_start, out=xb[blk][:], in_=x_all[r0:r0 + 128, :]), writes=[b_xb[blk]])
            for blk in range(4):
                xni, bxn = xn[blk % 2], b_xn[blk % 2]
                S.op("act", P(nc.scalar.activation, out=xni[:], in_=xb[blk][:], func=AF.Square,
                              accum_out=st[:, c0_ + blk:c0_ + blk + 1]), reads=[b_xb[blk]], writes=[bxn], shared=[b_st])
            S.op("act", P(nc.scalar.activation, out=st[:, 8 + c0_:12 + c0_], in_=st[:, c0_:c0_ + 4], func=AF.Ln,
                          bias=eps_c[:, 0:1], scale=1.0 / D), reads=[b_st, b_const], writes=[b_st])
            S.op("act", P(nc.scalar.activation, out=st[:, 8 + c0_:12 + c0_], in_=st[:, 8 + c0_:12 + c0_], func=AF.Exp,
                          scale=-0.5), reads=[b_st], writes=[b_st])

        def front2(t):
            xT, bxT = xnT[t % 2], b_xnT[t % 2]
            b_st = b_stl[t % 2]
            c0_ = (t % 2) * 4
            for blk in range(4):
                xni, bxn = xn[blk % 2], b_xn[blk % 2]
                S.op("dve", P(nc.vector.tensor_scalar, out=xni[:], in0=xb[blk][:],
                              scalar1=st[:, 8 + c0_ + blk:9 + c0_ + blk], scalar2=None, op0=ALU.mult),
                     reads=[b_xb[blk], b_st], writes=[bxn])
                pT, bT = ringT.next()
                pTb = pT.bitcast(BF16)
                for kc in range(8):
                    S.op("pe", P(nc.tensor.transpose, out=pTb[:, kc * 128:(kc + 1) * 128],
                                 in_=xni[:, kc * 128:(kc + 1) * 128], identity=ident[:]),
                         reads=[bxn, b_const], shared=[bT])
                evac_copy(xT[:, :, blk * 128:(blk + 1) * 128], pTb.rearrange("p (k n) -> p k n", k=8),
                          reads=[bT], shared=[bxT])

        front1(0)
        front2(0)
        for t in range(8):
            own = t < 4
            tok0 = t * 512
            xT, bxT = xnT[t % 2], b_xnT[t % 2]
            if t + 1 < 8:
                front1(t + 1)

            def proj(c0, m, xT=xT, bxT=bxT):
                p_, b_ = ringP.next()
                for kc in range(8):
                    S.op("pe", P(nc.tensor.matmul, p_[0:m, :], w_a[:, kc, c0:c0 + m], xT[:, kc, :],
                                 start=(kc == 0), stop=(kc == 7)), reads=[b_w, bxT], shared=[b_])
                return p_, b_

            def norm_group(pbs, nch, gcol, div, dst, bdst):
                gi = ngrp[0] % 2
                ngrp[0] += 1
                sq_, bsq, rs_, brs = sqb[gi], b_sq[gi], rsb[gi], b_rs[gi]
                for i, (p_, b_) in enumerate(pbs):
                    S.op("act", P(nc.scalar.activation, out=sq_[:, i, :], in_=p_, func=AF.Square),
                         reads=[b_], shared=[bsq])
                pss, bss = ringP.next()
                for i in range(nch):
                    S.op("pe", P(nc.tensor.matmul, pss, ones_b[:], sq_[:, i, :], start=(i == 0), stop=(i == nch - 1)),
                         reads=[bsq, b_const], shared=[bss])
                S.op("act", P(nc.scalar.activation, out=rs_[:], in_=pss, func=AF.Ln, bias=eps_c[:, 0:1],
                              scale=1.0 / div), reads=[bss, b_const], writes=[brs])
                S.op("act", P(nc.scalar.activation, out=rs_[:], in_=rs_[:], func=AF.Exp, scale=-0.5),
                     reads=[brs], writes=[brs])
                for i, (p_, b_) in enumerate(pbs):
                    S.op("dve", P(nc.vector.scalar_tensor_tensor, out=dst(i), in0=p_,
                                  scalar=vecs[:, gcol + i:gcol + i + 1], in1=rs_[:], op0=ALU.mult, op1=ALU.mult),
                         reads=[b_, brs, b_const], shared=[bdst])

            kvp = [proj(384 + 128 * i, 128) for i in range(2)]
            pA, bA = proj(640, 96)
            pB, bB = proj(736, 96)
            if t + 1 < 8:
                front2(t + 1)
            S.op("dve", P(nc.vector.tensor_tensor, out=t1[R, :], in0=pA[R, :], in1=cs[R, tok0:tok0 + 512],
                          op=ALU.mult), reads=[bA, b_cs], writes=[b_t1])
            S.op("dve", P(nc.vector.tensor_tensor, out=t2[R, :], in0=pB[R, :], in1=sn[R, tok0:tok0 + 512],
                          op=ALU.mult), reads=[bB, b_cs], writes=[b_t2])
            kr_, bkr = kr[t % 2], b_kr[t % 2]
            S.op("dve", P(nc.vector.tensor_tensor, out=kr_[R, :], in0=t1[R, :], in1=t2[R, :], op=ALU.add),
                 reads=[b_t1, b_t2], writes=[bkr])
            for h in range(8):
                if h % 2 == 0:
                    S.op("act", P(nc.scalar.copy, out=KT[R, h, tok0:tok0 + 512], in_=kr_[R, :]),
                         reads=[bkr], shared=[b_KT[t]])
                else:
                    S.op("dve", P(nc.vector.tensor_copy, KT[R, h, tok0:tok0 + 512], kr_[R, :]),
                         reads=[bkr], shared=[b_KT[t]])
            norm_group(kvp, 2, VC_KG, 256.0, lambda i, tok0=tok0: ckv[:, i, tok0:tok0 + 512], b_ckv[t])
            if own:
                qp = [proj(128 * i, 128) for i in range(3)]
                norm_group(qp, 3, VC_QG, 384.0, lambda i, tok0=tok0: cq[:, i, tok0:tok0 + 512], b_cq[t])
        S.barrier()
        S.op("pool", P(nc.gpsimd.memset, VA[:, :, :, 64:65], 1.0), shared=b_VA)
        ringQ = PsumRing(ps, [0, 1, 2, 3, 4, 5, 6, 7])
        nq = [0]
        for t in range(8):
            own = t < 4
            tok0 = t * 512
            for h in range(8):
                p_, b_ = ringQ.next()
                for kc in range(2):
                    S.op("pe", P(nc.tensor.matmul, p_[0:64, :], w_uk[:, kc, h * 64:(h + 1) * 64],
                                 ckv[:, kc, tok0:tok0 + 512], start=(kc == 0), stop=(kc == 1)),
                         reads=[b_w2, b_ckv[t]], shared=[b_])
                evac_copy(KT[0:64, h, tok0:tok0 + 512], p_[0:64, :], reads=[b_], shared=[b_KT[t]])
            for blk in range(4):
                p_, b_ = ringQ.next()
                for kc in range(2):
                    S.op("pe", P(nc.tensor.matmul, p_, ckv[:, kc, tok0 + blk * 128:tok0 + (blk + 1) * 128],
                                 w_uv[:, kc, :], start=(kc == 0), stop=(kc == 1)),
                         reads=[b_w2, b_ckv[t]], shared=[b_])
                evac_copy(VA[:, t * 4 + blk, :, 0:64], p_.rearrange("p (h d) -> p h d", h=8), reads=[b_],
                          shared=[b_VA[t]])
            if own:
                for h in range(8):
                    pa, ba = ringQ.next()
                    for kc in range(3):
                        S.op("pe", P(nc.tensor.matmul, pa[0:96, :], w_uqa[:, kc, h * 96:(h + 1) * 96],
                                     cq[:, kc, tok0:tok0 + 512], start=(kc == 0), stop=(kc == 2)),
                             reads=[b_w2, b_cq[t]], shared=[ba])
                    pb, bb = ringQ.next()
                    for kc in range(3):
                        S.op("pe", P(nc.tensor.matmul, pb[0:96, :], w_uqb[:, kc, h * 96:(h + 1) * 96],
                                     cq[:, kc, tok0:tok0 + 512], start=(kc == 0), stop=(kc == 2)),
                             reads=[b_w2, b_cq[t]], shared=[bb])
                    qi = nq[0] % 2
                    nq[0] += 1
                    S.op("act", P(nc.scalar.mul, out=QT[0:64, h, tok0:tok0 + 512], in_=pa[0:64, :],
                                  mul=SCALE_MLA), reads=[ba], shared=[b_QT[t]])
                    S.op("dve", P(nc.vector.scalar_tensor_tensor, out=t1q[qi][R, :], in0=pa[R, :], scalar=SCALE_MLA,
                                  in1=cs[R, tok0:tok0 + 512], op0=ALU.mult, op1=ALU.mult),
                         reads=[ba, b_cs], writes=[b_t1q[qi]])
                    S.op("dve", P(nc.vector.scalar_tensor_tensor, out=t2q[qi][R, :], in0=pb[R, :], scalar=SCALE_MLA,
                                  in1=sn[R, tok0:tok0 + 512], op0=ALU.mult, op1=ALU.mult),
                         reads=[bb, b_cs], writes=[b_t2q[qi]])
                    S.op("pool", P(nc.gpsimd.tensor_tensor, out=QT[R, h, tok0:tok0 + 512], in0=t1q[qi][R, :],
                                   in1=t2q[qi][R, :], op=ALU.add),
                         reads=[b_t1q[qi], b_t2q[qi]], shared=[b_QT[t]])
        S.barrier()
        if debug and ("kt" in dbg or "va" in dbg or "qt" in dbg):
            dstage = Arena(nc, 131584, 208768, "DBG").alloc([128, 2080], F32, "dstage")
            b_ds = Buf("dstage")
            if "kt" in dbg:
                for i, h in enumerate((0, 5)):
                    for half in range(2):
                        S.op("dve", P(nc.vector.tensor_copy, dstage[0:96, 0:2048], KT[0:96, h, half * 2048:(half + 1) * 2048]),
                             reads=b_KT, writes=[b_ds])
                        dump("kt", dstage[0:96, 0:2048], [b_ds], dst=dbg["kt"][i, :, half * 2048:(half + 1) * 2048])
            if "va" in dbg:
                S.op("dve", P(nc.vector.tensor_copy, dstage[:, 0:2080], VA[:, 0:4, :, :].rearrange("p a h d -> p (a h d)")),
                     reads=b_VA, writes=[b_ds])
                dump("va", dstage[:, 0:2080], [b_ds])
            if "qt" in dbg:
                S.op("dve", P(nc.vector.tensor_copy, dstage[0:96, 0:2048], QT[0:96, 3, :]), reads=b_QT, writes=[b_ds])
                dump("qt", dstage[0:96, 0:2048], [b_ds])
            S.barrier()

    if "B" in phases:
        B_ = Arena(nc, 164352, 208768, "B")
        maskd = B_.alloc([128, 2, 256], BF16, "maskd")
        maskf = B_.alloc([128, 8, 256], BF16, "maskf")
        NPT = 6
        PT = [B_.alloc([128, 512], BF16, "PT") for _ in range(NPT)]
        b_PT = [Buf("PT%d" % i) for i in range(NPT)]
        o_sb = [B_.alloc([128, 512], F32, "o_sb") for _ in range(2)]
        rden = [B_.alloc([128, 512], F32, "rden") for _ in range(2)]
        b_osb = [Buf("osb0"), Buf("osb1")]
        b_rden = [Buf("rden0"), Buf("rden1")]
        b_mask = Buf("masks")
        S.dma("pool", P(nc.gpsimd.dma_start, out=maskd[:], in_=maskd_in[:, :, :]), shared=[b_mask])
        S.dma("pool", P(nc.gpsimd.dma_start, out=maskf[:], in_=maskf_in[:, :, :]), shared=[b_mask])
        convert_weights()
        ringS = PsumRing(ps, [3, 4, 5, 6, 7])
        b_O = [Buf("O0"), Buf("O1")]
        b_bc = Buf("bc")
        items = []
        for T in range(4):
            for h in range(8):
                blocks = []
                for grp in range(2):
                    for j in range(2 * T + 2):
                        c = j + 8 * grp
                        for kb in range(2):
                            if j < 2 * T:
                                blocks.append((c, kb, 0, 512, None))
                            elif j == 2 * T:
                                m = maskd[:, kb, :] if grp == 0 else maskf[:, j, :]
                                blocks.append((c, kb, 0, 512, (m, 0)))
                            else:
                                m = maskd[:, kb, :] if grp == 0 else maskf[:, j, :]
                                blocks.append((c, kb, 256, 512, (m, 256)))
                for bi, blk in enumerate(blocks):
                    items.append((T, h, blk, bi == 0, bi == len(blocks) - 1))
        N = len(items)
        LOOK = 3
        st_ = {}

        def emit_qk(i):
            T, h, (c, kb, qlo, qhi, mask), first, last = items[i]
            sp_, sb_ = ringS.next()
            kcol = c * 256 + kb * 128
            S.op("pe", P(nc.tensor.matmul, sp_[:, qlo:qhi], KT[0:96, h, kcol:kcol + 128],
                         QT[0:96, h, T * 512 + qlo:T * 512 + qhi], start=True, stop=(mask is None)),
                 reads=[b_KT[c // 2], b_QT[T]], shared=[sb_])
            if mask is not None:
                m, mlo = mask
                S.op("pe", P(nc.tensor.matmul, sp_[:, mlo:mlo + 256], ident[:], m, start=False, stop=True),
                     reads=[b_mask, b_const], shared=[sb_])
            pt = PT[i % NPT]
            S.op("act", P(nc.scalar.activation, out=pt[:, qlo:qhi], in_=sp_[:, qlo:qhi], func=AF.Exp),
                 reads=[sb_], writes=[b_PT[i % NPT]])

        def emit_pv(i):
            T, h, (c, kb, qlo, qhi, mask), first, last = items[i]
            ob = (T * 8 + h) % 2
            O = ps[:, ob, :]
            pt = PT[i % NPT]
            S.op("pe", P(nc.tensor.matmul, O[0:65, qlo:qhi], VA[:, c * 2 + kb, h, :], pt[:, qlo:qhi],
                         start=first, stop=last),
                 reads=[b_VA[c // 2], b_PT[i % NPT]], shared=[b_O[ob]])
            if last:
                S.op("dve", P(nc.vector.tensor_copy, o_sb[ob][0:65, :], O[0:65, :]), reads=[b_O[ob]],
                     writes=[b_osb[ob]])
                S.op("dve", P(nc.vector.reciprocal, out=rden[ob][64:65, :], in_=o_sb[ob][64:65, :]),
                     reads=[b_osb[ob]], writes=[b_rden[ob]])
                pending.append((i + FIN_DELAY, T, h, ob))

        def emit_fin(T, h, ob):
            bc = ps[:, 2, :]
            S.op("pe", P(nc.tensor.matmul, bc[0:64, :], ones_f[64:65, 0:64], rden[ob][64:65, :],
                         start=True, stop=True), reads=[b_rden[ob], b_const], writes=[b_bc])
            S.op("dve", P(nc.vector.tensor_tensor, out=OG[0:64, T, h, :], in0=o_sb[ob][0:64, :],
                          in1=bc[0:64, :], op=ALU.mult), reads=[b_osb[ob], b_bc], shared=[b_OG[T]])

        pending = []
        FIN_DELAY = 7
        for i in range(N + LOOK):
            if i < N:
                emit_qk(i)
            if i >= LOOK:
                emit_pv(i - LOOK)
            while pending and pending[0][0] <= i - LOOK:
                _, T_, h_, ob_ = pending.pop(0)
                emit_fin(T_, h_, ob_)
        for _, T_, h_, ob_ in pending:
            emit_fin(T_, h_, ob_)
        S.barrier()
        if debug and "og" in dbg:
            dstage2 = Arena(nc, 0, 65536, "DBG2").alloc([128, 2048], F32, "dstage2")
            b_ds2 = Buf("dstage2")
            for i, h in enumerate((0, 6)):
                for T_ in range(4):
                    S.op("dve", P(nc.vector.tensor_copy, dstage2[0:64, T_ * 512:(T_ + 1) * 512], OG[0:64, T_, h, :]),
                         reads=b_OG, shared=[b_ds2])
                dump("og", dstage2[0:64, :], [b_ds2], dst=dbg["og"][i])
            S.barrier()

    if "C" in phases:
        C_ = Arena(nc, 0, 131584, "C1")
        C2_ = Arena(nc, 164352, 208768, "C2")
        fgt = C_.alloc([128, D], F32, "fgt")
        KM = C_.alloc([128, 4, 256], BF16, "KM")
        VM = C_.alloc([128, 2, 512], BF16, "VM")
        wh = C_.alloc([128, 124], F32, "wh")
        epsC = C_.alloc([128, 1], F32, "epsC")
        diag = C_.alloc([128, 124, 128], BF16, "diag")
        NWC = 4
        wc, wc_o = [], []
        for i in range(NWC):
            off_i = C_.off
            wc.append(C_.alloc([128, 8, 512], BF16, "wc"))
            wc_o.append(Arena(nc, off_i, off_i + 8192, "WO%d" % i).alloc([128, 4, 1024], BF16, "wo"))
        b_wc = [Buf("wc%d" % i) for i in range(NWC)]
        xbC = [C_.alloc([128, D], F32, "xbC") for _ in range(2)]
        b_xbC = [Buf("xbC0"), Buf("xbC1")]
        xnC = C_.alloc([128, D], BF16, "xnC")
        xo_ = [C_.alloc([128, 8, 512], BF16, "xo0"),
               Arena(nc, 131584, 131584 + 8192, "XO1").alloc([128, 8, 512], BF16, "xo1")]
        xh_ = [C_.alloc([128, 8, 64], BF16, "xh0"), C_.alloc([128, 8, 64], BF16, "xh1")]
        stC = C_.alloc([128, 16], F32, "stC")
        u2 = C_.alloc([128, 4, 2, 288], BF16, "u2")
        accd_off = C_.off
        acc_d = C_.alloc([128, 4, 512], F32, "acc_d")
        mean_sb = C_.alloc([128, 512], F32, "mean")
        rstd_sb = C_.alloc([128, 512], F32, "rstd")
        ug = C_.alloc([128, 4, 512], BF16, "ug")
        merged2 = C_.alloc([128, 8, 512], F32, "merged2")
        tg = [C2_.alloc([128, 512], F32, "tg") for _ in range(2)]
        sgb = [C2_.alloc([128, 512], BF16, "sg") for _ in range(4)]
        sgc = [C2_.alloc([128, 512], BF16, "sgc") for _ in range(4)]
        mbf_off = C2_.off
        xq = C2_.alloc([128, 4, 512], BF16, "xq")
        oxg = C2_.alloc([128, 4, 512], BF16, "oxg")
        merged_bf = Arena(nc, mbf_off, mbf_off + 8192, "MBF").alloc([128, 8, 512], BF16, "merged_bf")
        PTx = [C2_.alloc([128, 512], BF16, "PTx") for _ in range(2)]
        rr = C2_.alloc([128, 512], F32, "rr")
        tmpf = [C2_.alloc([128, 512], F32, "tmpf") for _ in range(2)]
        m2_sb = tmpf[0]
        ogp = C2_.alloc([128, 4, 512], BF16, "ogp")
        outb = [C2_.alloc([128, D], F32, "outb") for _ in range(2)]
        b_outb = [Buf("outb0"), Buf("outb1")]
        xnC2 = C2_.alloc([128, D], BF16, "xnC2")
        b_wres = Buf("wres")
        b_diag = Buf("diag")
        b_kvm = Buf("kvm")
        b_xnC = Buf("xnC")
        xnL, b_xnL = [xnC, xnC2], [b_xnC, Buf("xnC2")]
        b_xn_ = [Buf("xnTC0"), b_OG[0]]
        cur = {}

        def set_cur(T):
            cur["xo"], cur["xh"], cur["b"] = xo_[T % 2], xh_[T % 2], b_xn_[T % 2]
        b_stC = [Buf("stC%d" % i) for i in range(5)]
        b_u2 = Buf("u2")
        b_tg = [Buf("tg0"), Buf("tg1")]
        b_accd = [Buf("accd%d" % i) for i in range(4)]
        b_mean, b_rstd = Buf("mean"), Buf("rstd")
        b_sg = [Buf("sg%d" % i) for i in range(4)]
        b_sgc = [Buf("sgc%d" % i) for i in range(4)]
        b_ug, b_mg2, b_mbf = Buf("ug"), [Buf("mg2_%d" % i) for i in range(8)], Buf("mbf")
        b_xq, b_PTx, b_rr = Buf("xq"), [Buf("PTx0"), Buf("PTx1")], Buf("rr")
        b_tmpf = [Buf("tmpf0"), Buf("tmpf1")]
        b_m2 = b_tmpf[0]
        b_oxg, b_ogp, b_res = Buf("oxg"), Buf("ogp"), Buf("res")
        b_fin = Buf("fin_st")

        S.dma("sp", P(nc.sync.dma_start, out=fgt[:], in_=fg_in.partition_broadcast(128)), shared=[b_wres])
        S.op("dve", P(nc.vector.memset, epsC[:], EPS), shared=[b_wres])
        b_wh = Buf("wh")
        S.op("dve", P(nc.vector.tensor_scalar, out=wh[:], in0=vecs[:, VC_CW:VC_CW + 124], scalar1=0.5, scalar2=None,
                      op0=ALU.mult), reads=[b_const], writes=[b_wh])
        b_diagm = [Buf("diag%d" % i) for i in range(4)]

        def build_diag(mc):
            for i in range(mc * 31, (mc + 1) * 31):
                if i % 2 == 0:
                    S.op("dve", P(nc.vector.tensor_scalar, out=diag[:, i, :], in0=ident[:], scalar1=wh[:, i:i + 1],
                                  scalar2=None, op0=ALU.mult), reads=[b_wh, b_const], shared=[b_diagm[mc]])
                else:
                    S.op("act", P(nc.scalar.mul, out=diag[:, i, :], in_=ident[:], mul=wh[:, i:i + 1]),
                         reads=[b_wh, b_const], shared=[b_diagm[mc]])

        ringC = PsumRing(ps, [0, 1, 2, 3, 4, 5, 6, 7])

        def wsrc(ap):
            return ap.rearrange("(kc p) n -> p kc n", p=128)

        def WC(a):
            return ("w", wsrc(wc_bf[:, a:a + 512]), b_wcv[("c", a // 512)])
        names = ["xq", "xg", "cg", "mg", "wxo", "g2a", "g2b", "wmo", "g1a", "g1b", "wco", "g0a", "g0b"]
        srcs = {"val": WC(0), "glu": WC(512), "xq": WC(2560), "xg": WC(3072), "cg": WC(1024), "mg": WC(4608),
                "wxo": ("o", wsrc(wo_bf[0]), b_wcv[("o", 0)]), "g2a": WC(3584), "g2b": WC(4096), "wmo": ("o", wsrc(wo_bf[1]), b_wcv[("o", 1)]),
                "g1a": WC(5120), "g1b": WC(5632), "wco": ("o", wsrc(wo_bf[2]), b_wcv[("o", 2)]), "g0a": WC(1536), "g0b": WC(2048),
                "woa": WC(6144), "wob": WC(6656)}
        chunk_src = [("w", wsrc(wmkv_bf[:, 0:512]), b_wcv[("m", 0)]), ("w", wsrc(wmkv_bf[:, 512:1024]), b_wcv[("m", 1)])]
        CI = [dict() for _ in range(4)]

        def add_chunk(T, nm):
            CI[T][nm] = len(chunk_src)
            chunk_src.append(srcs[nm])
        add_chunk(0, "val")
        add_chunk(0, "glu")
        for T in range(4):
            for nm in names:
                add_chunk(T, nm)
            if T + 1 < 4:
                add_chunk(T + 1, "val")
                add_chunk(T + 1, "glu")
            add_chunk(T, "woa")
            add_chunk(T, "wob")
        issued = [0]
        consumed = set()

        def prefetch():
            low = 0
            while low in consumed:
                low += 1
            while issued[0] < min(low + NWC, len(chunk_src)):
                i = issued[0]
                kind, src, bsrc = chunk_src[i]
                dst = wc[i % NWC] if kind == "w" else wc_o[i % NWC]
                S.dma("pool", P(nc.gpsimd.dma_start, out=dst[:], in_=src), reads=[bsrc], writes=[b_wc[i % NWC]],
                      chan=b_wc[i % NWC])
                issued[0] += 1

        def chunk(i):
            assert i < issued[0], (i, issued[0])
            kind = chunk_src[i][0]
            return (wc[i % NWC] if kind == "w" else wc_o[i % NWC]), b_wc[i % NWC]

        def done(*idx):
            for i in idx:
                consumed.add(i)
            prefetch()

        prefetch()

        def rstd_chain(ss_ap, out_ap, buf, div):
            S.op("dve", P(nc.vector.tensor_scalar, out=out_ap, in0=ss_ap, scalar1=1.0 / div, scalar2=EPS, op0=ALU.mult,
                          op1=ALU.add), reads=[buf], shared=[buf])
            S.op("act", P(nc.scalar.sqrt, out=out_ap, in_=out_ap), reads=[buf], shared=[buf])
            S.op("dve", P(nc.vector.reciprocal, out=out_ap, in_=out_ap), reads=[buf], shared=[buf])

        b_dst = [None]
        nT_i = [0]

        def norm_T(x_sb, bx, np_, stb, sti, g_col, dst_cols):
            xnC, b_xnC = xnL[sti % 2], b_xnL[sti % 2]
            S.op("act", P(nc.scalar.activation, out=acc_d[:, 2:4, :].rearrange("p a n -> p (a n)")[0:np_, :],
                          in_=x_sb[0:np_, :], func=AF.Square, accum_out=stC[0:np_, sti:sti + 1]),
                 reads=[bx], writes=[b_accd[2], b_accd[3]], shared=[stb])
            rstd_chain(stC[0:np_, sti:sti + 1], stC[0:np_, 8 + sti:9 + sti], stb, float(D))
            S.op("dve", P(nc.vector.tensor_scalar, out=xnC[0:np_, :], in0=x_sb[0:np_, :],
                          scalar1=stC[0:np_, 8 + sti:9 + sti], scalar2=None, op0=ALU.mult), reads=[bx, stb], writes=[b_xnC])
            pT, bT = ringC.next()
            pTb = pT.bitcast(BF16)
            for kc in range(8):
                S.op("pe", P(nc.tensor.transpose, out=pTb[:, kc * np_:(kc + 1) * np_], in_=xnC[0:np_, kc * 128:(kc + 1) * 128],
                             identity=ident[0:np_, 0:np_]), reads=[b_xnC, b_const], shared=[bT])
            nT_i[0] += 1
            for kc in range(8):
                if nT_i[0] % 2 == 0:
                    S.op("dve", P(nc.vector.tensor_scalar, out=dst_cols(kc), in0=pTb[:, kc * np_:(kc + 1) * np_],
                                  scalar1=vecs[:, g_col + kc:g_col + kc + 1], scalar2=None, op0=ALU.mult),
                         reads=[bT, b_const], shared=[b_dst[0]])
                else:
                    S.op("act", P(nc.scalar.mul, out=dst_cols(kc), in_=pTb[:, kc * np_:(kc + 1) * np_],
                                  mul=vecs[:, g_col + kc:g_col + kc + 1]), reads=[bT, b_const], shared=[b_dst[0]])

        def proj(wt, wb, c0, m):
            p_, b_ = ringC.next()
            for kc in range(8):
                S.op("pe", P(nc.tensor.matmul, p_[0:m, :], wt[:, kc, c0:c0 + m], cur["xo"][:, kc, :],
                             start=(kc == 0), stop=(kc == 7)), reads=[wb, cur["b"]], shared=[b_])
            return p_, b_

        memT = Arena(nc, accd_off, accd_off + 4096, "MEMT").alloc([128, 8, 256], BF16, "memT")
        b_memT = Buf("memT")
        b_dst[0] = b_memT
        for mb in range(2):
            S.dma("sp", P(nc.sync.dma_start, out=xbC[mb][:], in_=mem_in[mb * 128:(mb + 1) * 128, :]), writes=[b_xbC[mb]])
            norm_T(xbC[mb], b_xbC[mb], 128, b_stC[mb], mb, VC_MG,
                   lambda kc, mb=mb: memT[:, kc, mb * 128:(mb + 1) * 128])
        wk_t, wk_b = chunk(0)
        for hx in range(4):
            p_, b_ = ringC.next()
            for kc in range(8):
                S.op("pe", P(nc.tensor.matmul, p_[:, 0:256], wk_t[:, kc, hx * 128:(hx + 1) * 128], memT[:, kc, :],
                             start=(kc == 0), stop=(kc == 7)), reads=[wk_b, b_memT, b_accd[0], b_accd[1]], shared=[b_])
            S.op("dve", P(nc.vector.tensor_copy, KM[:, hx, :], p_[:, 0:256]), reads=[b_], shared=[b_kvm])
        done(0)
        wv_t, wv_b = chunk(1)
        for mb in range(2):
            p_, b_ = ringC.next()
            for kc in range(8):
                S.op("pe", P(nc.tensor.matmul, p_, memT[:, kc, mb * 128:(mb + 1) * 128], wv_t[:, kc, :],
                             start=(kc == 0), stop=(kc == 7)), reads=[wv_b, b_memT, b_accd[0], b_accd[1]], shared=[b_])
            S.op("act", P(nc.scalar.copy, out=VM[:, mb, :], in_=p_), reads=[b_], shared=[b_kvm])
        done(1)

        def gate_merge_steps(cg, co, bias_col, rhs_t, rhs_b, mode):
            def step(fo):
                wo_t, wo_b = chunk(co)
                wt, wb = chunk(cg + fo // 4)
                py, by = ringC.next()
                for kc in range(4):
                    S.op("pe", P(nc.tensor.matmul, py, wo_t[:, kc, fo * 128:(fo + 1) * 128], rhs_t[:, kc, :],
                                 start=(kc == 0), stop=(kc == 3)), reads=[wo_b, rhs_b], shared=[by])
                pz, bz = proj(wt, wb, (fo % 4) * 128, 128)
                tt, tb = tg[fo % 2], b_tg[fo % 2]
                S.op("act", P(nc.scalar.activation, out=tt[:], in_=pz, func=AF.Tanh,
                              bias=hb[:, bias_col + fo:bias_col + fo + 1], scale=0.5), reads=[bz, b_hb], writes=[tb])
                if mode == "first":
                    S.op("dve", P(nc.vector.scalar_tensor_tensor, out=merged2[:, fo, :], in0=tt[:], scalar=1.0, in1=py,
                                  op0=ALU.add, op1=ALU.mult), reads=[tb, by], writes=[b_mg2[fo]])
                else:
                    tf, tfb = tmpf[fo % 2], b_tmpf[fo % 2]
                    S.op("dve", P(nc.vector.scalar_tensor_tensor, out=tf[:], in0=tt[:], scalar=1.0, in1=py,
                                  op0=ALU.add, op1=ALU.mult), reads=[tb, by], writes=[tfb])
                    if mode == "mid":
                        S.op("dve", P(nc.vector.tensor_tensor, out=merged2[:, fo, :], in0=merged2[:, fo, :], in1=tf[:],
                                       op=ALU.add), reads=[tfb, b_mg2[fo]], writes=[b_mg2[fo]])
                    else:
                        S.op("dve", P(nc.vector.tensor_tensor, out=merged_bf[:, fo, :], in0=merged2[:, fo, :], in1=tf[:],
                                       op=ALU.add), reads=[tfb, b_mg2[fo]], shared=[b_mbf, b_xq, b_oxg])
                if fo == 3:
                    done(cg)
                if fo == 7:
                    done(cg + 1, co)
            return [P(step, fo) for fo in range(8)]

        def run_steps(a, b=(), ratio=1):
            a, b = list(a), list(b)
            ia = ib = 0
            while ia < len(a) or ib < len(b):
                for _ in range(ratio):
                    if ia < len(a):
                        a[ia]()
                        ia += 1
                if ib < len(b):
                    b[ib]()
                    ib += 1

        def c0_blk(T, b):
            if b == 0:
                return 64, x_ext[T, 0:64, :], (lambda kc, T=T: xh_[T % 2][:, kc, :])
            return 128, x_ext[T, 64 + (b - 1) * 128:64 + b * 128, :], \
                (lambda kc, T=T, b=b: xo_[T % 2][:, kc, (b - 1) * 128:b * 128])

        junk_f = acc_d[:, 2:4, :].rearrange("p a n -> p (a n)")

        def c0_p1(T, b):
            np_, src, _ = c0_blk(T, b)
            xs, bxs, stb, sti = xbC[b % 2], b_xbC[b % 2], b_stC[b], b
            xnC, b_xnC = xnL[b % 2], b_xnL[b % 2]
            S.dma("sp", P(nc.sync.dma_start, out=xs[0:np_, :], in_=src), writes=[bxs])
            if T == 0:
                S.op("act", P(nc.scalar.activation, out=junk_f[0:np_, :], in_=xs[0:np_, :], func=AF.Square,
                              accum_out=stC[0:np_, sti:sti + 1]), reads=[bxs], writes=[b_accd[2], b_accd[3]], shared=[stb])
            else:
                S.op("act", P(nc.scalar.activation, out=xnC[0:np_, :], in_=xs[0:np_, :], func=AF.Square,
                              accum_out=stC[0:np_, sti:sti + 1]), reads=[bxs], writes=[b_xnC], shared=[stb])
            rstd_chain(stC[0:np_, sti:sti + 1], stC[0:np_, 8 + sti:9 + sti], stb, float(D))
            S.op("dve", P(nc.vector.tensor_scalar, out=xnC[0:np_, :], in0=xs[0:np_, :],
                          scalar1=stC[0:np_, 8 + sti:9 + sti], scalar2=None, op0=ALU.mult), reads=[bxs, stb], writes=[b_xnC])

        def c0_p2(T, b):
            np_, _, dst_cols = c0_blk(T, b)
            bdst = b_xn_[T % 2]
            xnC, b_xnC = xnL[b % 2], b_xnL[b % 2]
            pT, bT = ringC.next()
            pTb = pT.bitcast(BF16)
            for kc in range(8):
                S.op("pe", P(nc.tensor.transpose, out=pTb[:, kc * np_:(kc + 1) * np_], in_=xnC[0:np_, kc * 128:(kc + 1) * 128],
                             identity=ident[0:np_, 0:np_]), reads=[b_xnC, b_const], shared=[bT])
            nT_i[0] += 1
            for kc in range(8):
                if nT_i[0] % 2 == 0:
                    S.op("dve", P(nc.vector.tensor_scalar, out=dst_cols(kc), in0=pTb[:, kc * np_:(kc + 1) * np_],
                                  scalar1=vecs[:, VC_NG + kc:VC_NG + kc + 1], scalar2=None, op0=ALU.mult),
                         reads=[bT, b_const], shared=[bdst])
                else:
                    S.op("act", P(nc.scalar.mul, out=dst_cols(kc), in_=pTb[:, kc * np_:(kc + 1) * np_],
                                  mul=vecs[:, VC_NG + kc:VC_NG + kc + 1]), reads=[bT, b_const], shared=[bdst])

        def hoist(T, step):
            if T + 1 >= 4 or CSTOP < 5:
                return
            if step >= 1:
                c0_p2(T + 1, step - 1)
            if step <= 4:
                c0_p1(T + 1, step)

        def x_reload(T, blk):
            xs, bxs = xbC[blk % 2], b_xbC[blk % 2]
            S.dma("sp", P(nc.sync.dma_start, out=xs[:], in_=x_ext[T, 64 + blk * 128:64 + (blk + 1) * 128, :]),
                  writes=[bxs])

        def s2(T):
            set_cur(T)
            wval, bval = chunk(CI[T]["val"])
            wglu, bglu = chunk(CI[T]["glu"])
            ph, bh = ringC.next()
            for gi, (wt, wb) in enumerate(((wval, bval), (wglu, bglu))):
                for mc in range(4):
                    for kc in range(8):
                        S.op("pe", P(nc.tensor.matmul, ph[:, gi * 256 + mc * 64:gi * 256 + (mc + 1) * 64],
                                     wt[:, kc, mc * 128:(mc + 1) * 128], cur["xh"][:, kc, :], start=(kc == 0), stop=(kc == 7)),
                             reads=[wb, cur["b"]], shared=[bh])
            S.op("act", P(nc.scalar.activation, out=tg[0][:, 0:256], in_=ph[:, 256:512], func=AF.Tanh, scale=0.5),
                 reads=[bh], writes=[b_tg[0]])
            for mc in range(4):
                S.op("dve", P(nc.vector.scalar_tensor_tensor, out=u2[:, mc, :, 0:32],
                              in0=tg[0][:, mc * 64:(mc + 1) * 64].rearrange("p (c i) -> p c i", c=2), scalar=1.0,
                              in1=ph[:, mc * 64:(mc + 1) * 64].rearrange("p (c i) -> p c i", c=2),
                              op0=ALU.add, op1=ALU.mult), reads=[b_tg[0], bh], shared=[b_u2])
            for mc in range(4):
                pv, bv = proj(wval, bval, mc * 128, 128)
                pg, bg = proj(wglu, bglu, mc * 128, 128)
                tt, tb = tg[(mc + 1) % 2], b_tg[(mc + 1) % 2]
                S.op("act", P(nc.scalar.activation, out=tt[:], in_=pg, func=AF.Tanh, scale=0.5), reads=[bg], writes=[tb])
                S.op("dve", P(nc.vector.scalar_tensor_tensor, out=u2[:, mc, :, 32:288],
                              in0=tt[:].rearrange("p (c i) -> p c i", c=2), scalar=1.0,
                              in1=pv.rearrange("p (c i) -> p c i", c=2), op0=ALU.add, op1=ALU.mult),
                     reads=[tb, bv], shared=[b_u2])
            done(CI[T]["val"], CI[T]["glu"])


        for T in range(4 if CSTOP >= 5 else (1 if CSTOP >= 1 else 0)):
            set_cur(T)
            if T == 0:
                c0_p1(0, 0)
                for b in range(5):
                    if b + 1 < 5:
                        c0_p1(0, b + 1)
                    c0_p2(0, b)
            S.dma("sp", P(nc.sync.dma_start, out=ogp[0:64, :, :], in_=OG[0:64, T, 0:8:2, :]),
                  reads=[b_OG[T]], shared=[b_ogp], chan=b_ogp)
            S.dma("sp", P(nc.sync.dma_start, out=ogp[64:128, :, :], in_=OG[0:64, T, 1:8:2, :]),
                  reads=[b_OG[T]], shared=[b_ogp], chan=b_ogp)
            if CSTOP < 2:
                continue
            if T == 0:
                s2(0)
            set_cur(T)
            wxq, bxq = chunk(CI[T]["xq"])
            for hx in range(4):
                p_, b_ = proj(wxq, bxq, hx * 128, 128)
                S.op("act", P(nc.scalar.mul, out=xq[:, hx, :], in_=p_, mul=SCALE_X), reads=[b_], shared=[b_xq])
            done(CI[T]["xq"])
            wxg, bxg = chunk(CI[T]["xg"])
            for hx in range(4):
                p_, b_ = proj(wxg, bxg, hx * 128, 128)
                S.op("act", P(nc.scalar.activation, out=sgb[hx][:], in_=p_, func=AF.Silu), reads=[b_], writes=[b_sg[hx]])
            done(CI[T]["xg"])
            hoist(T, 0)
            wcg, bcg = chunk(CI[T]["cg"])
            for mc in range(4):
                pc, bc_ = proj(wcg, bcg, mc * 128, 128)
                S.op("act", P(nc.scalar.activation, out=sgc[mc][:], in_=pc, func=AF.Silu), reads=[bc_], writes=[b_sgc[mc]])
            done(CI[T]["cg"])
            wmg, bmg = chunk(CI[T]["mg"])
            for mc in range(4):
                p_, b_ = proj(wmg, bmg, mc * 128, 128)
                tf, tfb = tmpf[mc % 2], b_tmpf[mc % 2]
                S.op("act", P(nc.scalar.activation, out=tf[:], in_=p_, func=AF.Silu), reads=[b_], writes=[tfb])
                S.op("dve", P(nc.vector.tensor_tensor, out=ogp[:, mc, :], in0=ogp[:, mc, :], in1=tf[:], op=ALU.mult),
                     reads=[b_ogp, tfb], writes=[b_ogp])
            done(CI[T]["mg"])
            hoist(T, 1)

            def conv_step(mc, T=T):
                if T == 0 and mc == 0:
                    build_diag(0)
                if T == 0 and mc + 1 < 4:
                    build_diag(mc + 1)
                pcv, bcv = ringC.next()
                for chn in range(2):
                    for k in range(31):
                        S.op("pe", P(nc.tensor.matmul, pcv[:, chn * 256:(chn + 1) * 256], diag[:, mc * 31 + k, :],
                                     u2[:, mc, chn, 2 + k:2 + k + 256], start=(k == 0), stop=(k == 30)),
                             reads=[b_diagm[mc], b_u2], shared=[bcv])
                S.op("act", P(nc.scalar.activation, out=acc_d[:, mc, :], in_=pcv, func=AF.Identity,
                              bias=vecs[:, VC_CB + mc:VC_CB + mc + 1], scale=1.0), reads=[bcv, b_const],
                     writes=[b_accd[mc]])

            def cross_step(hx):
                po, bo = ringC.next()
                pd, bd = ringC.next()
                for mb in range(2):
                    psx, bsx = ringC.next()
                    S.op("pe", P(nc.tensor.matmul, psx, KM[:, hx, mb * 128:(mb + 1) * 128], xq[:, hx, :], start=True,
                                 stop=True), reads=[b_kvm, b_xq], shared=[bsx])
                    S.op("act", P(nc.scalar.activation, out=PTx[mb][:], in_=psx, func=AF.Exp), reads=[bsx],
                         writes=[b_PTx[mb]])
                    S.op("pe", P(nc.tensor.matmul, po, VM[:, mb, hx * 128:(hx + 1) * 128], PTx[mb][:], start=(mb == 0),
                                 stop=(mb == 1)), reads=[b_kvm, b_PTx[mb]], shared=[bo])
                    S.op("pe", P(nc.tensor.matmul, pd, ones_b[:], PTx[mb][:], start=(mb == 0), stop=(mb == 1)),
                         reads=[b_const, b_PTx[mb]], shared=[bd])
                S.op("dve", P(nc.vector.reciprocal, out=rr[:], in_=pd), reads=[bd], writes=[b_rr])
                tf, tfb = tmpf[hx % 2], b_tmpf[hx % 2]
                S.op("dve", P(nc.vector.tensor_tensor, out=tf[:], in0=po, in1=rr[:], op=ALU.mult), reads=[bo, b_rr],
                     writes=[tfb])
                S.op("dve", P(nc.vector.tensor_tensor, out=oxg[:, hx, :], in0=tf[:], in1=sgb[hx][:], op=ALU.mult),
                     reads=[tfb, b_sg[hx]], shared=[b_oxg])

            if T == 0:
                run_steps([P(cross_step, hx) for hx in range(4)], [P(conv_step, mc) for mc in range(4)])
            else:
                run_steps([P(conv_step, mc) for mc in range(4)], [P(cross_step, hx) for hx in range(4)])
            if T == 0:
                dump("cconv", acc_d[:].rearrange("p a n -> p (a n)"), b_accd)
            hoist(T, 2)
            p1, b1 = ringC.next()
            for mc in range(4):
                S.op("pe", P(nc.tensor.matmul, p1, ones_f[:], acc_d[:, mc, :], start=(mc == 0), stop=(mc == 3)),
                     reads=[b_accd[mc], b_const], shared=[b1])
            p2, b2 = ringC.next()
            for mc in range(4):
                tf, tfb = tmpf[mc % 2], b_tmpf[mc % 2]
                S.op("act", P(nc.scalar.activation, out=tf[:], in_=acc_d[:, mc, :], func=AF.Square),
                     reads=[b_accd[mc]], writes=[tfb])
                S.op("pe", P(nc.tensor.matmul, p2, ones_f[:], tf[:], start=(mc == 0), stop=(mc == 3)),
                     reads=[tfb, b_const], shared=[b2])
            S.op("dve", P(nc.vector.tensor_scalar, out=mean_sb[:], in0=p1, scalar1=1.0 / 512, scalar2=None, op0=ALU.mult),
                 reads=[b1], writes=[b_mean])
            S.op("dve", P(nc.vector.tensor_tensor, out=rr[:], in0=mean_sb[:], in1=mean_sb[:], op=ALU.mult),
                 reads=[b_mean], writes=[b_rr])
            S.op("dve", P(nc.vector.scalar_tensor_tensor, out=rstd_sb[:], in0=p2, scalar=1.0 / 512, in1=rr[:],
                          op0=ALU.mult, op1=ALU.subtract), reads=[b2, b_rr], writes=[b_rstd])
            S.op("act", P(nc.scalar.activation, out=rstd_sb[:], in_=rstd_sb[:], func=AF.Sqrt, bias=epsC[:, 0:1], scale=1.0),
                 reads=[b_rstd, b_wres], writes=[b_rstd])
            S.op("dve", P(nc.vector.reciprocal, out=rstd_sb[:], in_=rstd_sb[:]), reads=[b_rstd], writes=[b_rstd])

            def ln_step(mc):
                S.op("dve", P(nc.vector.tensor_tensor, out=acc_d[:, mc, :], in0=acc_d[:, mc, :], in1=mean_sb[:],
                              op=ALU.subtract), reads=[b_accd[mc], b_mean], writes=[b_accd[mc]])
                S.op("dve", P(nc.vector.tensor_tensor, out=acc_d[:, mc, :], in0=acc_d[:, mc, :], in1=rstd_sb[:],
                               op=ALU.mult), reads=[b_accd[mc], b_rstd], writes=[b_accd[mc]])
                S.op("act", P(nc.scalar.activation, out=acc_d[:, mc, :], in_=acc_d[:, mc, :], func=AF.Silu,
                              bias=vecs[:, VC_LB + mc:VC_LB + mc + 1], scale=vecs[:, VC_LG + mc:VC_LG + mc + 1]),
                     reads=[b_accd[mc], b_const], writes=[b_accd[mc]])
                S.op("dve", P(nc.vector.tensor_tensor, out=ug[:, mc, :], in0=acc_d[:, mc, :], in1=sgc[mc][:], op=ALU.mult),
                     reads=[b_accd[mc], b_sgc[mc]], shared=[b_ug])

            run_steps(gate_merge_steps(CI[T]["g2a"], CI[T]["wxo"], 16, oxg, b_oxg, "first"),
                      [P(ln_step, mc) for mc in range(4)], ratio=2)
            if T == 0:
                dump("m1", merged2[:].rearrange("p a n -> p (a n)"), b_mg2)
            hoist(T, 3)
            st10 = gate_merge_steps(CI[T]["g1a"], CI[T]["wmo"], 8, ogp, b_ogp, "mid")
            run_steps(st10[0:4])
            hoist(T, 4)
            run_steps(st10[4:8])
            hoist(T, 5)
            if CSTOP >= 5:
                x_reload(T, 0)
                x_reload(T, 1)
            run_steps(gate_merge_steps(CI[T]["g0a"], CI[T]["wco"], 0, ug, b_ug, "last"))
            if CSTOP < 5:
                continue
            if T + 1 < 4:
                s2(T + 1)
            wo = [chunk(CI[T]["woa"]), chunk(CI[T]["wob"])]
            for blk in range(4):
                xs, bxs = xbC[blk % 2], b_xbC[blk % 2]
                ob, bob = outb[blk % 2], b_outb[blk % 2]
                for half in range(2):
                    pf, bf = ringC.next()
                    wt, wb = wo[half]
                    for kc in range(8):
                        S.op("pe", P(nc.tensor.matmul, pf, merged_bf[:, kc, blk * 128:(blk + 1) * 128], wt[:, kc, :],
                                     start=(kc == 0), stop=(kc == 7)), reads=[b_mbf, b_xq, b_oxg, wb], shared=[bf])
                    S.op("dve", P(nc.vector.scalar_tensor_tensor, out=ob[:, half * 512:(half + 1) * 512], in0=pf, scalar=0.5,
                                  in1=xs[:, half * 512:(half + 1) * 512], op0=ALU.mult, op1=ALU.add),
                         reads=[bf, bxs], shared=[bob])
                if blk + 2 < 4:
                    x_reload(T, blk + 2)
                sti = 5 + blk % 2
                S.op("act", P(nc.scalar.activation, out=acc_d[:, 0:2, :].rearrange("p a n -> p (a n)"), in_=ob[:],
                              func=AF.Square, accum_out=stC[:, sti:sti + 1]), reads=[bob],
                     writes=[b_accd[0], b_accd[1]], shared=[b_fin])
                rstd_chain(stC[:, sti:sti + 1], stC[:, 8 + sti:9 + sti], b_fin, float(D))
                S.op("dve", P(nc.vector.scalar_tensor_tensor, out=ob[:], in0=ob[:], scalar=stC[:, 8 + sti:9 + sti],
                              in1=fgt[:], op0=ALU.mult, op1=ALU.mult), reads=[b_fin, b_wres, bob], writes=[bob])
                r0 = T * 512 + blk * 128
                S.dma("sp", P(nc.sync.dma_start, out=out[r0:r0 + 128, :], in_=ob[:]), reads=[bob], chan=bob, is_out=True)
            done(CI[T]["woa"], CI[T]["wob"])

    nwait = S.emit(nc, ES)
    return nc, nwait


ES = None


def make_inputs(c, x, mem, positions, norm_g, w_in, b_gate, conv_w, conv_b, conv_ln_g, conv_ln_b, w_conv_o,
                q_norm_g, w_uq, kv_norm_g, w_ukv, w_mla_o, mem_norm_g, w_mem_kv, w_x_o, w_out, final_norm_g,
                shared):
    b, p = c // 2, c % 2
    own, oth = OWN[p], OTH[p]
    order = own + oth
    xb = x[b]
    x_all = np.concatenate([xb[g * CH:(g + 1) * CH] for g in order], axis=0)
    pos_all = np.concatenate([positions[b, g * CH:(g + 1) * CH] for g in order], axis=0).astype(np.int32)
    x_ext = np.zeros((4, 576, D), np.float32)
    for t in range(4):
        for jj in range(2):
            g = own[2 * t + jj]
            if g > 0:
                x_ext[t, jj * 32:(jj + 1) * 32] = xb[g * CH - 32:g * CH]
            x_ext[t, 64 + jj * 256:64 + (jj + 1) * 256] = xb[g * CH:(g + 1) * CH]
    maskf = np.zeros((128, 8, 256), np.float32)
    for j in range(8):
        if not (oth[j] < own[j]):
            maskf[:, j, :] = NEG
    d = dict(shared)
    d.update(x_all=np.ascontiguousarray(x_all), x_ext=x_ext, pos_all=pos_all,
             mem=np.ascontiguousarray(mem[b]), maskf=maskf)
    return d


def make_shared(norm_g, w_in, b_gate, conv_w, conv_b, conv_ln_g, conv_ln_b, w_conv_o, q_norm_g, w_uq, kv_norm_g,
                w_ukv, w_mla_o, mem_norm_g, w_mem_kv, w_x_o, w_out, final_norm_g):
    f = np.float32
    w_in0 = w_in[0]
    vecs = np.zeros((128, NV), f)
    vecs[:, VC_BG:VC_BG + 24] = b_gate[0].reshape(24, 128).T
    vecs[:, VC_CB:VC_CB + 4] = conv_b[0].reshape(4, 128).T
    vecs[:, VC_LG:VC_LG + 4] = conv_ln_g[0].reshape(4, 128).T
    vecs[:, VC_LB:VC_LB + 4] = conv_ln_b[0].reshape(4, 128).T
    vecs[:, VC_QG:VC_QG + 3] = q_norm_g[0].reshape(3, 128).T
    vecs[:, VC_KG:VC_KG + 2] = kv_norm_g[0].reshape(2, 128).T
    inv_freq = (10000.0 ** (-np.arange(0, 32, 2, dtype=np.float32) / 32)).astype(f)
    vecs[:, VC_IF] = np.tile(inv_freq, 8)
    vecs[:, VC_PC] = np.pi / 2
    vecs[:, VC_PS] = np.tile(np.concatenate([np.full(16, np.pi), np.zeros(16)]), 4)
    vecs[:, VC_NG:VC_NG + 8] = norm_g[0].reshape(8, 128).T
    vecs[:, VC_MG:VC_MG + 8] = mem_norm_g[0].reshape(8, 128).T
    vecs[:, VC_CW:VC_CW + 124] = conv_w[0].T.reshape(4, 128, 31).transpose(1, 0, 2).reshape(128, 124)
    rope = np.arange(2176, 2208)
    rope_sw = np.concatenate([rope[16:], rope[:16]])
    junk = np.arange(1920, 1984)
    cols_a = np.concatenate([np.arange(1536, 1920), np.arange(1920, 2176), junk, rope, junk, rope_sw])
    w_a = np.ascontiguousarray(w_in0[:, cols_a])
    uq = w_uq[0]
    cols_b = []
    for h in range(8):
        base = h * 96
        cols_b += list(range(base, base + 64)) + list(range(base + 80, base + 96)) + list(range(base + 64, base + 80))
    w_uqb = np.ascontiguousarray(uq[:, cols_b])
    ukv = w_ukv[0].reshape(256, 8, 128)
    w_uk = np.ascontiguousarray(ukv[:, :, :64].reshape(256, 512))
    w_uv = np.ascontiguousarray(ukv[:, :, 64:].reshape(256, 512))
    seg = lambda a, n: np.arange(a, a + n)
    cols_c = np.concatenate([seg(0, 512), seg(512, 512), seg(1024, 512), seg(3744, 1024), seg(2720, 512),
                             seg(3232, 512), seg(5792, 1024), seg(2208, 512), seg(4768, 1024)])
    w_c = np.ascontiguousarray(np.concatenate([w_in0[:, cols_c], w_out[0]], axis=1))
    maskd = np.zeros((128, 2, 256), f)
    pp = np.arange(128)[:, None]
    qq = np.arange(256)[None, :]
    for kb in range(2):
        maskd[:, kb, :] = np.where(qq >= kb * 128 + pp, 0.0, NEG)
    return dict(vecs=vecs, norm_g=np.ascontiguousarray(norm_g[0]), final_g=np.ascontiguousarray(final_norm_g),
                mem_g=np.ascontiguousarray(mem_norm_g[0]), ident=np.eye(128, dtype=f), maskd=maskd,
                w_a=w_a, w_uqa=np.ascontiguousarray(uq), w_uqb=w_uqb, w_uk=w_uk, w_uv=w_uv, w_c=w_c,
                w_conv_o=np.ascontiguousarray(w_conv_o[0]), w_mla_o=np.ascontiguousarray(w_mla_o[0]),
                w_x_o=np.ascontiguousarray(w_x_o[0]), w_out=np.ascontiguousarray(w_out[0]),
                w_mkv=np.ascontiguousarray(w_mem_kv[0]))


def run(inputs, debug=None, phases="ABC", cores=None, trace=False):
    global ES
    inputs = {k: np.asarray(v) for k, v in inputs.items()}
    with contextlib.ExitStack() as es:
        ES = es
        nc, nwait = build_program(debug=debug, phases=phases)
        wkeys = ["norm_g", "w_in", "b_gate", "conv_w", "conv_b", "conv_ln_g", "conv_ln_b", "w_conv_o", "q_norm_g",
                 "w_uq", "kv_norm_g", "w_ukv", "w_mla_o", "mem_norm_g", "w_mem_kv", "w_x_o", "w_out", "final_norm_g"]
        shared = make_shared(**{k: inputs[k] for k in wkeys})
        cores = list(range(NCORES)) if cores is None else cores
        in_maps = [make_inputs(c, shared=shared, **inputs) for c in cores]
        res = run_bass_kernel_spmd(nc, in_maps, core_ids=list(range(len(cores))), **({"trace": True} if trace else {}))
    return res


def kernel(**inputs):
    res = run(inputs)
    x = np.asarray(inputs["x"])
    outp = np.zeros(x.shape, np.float32)
    for c in range(NCORES):
        b, p = c // 2, c % 2
        o = res.results[c]["out"]
        for j, g in enumerate(OWN[p]):
            outp[b, g * CH:(g + 1) * CH] = o[j * CH:(j + 1) * CH]
    return outp
```

```python
import contextlib
from functools import partial as P
import numpy as np
import concourse.bass as bass
import concourse.mybir as mybir
from concourse.bass_utils import run_bass_kernel_spmd

F32 = mybir.dt.float32
BF16 = mybir.dt.bfloat16
I32 = mybir.dt.int32
AF = mybir.ActivationFunctionType
ALU = mybir.AluOpType

NCORES = 8
D = 1024
SEQ = 4096
CH = 256
OWN = {0: [0, 3, 4, 7, 8, 11, 12, 15], 1: [1, 2, 5, 6, 9, 10, 13, 14]}
OTH = {0: OWN[1], 1: OWN[0]}
NEG = -30000.0
EPS = 1e-6
SCALE_MLA = 96.0 ** -0.5
SCALE_X = 128.0 ** -0.5
SB_BASE = 16512
TWO_PI = float(2 * np.pi)
CW1 = 6.28125
CW2 = float(2 * np.pi - 6.28125)
PI_LO = 3.1415925

VC_BG = 0
VC_CB = 24
VC_LG = 28
VC_LB = 32
VC_QG = 36
VC_KG = 39
VC_IF = 41
VC_PC = 42
VC_PS = 43
VC_CW = 44
VC_NG = 168
VC_MG = 176
VC_WH = 184
NV = 184


STRICT_SAME_ENGINE = True


class Buf:
    __slots__ = ("name", "writers", "readers", "sem", "dcount")

    def __init__(self, name):
        self.name = name
        self.writers = []
        self.readers = []
        self.sem = None
        self.dcount = 0


class Op:
    __slots__ = ("eng", "fn", "deps", "is_dma", "chan", "chan_count", "signal", "count")

    def __init__(self, eng, fn, is_dma=False):
        self.eng = eng
        self.fn = fn
        self.deps = []
        self.is_dma = is_dma
        self.chan = None
        self.chan_count = 0
        self.signal = False
        self.count = 0


class Sched:
    def __init__(self):
        self.ops = []
        self.last = {}
        self.barrier_deps = []
        self.chans = []
        self.chanmap = {}
        self.out_chans = []

    def _dep(self, x, y, kind):
        if y is x:
            return
        if (not y.is_dma) and (not x.is_dma) and y.eng == x.eng:
            if x.eng == "pe":
                return
            if kind != "RAW" and not STRICT_SAME_ENGINE:
                return
        x.deps.append(y)
        if not y.is_dma:
            y.signal = True

    def _add(self, x, reads, writes, shared):
        for y in self.barrier_deps:
            self._dep(x, y, "RAW")
        for b in reads:
            for w in b.writers:
                self._dep(x, w, "RAW")
        for b in writes:
            for w in b.writers:
                self._dep(x, w, "WAW")
            for r in b.readers:
                self._dep(x, r, "WAR")
        for b in shared:
            if b.readers:
                for w in b.writers:
                    self._dep(x, w, "WAW")
                for r in b.readers:
                    self._dep(x, r, "WAR")
        for b in reads:
            b.readers.append(x)
        for b in writes:
            b.writers = [x]
            b.readers = []
        for b in shared:
            if b.readers:
                b.writers = [x]
                b.readers = []
            else:
                b.writers.append(x)
        self.ops.append(x)
        self.last[x.eng if not x.is_dma else ("dma", id(x.chan))] = x

    def op(self, eng, fn, reads=(), writes=(), shared=()):
        x = Op(eng, fn)
        self._add(x, reads, writes, shared)
        return x

    def dma(self, eng, fn, reads=(), writes=(), shared=(), chan=None, is_out=False):
        x = Op(eng, fn, is_dma=True)
        if chan is None:
            chan = (list(writes) + list(shared) + list(reads))[0]
        key = (id(chan), eng)
        if key not in self.chanmap:
            self.chanmap[key] = Buf("chan_%s_%s" % (chan.name, eng))
            self.chans.append(self.chanmap[key])
        chan = self.chanmap[key]
        x.chan = chan
        chan.dcount += 1
        x.chan_count = chan.dcount
        if is_out and chan not in self.out_chans:
            self.out_chans.append(chan)
        self._add(x, reads, writes, shared)
        return x

    def barrier(self):
        self.barrier_deps = list(self.last.values())
        for y in self.barrier_deps:
            if not y.is_dma:
                y.signal = True

    def emit(self, nc, es):
        engs = {"pe": nc.tensor, "act": nc.scalar, "dve": nc.vector, "pool": nc.gpsimd, "sp": nc.sync}
        sems = {}
        for e in ("pe", "act", "dve", "pool"):
            sems[e] = es.enter_context(nc.semaphore("s_" + e))
        for i, c in enumerate(self.chans):
            c.sem = es.enter_context(nc.semaphore("d%d" % i))
        cnt = {e: 0 for e in sems}
        for x in self.ops:
            if not x.is_dma and x.signal:
                cnt[x.eng] += 1
                x.count = cnt[x.eng]
        known = {e: {} for e in engs}
        nwait = 0
        for x in self.ops:
            e = engs[x.eng]
            need = {}
            for y in x.deps:
                if y.is_dma:
                    key, val = y.chan.sem, 16 * y.chan_count
                else:
                    key, val = sems[y.eng], y.count
                k = id(key)
                if k not in need or need[k][1] < val:
                    need[k] = (key, val)
            kn = known[x.eng]
            for k, (key, val) in need.items():
                if kn.get(k, 0) < val:
                    e.wait_ge(key, val)
                    kn[k] = val
                    nwait += 1
            ins = x.fn()
            if x.is_dma:
                ins.then_inc(x.chan.sem, 16)
            elif x.signal:
                ins.then_inc(sems[x.eng], 1)
        for c in self.out_chans:
            nc.sync.wait_ge(c.sem, 16 * c.dcount)
        return nwait


class Arena:
    def __init__(self, nc, lo, hi, tag):
        self.nc, self.lo, self.hi, self.tag = nc, lo, hi, tag
        self.off = lo
        self.n = 0

    def alloc(self, shape, dtype, name="t"):
        nbytes = int(np.prod(shape[1:])) * (2 if dtype == BF16 else 4)
        nbytes = (nbytes + 63) // 64 * 64
        assert self.off + nbytes <= self.hi, (self.tag, name, self.off, nbytes, self.hi)
        t = self.nc.alloc_sbuf_tensor_at("%s_%s%d" % (self.tag, name, self.n), list(shape), dtype,
                                         offset=SB_BASE + self.off)
        self.off += nbytes
        self.n += 1
        return t


class PsumRing:
    def __init__(self, ps, banks):
        self.ps = ps
        self.banks = list(banks)
        self.bufs = {b: Buf("ps%d" % b) for b in self.banks}
        self.i = 0

    def next(self):
        b = self.banks[self.i % len(self.banks)]
        self.i += 1
        return self.ps[:, b, :], self.bufs[b]


CSTOP = 9


def build_program(debug=None, phases="ABC"):
    nc = bass.Bass("TRN2", target_bir_lowering=False)
    S = Sched()

    def din(name, shape, dt=F32):
        return nc.dram_tensor(name, list(shape), dt, kind="ExternalInput").ap()

    x_all = din("x_all", [4096, D])
    x_ext = din("x_ext", [4, 576, D])
    pos_all = din("pos_all", [4096], I32)
    mem_in = din("mem", [256, D])
    vecs_in = din("vecs", [128, NV])
    gb_in = din("norm_g", [D])
    fg_in = din("final_g", [D])
    mg_in = din("mem_g", [D])
    ident_in = din("ident", [128, 128])
    maskd_in = din("maskd", [128, 2, 256])
    maskf_in = din("maskf", [128, 8, 256])
    w_a_in = din("w_a", [D, 832])
    w_uqa_in = din("w_uqa", [384, 768])
    w_uqb_in = din("w_uqb", [384, 768])
    w_uk_in = din("w_uk", [256, 512])
    w_uv_in = din("w_uv", [256, 512])
    w_c_in = din("w_c", [D, 7168])
    w_conv_o_in = din("w_conv_o", [512, D])
    w_mla_o_in = din("w_mla_o", [512, D])
    w_x_o_in = din("w_x_o", [512, D])
    w_out_in = din("w_out", [D, D])
    w_mkv_in = din("w_mkv", [D, D])
    out = nc.dram_tensor("out", [2048, D], F32, kind="ExternalOutput").ap()
    dbg = {}
    if debug:
        for nm, shp in debug.items():
            dbg[nm] = nc.dram_tensor("dbg_" + nm, list(shp), F32, kind="ExternalOutput").ap()

    ps = nc.alloc_psum_tensor("ps", [128, 8, 512], F32)

    P_ = Arena(nc, 0, 164352, "P")
    KT = P_.alloc([128, 8, 4096], BF16, "KT")
    VA = P_.alloc([128, 32, 8, 65], BF16, "VA")
    QT = P_.alloc([128, 8, 2048], BF16, "QT")
    OG = P_.alloc([128, 4, 8, 512], BF16, "OG")
    CST = Arena(nc, 208768, 212864, "C")
    ident = CST.alloc([128, 128], BF16, "ident")
    ones_f = CST.alloc([128, 128], F32, "ones_f")
    ones_b = CST.alloc([128, 128], BF16, "ones_b")
    vecs = CST.alloc([128, NV], F32, "vecs")
    hb = CST.alloc([128, 24], F32, "hb")
    zero_c = CST.alloc([128, 1], F32, "zero")
    eps_c = CST.alloc([128, 1], F32, "eps")
    b_const = Buf("consts")
    b_KT = [Buf("KT%d" % t) for t in range(8)]
    b_VA = [Buf("VA%d" % t) for t in range(8)]
    b_QT = [Buf("QT%d" % t) for t in range(4)]
    b_OG = [Buf("OG%d" % t) for t in range(4)]

    def dump(name, ap_sb, bufs, dst=None, cast=False):
        if name in dbg and cast:
            S.dma("pool", P(nc.gpsimd.dma_start, out=(dst if dst is not None else dbg[name]), in_=ap_sb),
                  reads=bufs, chan=Buf("dbg_" + name), is_out=True)
        elif name in dbg:
            S.dma("sp", P(nc.sync.dma_start, out=(dst if dst is not None else dbg[name]), in_=ap_sb),
                  reads=bufs, chan=Buf("dbg_" + name), is_out=True)

    S.dma("pool", P(nc.gpsimd.dma_start, out=ident[:], in_=ident_in[:, :]), shared=[b_const])
    S.dma("sp", P(nc.sync.dma_start, out=vecs[:], in_=vecs_in[:, :]), shared=[b_const])
    S.op("dve", P(nc.vector.memset, ones_f[:], 1.0), shared=[b_const])
    S.op("dve", P(nc.vector.memset, ones_b[:], 1.0), shared=[b_const])
    S.op("dve", P(nc.vector.memset, zero_c[:], 0.0), shared=[b_const])
    S.op("dve", P(nc.vector.memset, eps_c[:], EPS), shared=[b_const])
    b_hb = Buf("hb")
    S.op("dve", P(nc.vector.tensor_scalar, out=hb[:], in0=vecs[:, VC_BG:VC_BG + 24], scalar1=0.5,
                                                scalar2=None, op0=ALU.mult), reads=[b_const], writes=[b_hb])

    wc_bf = nc.dram_tensor("wc_bf", [14, 128, 8, 512], BF16, kind="Internal").ap()
    wo_bf = nc.dram_tensor("wo_bf", [3, 128, 4, D], BF16, kind="Internal").ap()
    wmkv_bf = nc.dram_tensor("wmkv_bf", [2, 128, 8, 512], BF16, kind="Internal").ap()
    b_wcv = {("c", ck): Buf("wcv_c%d" % ck) for ck in range(14)}
    b_wcv.update({("o", i): Buf("wcv_o%d" % i) for i in range(3)})
    b_wcv.update({("m", ck): Buf("wcv_m%d" % ck) for ck in range(2)})

    def convert_weights():
        def cv_c(ck):
            S.dma("pool", P(nc.gpsimd.dma_start, out=wc_bf[ck],
                            in_=w_c_in[:, ck * 512:(ck + 1) * 512].rearrange("(kc p) n -> p kc n", p=128)),
                  writes=[b_wcv[("c", ck)]])

        def cv_o(i):
            win_ = (w_x_o_in, w_mla_o_in, w_conv_o_in)[i]
            S.dma("pool", P(nc.gpsimd.dma_start, out=wo_bf[i], in_=win_.rearrange("(kc p) n -> p kc n", p=128)),
                  writes=[b_wcv[("o", i)]])
        for ck in range(2):
            S.dma("pool", P(nc.gpsimd.dma_start, out=wmkv_bf[ck],
                            in_=w_mkv_in[:, ck * 512:(ck + 1) * 512].rearrange("(kc p) n -> p kc n", p=128)),
                  writes=[b_wcv[("m", ck)]])
        for ck in (0, 1, 5, 6, 2, 9):
            cv_c(ck)
        cv_o(0)
        cv_c(7)
        cv_c(8)
        cv_o(1)
        cv_c(10)
        cv_c(11)
        cv_o(2)
        for ck in (3, 4, 12, 13):
            cv_c(ck)

    if "A" in phases:
        A_ = Arena(nc, 131584, 208768, "A")
        ckv = A_.alloc([128, 2, 4096], BF16, "ckv")
        cq = A_.alloc([128, 3, 2048], BF16, "cq")
        cs = A_.alloc([128, 4096], BF16, "cs")
        sn = A_.alloc([128, 4096], BF16, "sn")
        w_uqa = A_.alloc([128, 3, 768], BF16, "w_uqa")
        w_uqb = A_.alloc([128, 3, 768], BF16, "w_uqb")
        w_uk = A_.alloc([128, 2, 512], BF16, "w_uk")
        w_uv = A_.alloc([128, 2, 512], BF16, "w_uv")
        t1q = [A_.alloc([128, 512], F32, "t1q") for _ in range(2)]
        t2q = [A_.alloc([128, 512], F32, "t2q") for _ in range(2)]
        A1_ = Arena(nc, 65536, 131584, "A1")
        w_a = A1_.alloc([128, 8, 832], BF16, "w_a")
        xb = [A1_.alloc([128, D], F32, "xb") for _ in range(4)]
        xn = [A1_.alloc([128, D], BF16, "xn") for _ in range(2)]
        xnT = [A1_.alloc([128, 8, 512], BF16, "xnT") for _ in range(2)]
        a1_mark = A1_.off
        sqb = [A1_.alloc([128, 3, 512], BF16, "sqb") for _ in range(2)]
        rsb = [A1_.alloc([128, 512], F32, "rsb") for _ in range(2)]
        t1 = A1_.alloc([128, 512], F32, "t1")
        t2 = A1_.alloc([128, 512], F32, "t2")
        kr = [A_.alloc([128, 512], BF16, "kr") for _ in range(2)]
        st = A_.alloc([128, 16], F32, "st")
        A2_ = Arena(nc, a1_mark, 131584, "A2")
        rp_i = A2_.alloc([128, 1024], I32, "rp_i")
        rp_a = A2_.alloc([128, 1024], F32, "rp_a")
        rp_n = A2_.alloc([128, 1024], F32, "rp_n")
        rp_o = A2_.alloc([128, 1024], BF16, "rp_o")
        b_w, b_w0, b_w2 = Buf("wA"), Buf("w_a_raw"), Buf("wA2")
        b_cs = Buf("cs")
        b_xb = [Buf("xb%d" % i) for i in range(4)]
        b_xn = [Buf("xn0"), Buf("xn1")]
        b_xnT = [Buf("xnT0"), Buf("xnT1")]
        b_sq, b_rs = [Buf("sq0"), Buf("sq1")], [Buf("rs0"), Buf("rs1")]
        b_t1, b_t2, b_kr = Buf("t1"), Buf("t2"), [Buf("kr0"), Buf("kr1")]
        b_stl = [Buf("st%d" % i) for i in range(8)]
        b_rpi, b_rpa, b_rpn, b_rpo = Buf("rpi"), Buf("rpa"), Buf("rpn"), Buf("rpo")
        b_ckv = [Buf("ckv%d" % t) for t in range(8)]
        b_cq = [Buf("cq%d" % t) for t in range(4)]
        b_t1q, b_t2q = [Buf("t1q0"), Buf("t1q1")], [Buf("t2q0"), Buf("t2q1")]

        S.dma("pool", P(nc.gpsimd.dma_start, out=w_a[:], in_=w_a_in.rearrange("(kc p) n -> p kc n", p=128)),
              writes=[b_w0])
        for wt_, win_ in ((w_uk, w_uk_in), (w_uv, w_uv_in), (w_uqa, w_uqa_in), (w_uqb, w_uqb_in)):
            S.dma("pool", P(nc.gpsimd.dma_start, out=wt_[:], in_=win_.rearrange("(kc p) n -> p kc n", p=128)),
                  shared=[b_w2])
        for g in range(4):
            S.dma("sp", P(nc.sync.dma_start, out=rp_i[32 * g:32 * g + 32, :],
                          in_=pos_all[g * 1024:(g + 1) * 1024].partition_broadcast(32)), shared=[b_rpi])
        for tbl, pcol in ((cs, VC_PC), (sn, VC_PS)):
            S.op("dve", P(nc.vector.tensor_copy, rp_a[:], rp_i[:]), reads=[b_rpi], writes=[b_rpa])
            S.op("dve", P(nc.vector.tensor_scalar, out=rp_a[:], in0=rp_a[:], scalar1=vecs[:, VC_IF:VC_IF + 1],
                          scalar2=vecs[:, pcol:pcol + 1], op0=ALU.mult, op1=ALU.add),
                 reads=[b_rpa, b_const], writes=[b_rpa])
            S.op("dve", P(nc.vector.tensor_scalar, out=rp_n[:], in0=rp_a[:], scalar1=1.0 / TWO_PI,
                          scalar2=None, op0=ALU.mult), reads=[b_rpa], writes=[b_rpn])
            S.op("dve", P(nc.vector.tensor_copy, rp_i[:], rp_n[:]), reads=[b_rpn], writes=[b_rpi])
            S.op("dve", P(nc.vector.tensor_copy, rp_n[:], rp_i[:]), reads=[b_rpi], writes=[b_rpn])
            if tbl is cs:
                for g in range(4):
                    S.dma("sp", P(nc.sync.dma_start, out=rp_i[32 * g:32 * g + 32, :],
                                  in_=pos_all[g * 1024:(g + 1) * 1024].partition_broadcast(32)), shared=[b_rpi])
            S.op("dve", P(nc.vector.scalar_tensor_tensor, out=rp_a[:], in0=rp_n[:], scalar=-CW1,
                          in1=rp_a[:], op0=ALU.mult, op1=ALU.add), reads=[b_rpn, b_rpa], writes=[b_rpa])
            S.op("dve", P(nc.vector.scalar_tensor_tensor, out=rp_a[:], in0=rp_n[:], scalar=-CW2,
                          in1=rp_a[:], op0=ALU.mult, op1=ALU.add), reads=[b_rpn, b_rpa], writes=[b_rpa])
            S.op("dve", P(nc.vector.tensor_scalar, out=rp_a[:], in0=rp_a[:], scalar1=-PI_LO, scalar2=PI_LO,
                          op0=ALU.max, op1=ALU.min), reads=[b_rpa], writes=[b_rpa])
            S.op("act", P(nc.scalar.activation, out=rp_o[:], in_=rp_a[:], func=AF.Sin, bias=zero_c[:, :], scale=1.0),
                 reads=[b_rpa, b_const], writes=[b_rpo])
            for g in range(4):
                S.dma("sp", P(nc.sync.dma_start, out=tbl[64:96, g * 1024:(g + 1) * 1024], in_=rp_o[32 * g:32 * g + 32, :]),
                      reads=[b_rpo], shared=[b_cs], chan=b_cs)
        for kc in range(8):
            S.op("dve", P(nc.vector.tensor_scalar, out=w_a[:, kc, :], in0=w_a[:, kc, :],
                          scalar1=vecs[:, VC_NG + kc:VC_NG + kc + 1], scalar2=None, op0=ALU.mult),
                 reads=[b_w0, b_const], shared=[b_w])
        S.barrier()

        R = slice(64, 96)
        ringT = PsumRing(ps, [0, 1])
        ringP = PsumRing(ps, [2, 3, 4, 5, 6, 7])
        evac_i = [0]

        def evac_copy(out_ap, in_ap, reads, writes=(), shared=(), scale=None):
            evac_i[0] += 1
            if evac_i[0] % 2 == 0:
                if scale is None:
                    S.op("act", P(nc.scalar.copy, out=out_ap, in_=in_ap), reads=reads, writes=writes, shared=shared)
                else:
                    S.op("act", P(nc.scalar.mul, out=out_ap, in_=in_ap, mul=scale), reads=reads, writes=writes,
                         shared=shared)
            else:
                if scale is None:
                    S.op("dve", P(nc.vector.tensor_copy, out_ap, in_ap), reads=reads, writes=writes, shared=shared)
                else:
                    S.op("dve", P(nc.vector.tensor_scalar, out=out_ap, in0=in_ap, scalar1=scale, scalar2=None,
                                  op0=ALU.mult), reads=reads, writes=writes, shared=shared)

        ngrp = [0]

        def front1(t):
            tok0 = t * 512
            b_st = b_stl[t % 2]
            c0_ = (t % 2) * 4
            for blk in range(4):
                r0 = tok0 + blk * 128
                S.dma("sp", P(nc.sync.dma_start, out=xb[blk][:], in_=x_all[r0:r0 + 128, :]), writes=[b_xb[blk]])
            for blk in range(4):
                xni, bxn = xn[blk % 2], b_xn[blk % 2]
                S.op("act", P(nc.scalar.activation, out=xni[:], in_=xb[blk][:], func=AF.Square,
                              accum_out=st[:, c0_ + blk:c0_ + blk + 1]), reads=[b_xb[blk]], writes=[bxn], shared=[b_st])
            S.op("act", P(nc.scalar.activation, out=st[:, 8 + c0_:12 + c0_], in_=st[:, c0_:c0_ + 4], func=AF.Ln,
                          bias=eps_c[:, 0:1], scale=1.0 / D), reads=[b_st, b_const], writes=[b_st])
            S.op("act", P(nc.scalar.activation, out=st[:, 8 + c0_:12 + c0_], in_=st[:, 8 + c0_:12 + c0_], func=AF.Exp,
                          scale=-0.5), reads=[b_st], writes=[b_st])

        def front2(t):
            xT, bxT = xnT[t % 2], b_xnT[t % 2]
            b_st = b_stl[t % 2]
            c0_ = (t % 2) * 4
            for blk in range(4):
                xni, bxn = xn[blk % 2], b_xn[blk % 2]
                S.op("dve", P(nc.vector.tensor_scalar, out=xni[:], in0=xb[blk][:],
                              scalar1=st[:, 8 + c0_ + blk:9 + c0_ + blk], scalar2=None, op0=ALU.mult),
                     reads=[b_xb[blk], b_st], writes=[bxn])
                pT, bT = ringT.next()
                pTb = pT.bitcast(BF16)
                for kc in range(8):
                    S.op("pe", P(nc.tensor.transpose, out=pTb[:, kc * 128:(kc + 1) * 128],
                                 in_=xni[:, kc * 128:(kc + 1) * 128], identity=ident[:]),
                         reads=[bxn, b_const], shared=[bT])
                evac_copy(xT[:, :, blk * 128:(blk + 1) * 128], pTb.rearrange("p (k n) -> p k n", k=8),
                          reads=[bT], shared=[bxT])

        front1(0)
        front2(0)
        for t in range(8):
            own = t < 4
            tok0 = t * 512
            xT, bxT = xnT[t % 2], b_xnT[t % 2]
            if t + 1 < 8:
                front1(t + 1)

            def proj(c0, m, xT=xT, bxT=bxT):
                p_, b_ = ringP.next()
                for kc in range(8):
                    S.op("pe", P(nc.tensor.matmul, p_[0:m, :], w_a[:, kc, c0:c0 + m], xT[:, kc, :],
                                 start=(kc == 0), stop=(kc == 7)), reads=[b_w, bxT], shared=[b_])
                return p_, b_

            def norm_group(pbs, nch, gcol, div, dst, bdst):
                gi = ngrp[0] % 2
                ngrp[0] += 1
                sq_, bsq, rs_, brs = sqb[gi], b_sq[gi], rsb[gi], b_rs[gi]
                for i, (p_, b_) in enumerate(pbs):
                    S.op("act", P(nc.scalar.activation, out=sq_[:, i, :], in_=p_, func=AF.Square),
                         reads=[b_], shared=[bsq])
                pss, bss = ringP.next()
                for i in range(nch):
                    S.op("pe", P(nc.tensor.matmul, pss, ones_b[:], sq_[:, i, :], start=(i == 0), stop=(i == nch - 1)),
                         reads=[bsq, b_const], shared=[bss])
                S.op("act", P(nc.scalar.activation, out=rs_[:], in_=pss, func=AF.Ln, bias=eps_c[:, 0:1],
                              scale=1.0 / div), reads=[bss, b_const], writes=[brs])
                S.op("act", P(nc.scalar.activation, out=rs_[:], in_=rs_[:], func=AF.Exp, scale=-0.5),
                     reads=[brs], writes=[brs])
                for i, (p_, b_) in enumerate(pbs):
                    S.op("dve", P(nc.vector.scalar_tensor_tensor, out=dst(i), in0=p_,
                                  scalar=vecs[:, gcol + i:gcol + i + 1], in1=rs_[:], op0=ALU.mult, op1=ALU.mult),
                         reads=[b_, brs, b_const], shared=[bdst])

            kvp = [proj(384 + 128 * i, 128) for i in range(2)]
            pA, bA = proj(640, 96)
            pB, bB = proj(736, 96)
            if t + 1 < 8:
                front2(t + 1)
            S.op("dve", P(nc.vector.tensor_tensor, out=t1[R, :], in0=pA[R, :], in1=cs[R, tok0:tok0 + 512],
                          op=ALU.mult), reads=[bA, b_cs], writes=[b_t1])
            S.op("dve", P(nc.vector.tensor_tensor, out=t2[R, :], in0=pB[R, :], in1=sn[R, tok0:tok0 + 512],
                          op=ALU.mult), reads=[bB, b_cs], writes=[b_t2])
            kr_, bkr = kr[t % 2], b_kr[t % 2]
            S.op("dve", P(nc.vector.tensor_tensor, out=kr_[R, :], in0=t1[R, :], in1=t2[R, :], op=ALU.add),
                 reads=[b_t1, b_t2], writes=[bkr])
            for h in range(8):
                if h % 2 == 0:
                    S.op("act", P(nc.scalar.copy, out=KT[R, h, tok0:tok0 + 512], in_=kr_[R, :]),
                         reads=[bkr], shared=[b_KT[t]])
                else:
                    S.op("dve", P(nc.vector.tensor_copy, KT[R, h, tok0:tok0 + 512], kr_[R, :]),
                         reads=[bkr], shared=[b_KT[t]])
            norm_group(kvp, 2, VC_KG, 256.0, lambda i, tok0=tok0: ckv[:, i, tok0:tok0 + 512], b_ckv[t])
            if own:
                qp = [proj(128 * i, 128) for i in range(3)]
                norm_group(qp, 3, VC_QG, 384.0, lambda i, tok0=tok0: cq[:, i, tok0:tok0 + 512], b_cq[t])
        S.barrier()
        S.op("pool", P(nc.gpsimd.memset, VA[:, :, :, 64:65], 1.0), shared=b_VA)
        ringQ = PsumRing(ps, [0, 1, 2, 3, 4, 5, 6, 7])
        nq = [0]
        for t in range(8):
            own = t < 4
            tok0 = t * 512
            for h in range(8):
                p_, b_ = ringQ.next()
                for kc in range(2):
                    S.op("pe", P(nc.tensor.matmul, p_[0:64, :], w_uk[:, kc, h * 64:(h + 1) * 64],
                                 ckv[:, kc, tok0:tok0 + 512], start=(kc == 0), stop=(kc == 1)),
                         reads=[b_w2, b_ckv[t]], shared=[b_])
                evac_copy(KT[0:64, h, tok0:tok0 + 512], p_[0:64, :], reads=[b_], shared=[b_KT[t]])
            for blk in range(4):
                p_, b_ = ringQ.next()
                for kc in range(2):
                    S.op("pe", P(nc.tensor.matmul, p_, ckv[:, kc, tok0 + blk * 128:tok0 + (blk + 1) * 128],
                                 w_uv[:, kc, :], start=(kc == 0), stop=(kc == 1)),
                         reads=[b_w2, b_ckv[t]], shared=[b_])
                evac_copy(VA[:, t * 4 + blk, :, 0:64], p_.rearrange("p (h d) -> p h d", h=8), reads=[b_],
                          shared=[b_VA[t]])
            if own:
                for h in range(8):
                    pa, ba = ringQ.next()
                    for kc in range(3):
                        S.op("pe", P(nc.tensor.matmul, pa[0:96, :], w_uqa[:, kc, h * 96:(h + 1) * 96],
                                     cq[:, kc, tok0:tok0 + 512], start=(kc == 0), stop=(kc == 2)),
                             reads=[b_w2, b_cq[t]], shared=[ba])
                    pb, bb = ringQ.next()
                    for kc in range(3):
                        S.op("pe", P(nc.tensor.matmul, pb[0:96, :], w_uqb[:, kc, h * 96:(h + 1) * 96],
                                     cq[:, kc, tok0:tok0 + 512], start=(kc == 0), stop=(kc == 2)),
                             reads=[b_w2, b_cq[t]], shared=[bb])
                    qi = nq[0] % 2
                    nq[0] += 1
                    S.op("act", P(nc.scalar.mul, out=QT[0:64, h, tok0:tok0 + 512], in_=pa[0:64, :],
                                  mul=SCALE_MLA), reads=[ba], shared=[b_QT[t]])
                    S.op("dve", P(nc.vector.scalar_tensor_tensor, out=t1q[qi][R, :], in0=pa[R, :], scalar=SCALE_MLA,
                                  in1=cs[R, tok0:tok0 + 512], op0=ALU.mult, op1=ALU.mult),
                         reads=[ba, b_cs], writes=[b_t1q[qi]])
                    S.op("dve", P(nc.vector.scalar_tensor_tensor, out=t2q[qi][R, :], in0=pb[R, :], scalar=SCALE_MLA,
                                  in1=sn[R, tok0:tok0 + 512], op0=ALU.mult, op1=ALU.mult),
                         reads=[bb, b_cs], writes=[b_t2q[qi]])
                    S.op("pool", P(nc.gpsimd.tensor_tensor, out=QT[R, h, tok0:tok0 + 512], in0=t1q[qi][R, :],
                                   in1=t2q[qi][R, :], op=ALU.add),
                         reads=[b_t1q[qi], b_t2q[qi]], shared=[b_QT[t]])
        S.barrier()
        if debug and ("kt" in dbg or "va" in dbg or "qt" in dbg):
            dstage = Arena(nc, 131584, 208768, "DBG").alloc([128, 2080], F32, "dstage")
            b_ds = Buf("dstage")
            if "kt" in dbg:
                for i, h in enumerate((0, 5)):
                    for half in range(2):
                        S.op("dve", P(nc.vector.tensor_copy, dstage[0:96, 0:2048], KT[0:96, h, half * 2048:(half + 1) * 2048]),
                             reads=b_KT, writes=[b_ds])
                        dump("kt", dstage[0:96, 0:2048], [b_ds], dst=dbg["kt"][i, :, half * 2048:(half + 1) * 2048])
            if "va" in dbg:
                S.op("dve", P(nc.vector.tensor_copy, dstage[:, 0:2080], VA[:, 0:4, :, :].rearrange("p a h d -> p (a h d)")),
                     reads=b_VA, writes=[b_ds])
                dump("va", dstage[:, 0:2080], [b_ds])
            if "qt" in dbg:
                S.op("dve", P(nc.vector.tensor_copy, dstage[0:96, 0:2048], QT[0:96, 3, :]), reads=b_QT, writes=[b_ds])
                dump("qt", dstage[0:96, 0:2048], [b_ds])
            S.barrier()

    if "B" in phases:
        B_ = Arena(nc, 164352, 208768, "B")
        maskd = B_.alloc([128, 2, 256], BF16, "maskd")
        maskf = B_.alloc([128, 8, 256], BF16, "maskf")
        NPT = 6
        PT = [B_.alloc([128, 512], BF16, "PT") for _ in range(NPT)]
        b_PT = [Buf("PT%d" % i) for i in range(NPT)]
        o_sb = [B_.alloc([128, 512], F32, "o_sb") for _ in range(2)]
        rden = [B_.alloc([128, 512], F32, "rden") for _ in range(2)]
        b_osb = [Buf("osb0"), Buf("osb1")]
        b_rden = [Buf("rden0"), Buf("rden1")]
        b_mask = Buf("masks")
        S.dma("pool", P(nc.gpsimd.dma_start, out=maskd[:], in_=maskd_in[:, :, :]), shared=[b_mask])
        S.dma("pool", P(nc.gpsimd.dma_start, out=maskf[:], in_=maskf_in[:, :, :]), shared=[b_mask])
        convert_weights()
        ringS = PsumRing(ps, [3, 4, 5, 6, 7])
        b_O = [Buf("O0"), Buf("O1")]
        b_bc = Buf("bc")
        items = []
        for T in range(4):
            for h in range(8):
                blocks = []
                for grp in range(2):
                    for j in range(2 * T + 2):
                        c = j + 8 * grp
                        for kb in range(2):
                            if j < 2 * T:
                                blocks.append((c, kb, 0, 512, None))
                            elif j == 2 * T:
                                m = maskd[:, kb, :] if grp == 0 else maskf[:, j, :]
                                blocks.append((c, kb, 0, 512, (m, 0)))
                            else:
                                m = maskd[:, kb, :] if grp == 0 else maskf[:, j, :]
                                blocks.append((c, kb, 256, 512, (m, 256)))
                for bi, blk in enumerate(blocks):
                    items.append((T, h, blk, bi == 0, bi == len(blocks) - 1))
        N = len(items)
        LOOK = 3
        st_ = {}

        def emit_qk(i):
            T, h, (c, kb, qlo, qhi, mask), first, last = items[i]
            sp_, sb_ = ringS.next()
            kcol = c * 256 + kb * 128
            S.op("pe", P(nc.tensor.matmul, sp_[:, qlo:qhi], KT[0:96, h, kcol:kcol + 128],
                         QT[0:96, h, T * 512 + qlo:T * 512 + qhi], start=True, stop=(mask is None)),
                 reads=[b_KT[c // 2], b_QT[T]], shared=[sb_])
            if mask is not None:
                m, mlo = mask
                S.op("pe", P(nc.tensor.matmul, sp_[:, mlo:mlo + 256], ident[:], m, start=False, stop=True),
                     reads=[b_mask, b_const], shared=[sb_])
            pt = PT[i % NPT]
            S.op("act", P(nc.scalar.activation, out=pt[:, qlo:qhi], in_=sp_[:, qlo:qhi], func=AF.Exp),
                 reads=[sb_], writes=[b_PT[i % NPT]])

        def emit_pv(i):
            T, h, (c, kb, qlo, qhi, mask), first, last = items[i]
            ob = (T * 8 + h) % 2
            O = ps[:, ob, :]
            pt = PT[i % NPT]
            S.op("pe", P(nc.tensor.matmul, O[0:65, qlo:qhi], VA[:, c * 2 + kb, h, :], pt[:, qlo:qhi],
                         start=first, stop=last),
                 reads=[b_VA[c // 2], b_PT[i % NPT]], shared=[b_O[ob]])
            if last:
                S.op("dve", P(nc.vector.tensor_copy, o_sb[ob][0:65, :], O[0:65, :]), reads=[b_O[ob]],
                     writes=[b_osb[ob]])
                S.op("dve", P(nc.vector.reciprocal, out=rden[ob][64:65, :], in_=o_sb[ob][64:65, :]),
                     reads=[b_osb[ob]], writes=[b_rden[ob]])
                pending.append((i + FIN_DELAY, T, h, ob))

        def emit_fin(T, h, ob):
            bc = ps[:, 2, :]
            S.op("pe", P(nc.tensor.matmul, bc[0:64, :], ones_f[64:65, 0:64], rden[ob][64:65, :],
                         start=True, stop=True), reads=[b_rden[ob], b_const], writes=[b_bc])
            S.op("dve", P(nc.vector.tensor_tensor, out=OG[0:64, T, h, :], in0=o_sb[ob][0:64, :],
                          in1=bc[0:64, :], op=ALU.mult), reads=[b_osb[ob], b_bc], shared=[b_OG[T]])

        pending = []
        FIN_DELAY = 7
        for i in range(N + LOOK):
            if i < N:
                emit_qk(i)
            if i >= LOOK:
                emit_pv(i - LOOK)
            while pending and pending[0][0] <= i - LOOK:
                _, T_, h_, ob_ = pending.pop(0)
                emit_fin(T_, h_, ob_)
        for _, T_, h_, ob_ in pending:
            emit_fin(T_, h_, ob_)
        S.barrier()
        if debug and "og" in dbg:
            dstage2 = Arena(nc, 0, 65536, "DBG2").alloc([128, 2048], F32, "dstage2")
            b_ds2 = Buf("dstage2")
            for i, h in enumerate((0, 6)):
                for T_ in range(4):
                    S.op("dve", P(nc.vector.tensor_copy, dstage2[0:64, T_ * 512:(T_ + 1) * 512], OG[0:64, T_, h, :]),
                         reads=b_OG, shared=[b_ds2])
                dump("og", dstage2[0:64, :], [b_ds2], dst=dbg["og"][i])
            S.barrier()

    if "C" in phases:
        C_ = Arena(nc, 0, 131584, "C1")
        C2_ = Arena(nc, 164352, 208768, "C2")
        fgt = C_.alloc([128, D], F32, "fgt")
        KM = C_.alloc([128, 4, 256], BF16, "KM")
        VM = C_.alloc([128, 2, 512], BF16, "VM")
        wh = C_.alloc([128, 124], F32, "wh")
        epsC = C_.alloc([128, 1], F32, "epsC")
        diag = C_.alloc([128, 124, 128], BF16, "diag")
        NWC = 4
        wc, wc_o = [], []
        for i in range(NWC):
            off_i = C_.off
            wc.append(C_.alloc([128, 8, 512], BF16, "wc"))
            wc_o.append(Arena(nc, off_i, off_i + 8192, "WO%d" % i).alloc([128, 4, 1024], BF16, "wo"))
        b_wc = [Buf("wc%d" % i) for i in range(NWC)]
        xbC = [C_.alloc([128, D], F32, "xbC") for _ in range(2)]
        b_xbC = [Buf("xbC0"), Buf("xbC1")]
        xnC = C_.alloc([128, D], BF16, "xnC")
        xo_ = [C_.alloc([128, 8, 512], BF16, "xo0"),
               Arena(nc, 131584, 131584 + 8192, "XO1").alloc([128, 8, 512], BF16, "xo1")]
        xh_ = [C_.alloc([128, 8, 64], BF16, "xh0"), C_.alloc([128, 8, 64], BF16, "xh1")]
        stC = C_.alloc([128, 16], F32, "stC")
        u2 = C_.alloc([128, 4, 2, 288], BF16, "u2")
        accd_off = C_.off
        acc_d = C_.alloc([128, 4, 512], F32, "acc_d")
        mean_sb = C_.alloc([128, 512], F32, "mean")
        rstd_sb = C_.alloc([128, 512], F32, "rstd")
        ug = C_.alloc([128, 4, 512], BF16, "ug")
        merged2 = C_.alloc([128, 8, 512], F32, "merged2")
        tg = [C2_.alloc([128, 512], F32, "tg") for _ in range(2)]
        sgb = [C2_.alloc([128, 512], BF16, "sg") for _ in range(4)]
        sgc = [C2_.alloc([128, 512], BF16, "sgc") for _ in range(4)]
        mbf_off = C2_.off
        xq = C2_.alloc([128, 4, 512], BF16, "xq")
        oxg = C2_.alloc([128, 4, 512], BF16, "oxg")
        merged_bf = Arena(nc, mbf_off, mbf_off + 8192, "MBF").alloc([128, 8, 512], BF16, "merged_bf")
        PTx = [C2_.alloc([128, 512], BF16, "PTx") for _ in range(2)]
        rr = C2_.alloc([128, 512], F32, "rr")
        tmpf = [C2_.alloc([128, 512], F32, "tmpf") for _ in range(2)]
        m2_sb = tmpf[0]
        ogp = C2_.alloc([128, 4, 512], BF16, "ogp")
        outb = [C2_.alloc([128, D], F32, "outb") for _ in range(2)]
        b_outb = [Buf("outb0"), Buf("outb1")]
        xnC2 = C2_.alloc([128, D], BF16, "xnC2")
        b_wres = Buf("wres")
        b_diag = Buf("diag")
        b_kvm = Buf("kvm")
        b_xnC = Buf("xnC")
        xnL, b_xnL = [xnC, xnC2], [b_xnC, Buf("xnC2")]
        b_xn_ = [Buf("xnTC0"), b_OG[0]]
        cur = {}

        def set_cur(T):
            cur["xo"], cur["xh"], cur["b"] = xo_[T % 2], xh_[T % 2], b_xn_[T % 2]
        b_stC = [Buf("stC%d" % i) for i in range(5)]
        b_u2 = Buf("u2")
        b_tg = [Buf("tg0"), Buf("tg1")]
        b_accd = [Buf("accd%d" % i) for i in range(4)]
        b_mean, b_rstd = Buf("mean"), Buf("rstd")
        b_sg = [Buf("sg%d" % i) for i in range(4)]
        b_sgc = [Buf("sgc%d" % i) for i in range(4)]
        b_ug, b_mg2, b_mbf = Buf("ug"), [Buf("mg2_%d" % i) for i in range(8)], Buf("mbf")
        b_xq, b_PTx, b_rr = Buf("xq"), [Buf("PTx0"), Buf("PTx1")], Buf("rr")
        b_tmpf = [Buf("tmpf0"), Buf("tmpf1")]
        b_m2 = b_tmpf[0]
        b_oxg, b_ogp, b_res = Buf("oxg"), Buf("ogp"), Buf("res")
        b_fin = Buf("fin_st")

        S.dma("sp", P(nc.sync.dma_start, out=fgt[:], in_=fg_in.partition_broadcast(128)), shared=[b_wres])
        S.op("dve", P(nc.vector.memset, epsC[:], EPS), shared=[b_wres])
        b_wh = Buf("wh")
        S.op("dve", P(nc.vector.tensor_scalar, out=wh[:], in0=vecs[:, VC_CW:VC_CW + 124], scalar1=0.5, scalar2=None,
                      op0=ALU.mult), reads=[b_const], writes=[b_wh])
        b_diagm = [Buf("diag%d" % i) for i in range(4)]

        def build_diag(mc):
            for i in range(mc * 31, (mc + 1) * 31):
                if i % 2 == 0:
                    S.op("dve", P(nc.vector.tensor_scalar, out=diag[:, i, :], in0=ident[:], scalar1=wh[:, i:i + 1],
                                  scalar2=None, op0=ALU.mult), reads=[b_wh, b_const], shared=[b_diagm[mc]])
                else:
                    S.op("act", P(nc.scalar.mul, out=diag[:, i, :], in_=ident[:], mul=wh[:, i:i + 1]),
                         reads=[b_wh, b_const], shared=[b_diagm[mc]])

        ringC = PsumRing(ps, [0, 1, 2, 3, 4, 5, 6, 7])

        def wsrc(ap):
            return ap.rearrange("(kc p) n -> p kc n", p=128)

        def WC(a):
            return ("w", wc_bf[a // 512], b_wcv[("c", a // 512)])
        names = ["xq", "xg", "cg", "mg", "wxo", "g2a", "g2b", "wmo", "g1a", "g1b", "wco", "g0a", "g0b"]
        srcs = {"val": WC(0), "glu": WC(512), "xq": WC(2560), "xg": WC(3072), "cg": WC(1024), "mg": WC(4608),
                "wxo": ("o", wo_bf[0], b_wcv[("o", 0)]), "g2a": WC(3584), "g2b": WC(4096), "wmo": ("o", wo_bf[1], b_wcv[("o", 1)]),
                "g1a": WC(5120), "g1b": WC(5632), "wco": ("o", wo_bf[2], b_wcv[("o", 2)]), "g0a": WC(1536), "g0b": WC(2048),
                "woa": WC(6144), "wob": WC(6656)}
        chunk_src = [("w", wmkv_bf[0], b_wcv[("m", 0)]), ("w", wmkv_bf[1], b_wcv[("m", 1)])]
        CI = [dict() for _ in range(4)]

        def add_chunk(T, nm):
            CI[T][nm] = len(chunk_src)
            chunk_src.append(srcs[nm])
        add_chunk(0, "val")
        add_chunk(0, "glu")
        for T in range(4):
            for nm in names:
                add_chunk(T, nm)
            if T + 1 < 4:
                add_chunk(T + 1, "val")
                add_chunk(T + 1, "glu")
            add_chunk(T, "woa")
            add_chunk(T, "wob")
        issued = [0]
        consumed = set()

        def prefetch():
            low = 0
            while low in consumed:
                low += 1
            while issued[0] < min(low + NWC, len(chunk_src)):
                i = issued[0]
                kind, src, bsrc = chunk_src[i]
                dst = wc[i % NWC] if kind == "w" else wc_o[i % NWC]
                S.dma("pool", P(nc.gpsimd.dma_start, out=dst[:], in_=src), reads=[bsrc], writes=[b_wc[i % NWC]],
                      chan=b_wc[i % NWC])
                issued[0] += 1

        def chunk(i):
            assert i < issued[0], (i, issued[0])
            kind = chunk_src[i][0]
            return (wc[i % NWC] if kind == "w" else wc_o[i % NWC]), b_wc[i % NWC]

        def done(*idx):
            for i in idx:
                consumed.add(i)
            prefetch()

        prefetch()

        def rstd_chain(ss_ap, out_ap, buf, div):
            S.op("dve", P(nc.vector.tensor_scalar, out=out_ap, in0=ss_ap, scalar1=1.0 / div, scalar2=EPS, op0=ALU.mult,
                          op1=ALU.add), reads=[buf], shared=[buf])
            S.op("act", P(nc.scalar.sqrt, out=out_ap, in_=out_ap), reads=[buf], shared=[buf])
            S.op("dve", P(nc.vector.reciprocal, out=out_ap, in_=out_ap), reads=[buf], shared=[buf])

        b_dst = [None]
        nT_i = [0]

        def norm_T(x_sb, bx, np_, stb, sti, g_col, dst_cols):
            xnC, b_xnC = xnL[sti % 2], b_xnL[sti % 2]
            S.op("act", P(nc.scalar.activation, out=acc_d[:, 2:4, :].rearrange("p a n -> p (a n)")[0:np_, :],
                          in_=x_sb[0:np_, :], func=AF.Square, accum_out=stC[0:np_, sti:sti + 1]),
                 reads=[bx], writes=[b_accd[2], b_accd[3]], shared=[stb])
            rstd_chain(stC[0:np_, sti:sti + 1], stC[0:np_, 8 + sti:9 + sti], stb, float(D))
            S.op("dve", P(nc.vector.tensor_scalar, out=xnC[0:np_, :], in0=x_sb[0:np_, :],
                          scalar1=stC[0:np_, 8 + sti:9 + sti], scalar2=None, op0=ALU.mult), reads=[bx, stb], writes=[b_xnC])
            pT, bT = ringC.next()
            pTb = pT.bitcast(BF16)
            for kc in range(8):
                S.op("pe", P(nc.tensor.transpose, out=pTb[:, kc * np_:(kc + 1) * np_], in_=xnC[0:np_, kc * 128:(kc + 1) * 128],
                             identity=ident[0:np_, 0:np_]), reads=[b_xnC, b_const], shared=[bT])
            nT_i[0] += 1
            for kc in range(8):
                if nT_i[0] % 2 == 0:
                    S.op("dve", P(nc.vector.tensor_scalar, out=dst_cols(kc), in0=pTb[:, kc * np_:(kc + 1) * np_],
                                  scalar1=vecs[:, g_col + kc:g_col + kc + 1], scalar2=None, op0=ALU.mult),
                         reads=[bT, b_const], shared=[b_dst[0]])
                else:
                    S.op("act", P(nc.scalar.mul, out=dst_cols(kc), in_=pTb[:, kc * np_:(kc + 1) * np_],
                                  mul=vecs[:, g_col + kc:g_col + kc + 1]), reads=[bT, b_const], shared=[b_dst[0]])

        def proj(wt, wb, c0, m):
            p_, b_ = ringC.next()
            for kc in range(8):
                S.op("pe", P(nc.tensor.matmul, p_[0:m, :], wt[:, kc, c0:c0 + m], cur["xo"][:, kc, :],
                             start=(kc == 0), stop=(kc == 7)), reads=[wb, cur["b"]], shared=[b_])
            return p_, b_

        memT = Arena(nc, accd_off, accd_off + 4096, "MEMT").alloc([128, 8, 256], BF16, "memT")
        b_memT = Buf("memT")
        b_dst[0] = b_memT
        for mb in range(2):
            S.dma("sp", P(nc.sync.dma_start, out=xbC[mb][:], in_=mem_in[mb * 128:(mb + 1) * 128, :]), writes=[b_xbC[mb]])
            norm_T(xbC[mb], b_xbC[mb], 128, b_stC[mb], mb, VC_MG,
                   lambda kc, mb=mb: memT[:, kc, mb * 128:(mb + 1) * 128])
        wk_t, wk_b = chunk(0)
        for hx in range(4):
            p_, b_ = ringC.next()
            for kc in range(8):
                S.op("pe", P(nc.tensor.matmul, p_[:, 0:256], wk_t[:, kc, hx * 128:(hx + 1) * 128], memT[:, kc, :],
                             start=(kc == 0), stop=(kc == 7)), reads=[wk_b, b_memT, b_accd[0], b_accd[1]], shared=[b_])
            S.op("dve", P(nc.vector.tensor_copy, KM[:, hx, :], p_[:, 0:256]), reads=[b_], shared=[b_kvm])
        done(0)
        wv_t, wv_b = chunk(1)
        for mb in range(2):
            p_, b_ = ringC.next()
            for kc in range(8):
                S.op("pe", P(nc.tensor.matmul, p_, memT[:, kc, mb * 128:(mb + 1) * 128], wv_t[:, kc, :],
                             start=(kc == 0), stop=(kc == 7)), reads=[wv_b, b_memT, b_accd[0], b_accd[1]], shared=[b_])
            S.op("act", P(nc.scalar.copy, out=VM[:, mb, :], in_=p_), reads=[b_], shared=[b_kvm])
        done(1)

        def gate_merge_steps(cg, co, bias_col, rhs_t, rhs_b, mode):
            def step(fo):
                wo_t, wo_b = chunk(co)
                wt, wb = chunk(cg + fo // 4)
                py, by = ringC.next()
                for kc in range(4):
                    S.op("pe", P(nc.tensor.matmul, py, wo_t[:, kc, fo * 128:(fo + 1) * 128], rhs_t[:, kc, :],
                                 start=(kc == 0), stop=(kc == 3)), reads=[wo_b, rhs_b], shared=[by])
                pz, bz = proj(wt, wb, (fo % 4) * 128, 128)
                tt, tb = tg[fo % 2], b_tg[fo % 2]
                S.op("act", P(nc.scalar.activation, out=tt[:], in_=pz, func=AF.Tanh,
                              bias=hb[:, bias_col + fo:bias_col + fo + 1], scale=0.5), reads=[bz, b_hb], writes=[tb])
                if mode == "first":
                    S.op("dve", P(nc.vector.scalar_tensor_tensor, out=merged2[:, fo, :], in0=tt[:], scalar=1.0, in1=py,
                                  op0=ALU.add, op1=ALU.mult), reads=[tb, by], writes=[b_mg2[fo]])
                else:
                    tf, tfb = tmpf[fo % 2], b_tmpf[fo % 2]
                    S.op("dve", P(nc.vector.scalar_tensor_tensor, out=tf[:], in0=tt[:], scalar=1.0, in1=py,
                                  op0=ALU.add, op1=ALU.mult), reads=[tb, by], writes=[tfb])
                    if mode == "mid":
                        S.op("dve", P(nc.vector.tensor_tensor, out=merged2[:, fo, :], in0=merged2[:, fo, :], in1=tf[:],
                                       op=ALU.add), reads=[tfb, b_mg2[fo]], writes=[b_mg2[fo]])
                    else:
                        S.op("dve", P(nc.vector.tensor_tensor, out=merged_bf[:, fo, :], in0=merged2[:, fo, :], in1=tf[:],
                                       op=ALU.add), reads=[tfb, b_mg2[fo]], shared=[b_mbf, b_xq, b_oxg])
                if fo == 3:
                    done(cg)
                if fo == 7:
                    done(cg + 1, co)
            return [P(step, fo) for fo in range(8)]

        def run_steps(a, b=(), ratio=1):
            a, b = list(a), list(b)
            ia = ib = 0
            while ia < len(a) or ib < len(b):
                for _ in range(ratio):
                    if ia < len(a):
                        a[ia]()
                        ia += 1
                if ib < len(b):
                    b[ib]()
                    ib += 1

        def c0_blk(T, b):
            if b == 0:
                return 64, x_ext[T, 0:64, :], (lambda kc, T=T: xh_[T % 2][:, kc, :])
            return 128, x_ext[T, 64 + (b - 1) * 128:64 + b * 128, :], \
                (lambda kc, T=T, b=b: xo_[T % 2][:, kc, (b - 1) * 128:b * 128])

        junk_f = acc_d[:, 2:4, :].rearrange("p a n -> p (a n)")

        def c0_p1(T, b):
            np_, src, _ = c0_blk(T, b)
            xs, bxs, stb, sti = xbC[b % 2], b_xbC[b % 2], b_stC[b], b
            xnC, b_xnC = xnL[b % 2], b_xnL[b % 2]
            S.dma("sp", P(nc.sync.dma_start, out=xs[0:np_, :], in_=src), writes=[bxs])
            if T == 0:
                S.op("act", P(nc.scalar.activation, out=junk_f[0:np_, :], in_=xs[0:np_, :], func=AF.Square,
                              accum_out=stC[0:np_, sti:sti + 1]), reads=[bxs], writes=[b_accd[2], b_accd[3]], shared=[stb])
            else:
                S.op("act", P(nc.scalar.activation, out=xnC[0:np_, :], in_=xs[0:np_, :], func=AF.Square,
                              accum_out=stC[0:np_, sti:sti + 1]), reads=[bxs], writes=[b_xnC], shared=[stb])
            rstd_chain(stC[0:np_, sti:sti + 1], stC[0:np_, 8 + sti:9 + sti], stb, float(D))
            S.op("dve", P(nc.vector.tensor_scalar, out=xnC[0:np_, :], in0=xs[0:np_, :],
                          scalar1=stC[0:np_, 8 + sti:9 + sti], scalar2=None, op0=ALU.mult), reads=[bxs, stb], writes=[b_xnC])

        def c0_p2(T, b):
            np_, _, dst_cols = c0_blk(T, b)
            bdst = b_xn_[T % 2]
            xnC, b_xnC = xnL[b % 2], b_xnL[b % 2]
            pT, bT = ringC.next()
            pTb = pT.bitcast(BF16)
            for kc in range(8):
                S.op("pe", P(nc.tensor.transpose, out=pTb[:, kc * np_:(kc + 1) * np_], in_=xnC[0:np_, kc * 128:(kc + 1) * 128],
                             identity=ident[0:np_, 0:np_]), reads=[b_xnC, b_const], shared=[bT])
            nT_i[0] += 1
            for kc in range(8):
                if nT_i[0] % 2 == 0:
                    S.op("dve", P(nc.vector.tensor_scalar, out=dst_cols(kc), in0=pTb[:, kc * np_:(kc + 1) * np_],
                                  scalar1=vecs[:, VC_NG + kc:VC_NG + kc + 1], scalar2=None, op0=ALU.mult),
                         reads=[bT, b_const], shared=[bdst])
                else:
                    S.op("act", P(nc.scalar.mul, out=dst_cols(kc), in_=pTb[:, kc * np_:(kc + 1) * np_],
                                  mul=vecs[:, VC_NG + kc:VC_NG + kc + 1]), reads=[bT, b_const], shared=[bdst])

        def hoist(T, step):
            if T + 1 >= 4 or CSTOP < 5:
                return
            if step >= 1:
                c0_p2(T + 1, step - 1)
            if step <= 4:
                c0_p1(T + 1, step)

        def x_reload(T, blk):
            xs, bxs = xbC[blk % 2], b_xbC[blk % 2]
            S.dma("sp", P(nc.sync.dma_start, out=xs[:], in_=x_ext[T, 64 + blk * 128:64 + (blk + 1) * 128, :]),
                  writes=[bxs])

        def s2(T):
            set_cur(T)
            wval, bval = chunk(CI[T]["val"])
            wglu, bglu = chunk(CI[T]["glu"])
            ph, bh = ringC.next()
            for gi, (wt, wb) in enumerate(((wval, bval), (wglu, bglu))):
                for mc in range(4):
                    for kc in range(8):
                        S.op("pe", P(nc.tensor.matmul, ph[:, gi * 256 + mc * 64:gi * 256 + (mc + 1) * 64],
                                     wt[:, kc, mc * 128:(mc + 1) * 128], cur["xh"][:, kc, :], start=(kc == 0), stop=(kc == 7)),
                             reads=[wb, cur["b"]], shared=[bh])
            S.op("act", P(nc.scalar.activation, out=tg[0][:, 0:256], in_=ph[:, 256:512], func=AF.Tanh, scale=0.5),
                 reads=[bh], writes=[b_tg[0]])
            for mc in range(4):
                S.op("dve", P(nc.vector.scalar_tensor_tensor, out=u2[:, mc, :, 0:32],
                              in0=tg[0][:, mc * 64:(mc + 1) * 64].rearrange("p (c i) -> p c i", c=2), scalar=1.0,
                              in1=ph[:, mc * 64:(mc + 1) * 64].rearrange("p (c i) -> p c i", c=2),
                              op0=ALU.add, op1=ALU.mult), reads=[b_tg[0], bh], shared=[b_u2])
            for mc in range(4):
                pv, bv = proj(wval, bval, mc * 128, 128)
                pg, bg = proj(wglu, bglu, mc * 128, 128)
                tt, tb = tg[(mc + 1) % 2], b_tg[(mc + 1) % 2]
                S.op("act", P(nc.scalar.activation, out=tt[:], in_=pg, func=AF.Tanh, scale=0.5), reads=[bg], writes=[tb])
                S.op("dve", P(nc.vector.scalar_tensor_tensor, out=u2[:, mc, :, 32:288],
                              in0=tt[:].rearrange("p (c i) -> p c i", c=2), scalar=1.0,
                              in1=pv.rearrange("p (c i) -> p c i", c=2), op0=ALU.add, op1=ALU.mult),
                     reads=[tb, bv], shared=[b_u2])
            done(CI[T]["val"], CI[T]["glu"])


        for T in range(4 if CSTOP >= 5 else (1 if CSTOP >= 1 else 0)):
            set_cur(T)
            if T == 0:
                c0_p1(0, 0)
                for b in range(5):
                    if b + 1 < 5:
                        c0_p1(0, b + 1)
                    c0_p2(0, b)
            S.dma("sp", P(nc.sync.dma_start, out=ogp[0:64, :, :], in_=OG[0:64, T, 0:8:2, :]),
                  reads=[b_OG[T]], shared=[b_ogp], chan=b_ogp)
            S.dma("sp", P(nc.sync.dma_start, out=ogp[64:128, :, :], in_=OG[0:64, T, 1:8:2, :]),
                  reads=[b_OG[T]], shared=[b_ogp], chan=b_ogp)
            if CSTOP < 2:
                continue
            if T == 0:
                s2(0)
            set_cur(T)
            wxq, bxq = chunk(CI[T]["xq"])
            for hx in range(4):
                p_, b_ = proj(wxq, bxq, hx * 128, 128)
                S.op("act", P(nc.scalar.mul, out=xq[:, hx, :], in_=p_, mul=SCALE_X), reads=[b_], shared=[b_xq])
            done(CI[T]["xq"])
            wxg, bxg = chunk(CI[T]["xg"])
            for hx in range(4):
                p_, b_ = proj(wxg, bxg, hx * 128, 128)
                S.op("act", P(nc.scalar.activation, out=sgb[hx][:], in_=p_, func=AF.Silu), reads=[b_], writes=[b_sg[hx]])
            done(CI[T]["xg"])
            hoist(T, 0)
            wcg, bcg = chunk(CI[T]["cg"])
            for mc in range(4):
                pc, bc_ = proj(wcg, bcg, mc * 128, 128)
                S.op("act", P(nc.scalar.activation, out=sgc[mc][:], in_=pc, func=AF.Silu), reads=[bc_], writes=[b_sgc[mc]])
            done(CI[T]["cg"])
            wmg, bmg = chunk(CI[T]["mg"])
            for mc in range(4):
                p_, b_ = proj(wmg, bmg, mc * 128, 128)
                tf, tfb = tmpf[mc % 2], b_tmpf[mc % 2]
                S.op("act", P(nc.scalar.activation, out=tf[:], in_=p_, func=AF.Silu), reads=[b_], writes=[tfb])
                S.op("dve", P(nc.vector.tensor_tensor, out=ogp[:, mc, :], in0=ogp[:, mc, :], in1=tf[:], op=ALU.mult),
                     reads=[b_ogp, tfb], writes=[b_ogp])
            done(CI[T]["mg"])
            hoist(T, 1)

            def conv_step(mc, T=T):
                if T == 0 and mc == 0:
                    build_diag(0)
                if T == 0 and mc + 1 < 4:
                    build_diag(mc + 1)
                pcv, bcv = ringC.next()
                for chn in range(2):
                    for k in range(31):
                        S.op("pe", P(nc.tensor.matmul, pcv[:, chn * 256:(chn + 1) * 256], diag[:, mc * 31 + k, :],
                                     u2[:, mc, chn, 2 + k:2 + k + 256], start=(k == 0), stop=(k == 30)),
                             reads=[b_diagm[mc], b_u2], shared=[bcv])
                S.op("act", P(nc.scalar.activation, out=acc_d[:, mc, :], in_=pcv, func=AF.Identity,
                              bias=vecs[:, VC_CB + mc:VC_CB + mc + 1], scale=1.0), reads=[bcv, b_const],
                     writes=[b_accd[mc]])

            def cross_step(hx):
                po, bo = ringC.next()
                pd, bd = ringC.next()
                for mb in range(2):
                    psx, bsx = ringC.next()
                    S.op("pe", P(nc.tensor.matmul, psx, KM[:, hx, mb * 128:(mb + 1) * 128], xq[:, hx, :], start=True,
                                 stop=True), reads=[b_kvm, b_xq], shared=[bsx])
                    S.op("act", P(nc.scalar.activation, out=PTx[mb][:], in_=psx, func=AF.Exp), reads=[bsx],
                         writes=[b_PTx[mb]])
                    S.op("pe", P(nc.tensor.matmul, po, VM[:, mb, hx * 128:(hx + 1) * 128], PTx[mb][:], start=(mb == 0),
                                 stop=(mb == 1)), reads=[b_kvm, b_PTx[mb]], shared=[bo])
                    S.op("pe", P(nc.tensor.matmul, pd, ones_b[:], PTx[mb][:], start=(mb == 0), stop=(mb == 1)),
                         reads=[b_const, b_PTx[mb]], shared=[bd])
                S.op("dve", P(nc.vector.reciprocal, out=rr[:], in_=pd), reads=[bd], writes=[b_rr])
                tf, tfb = tmpf[hx % 2], b_tmpf[hx % 2]
                S.op("dve", P(nc.vector.tensor_tensor, out=tf[:], in0=po, in1=rr[:], op=ALU.mult), reads=[bo, b_rr],
                     writes=[tfb])
                S.op("dve", P(nc.vector.tensor_tensor, out=oxg[:, hx, :], in0=tf[:], in1=sgb[hx][:], op=ALU.mult),
                     reads=[tfb, b_sg[hx]], shared=[b_oxg])

            if T == 0:
                run_steps([P(cross_step, hx) for hx in range(4)], [P(conv_step, mc) for mc in range(4)])
            else:
                run_steps([P(conv_step, mc) for mc in range(4)], [P(cross_step, hx) for hx in range(4)])
            if T == 0:
                dump("cconv", acc_d[:].rearrange("p a n -> p (a n)"), b_accd)
            hoist(T, 2)
            p1, b1 = ringC.next()
            for mc in range(4):
                S.op("pe", P(nc.tensor.matmul, p1, ones_f[:], acc_d[:, mc, :], start=(mc == 0), stop=(mc == 3)),
                     reads=[b_accd[mc], b_const], shared=[b1])
            p2, b2 = ringC.next()
            for mc in range(4):
                tf, tfb = tmpf[mc % 2], b_tmpf[mc % 2]
                S.op("act", P(nc.scalar.activation, out=tf[:], in_=acc_d[:, mc, :], func=AF.Square),
                     reads=[b_accd[mc]], writes=[tfb])
                S.op("pe", P(nc.tensor.matmul, p2, ones_f[:], tf[:], start=(mc == 0), stop=(mc == 3)),
                     reads=[tfb, b_const], shared=[b2])
            S.op("dve", P(nc.vector.tensor_scalar, out=mean_sb[:], in0=p1, scalar1=1.0 / 512, scalar2=None, op0=ALU.mult),
                 reads=[b1], writes=[b_mean])
            S.op("dve", P(nc.vector.tensor_tensor, out=rr[:], in0=mean_sb[:], in1=mean_sb[:], op=ALU.mult),
                 reads=[b_mean], writes=[b_rr])
            S.op("dve", P(nc.vector.scalar_tensor_tensor, out=rstd_sb[:], in0=p2, scalar=1.0 / 512, in1=rr[:],
                          op0=ALU.mult, op1=ALU.subtract), reads=[b2, b_rr], writes=[b_rstd])
            S.op("act", P(nc.scalar.activation, out=rstd_sb[:], in_=rstd_sb[:], func=AF.Sqrt, bias=epsC[:, 0:1], scale=1.0),
                 reads=[b_rstd, b_wres], writes=[b_rstd])
            S.op("dve", P(nc.vector.reciprocal, out=rstd_sb[:], in_=rstd_sb[:]), reads=[b_rstd], writes=[b_rstd])

            def ln_step(mc):
                S.op("dve", P(nc.vector.tensor_tensor, out=acc_d[:, mc, :], in0=acc_d[:, mc, :], in1=mean_sb[:],
                              op=ALU.subtract), reads=[b_accd[mc], b_mean], writes=[b_accd[mc]])
                S.op("dve", P(nc.vector.tensor_tensor, out=acc_d[:, mc, :], in0=acc_d[:, mc, :], in1=rstd_sb[:],
                               op=ALU.mult), reads=[b_accd[mc], b_rstd], writes=[b_accd[mc]])
                S.op("act", P(nc.scalar.activation, out=acc_d[:, mc, :], in_=acc_d[:, mc, :], func=AF.Silu,
                              bias=vecs[:, VC_LB + mc:VC_LB + mc + 1], scale=vecs[:, VC_LG + mc:VC_LG + mc + 1]),
                     reads=[b_accd[mc], b_const], writes=[b_accd[mc]])
                S.op("dve", P(nc.vector.tensor_tensor, out=ug[:, mc, :], in0=acc_d[:, mc, :], in1=sgc[mc][:], op=ALU.mult),
                     reads=[b_accd[mc], b_sgc[mc]], shared=[b_ug])

            run_steps(gate_merge_steps(CI[T]["g2a"], CI[T]["wxo"], 16, oxg, b_oxg, "first"),
                      [P(ln_step, mc) for mc in range(4)], ratio=2)
            if T == 0:
                dump("m1", merged2[:].rearrange("p a n -> p (a n)"), b_mg2)
            hoist(T, 3)
            st10 = gate_merge_steps(CI[T]["g1a"], CI[T]["wmo"], 8, ogp, b_ogp, "mid")
            run_steps(st10[0:4])
            hoist(T, 4)
            run_steps(st10[4:8])
            hoist(T, 5)
            if CSTOP >= 5:
                x_reload(T, 0)
                x_reload(T, 1)
            run_steps(gate_merge_steps(CI[T]["g0a"], CI[T]["wco"], 0, ug, b_ug, "last"))
            if CSTOP < 5:
                continue
            if T + 1 < 4:
                s2(T + 1)
            wo = [chunk(CI[T]["woa"]), chunk(CI[T]["wob"])]
            for blk in range(4):
                xs, bxs = xbC[blk % 2], b_xbC[blk % 2]
                ob, bob = outb[blk % 2], b_outb[blk % 2]
                for half in range(2):
                    pf, bf = ringC.next()
                    wt, wb = wo[half]
                    for kc in range(8):
                        S.op("pe", P(nc.tensor.matmul, pf, merged_bf[:, kc, blk * 128:(blk + 1) * 128], wt[:, kc, :],
                                     start=(kc == 0), stop=(kc == 7)), reads=[b_mbf, b_xq, b_oxg, wb], shared=[bf])
                    S.op("dve", P(nc.vector.scalar_tensor_tensor, out=ob[:, half * 512:(half + 1) * 512], in0=pf, scalar=0.5,
                                  in1=xs[:, half * 512:(half + 1) * 512], op0=ALU.mult, op1=ALU.add),
                         reads=[bf, bxs], shared=[bob])
                if blk + 2 < 4:
                    x_reload(T, blk + 2)
                sti = 5 + blk % 2
                S.op("act", P(nc.scalar.activation, out=acc_d[:, 0:2, :].rearrange("p a n -> p (a n)"), in_=ob[:],
                              func=AF.Square, accum_out=stC[:, sti:sti + 1]), reads=[bob],
                     writes=[b_accd[0], b_accd[1]], shared=[b_fin])
                rstd_chain(stC[:, sti:sti + 1], stC[:, 8 + sti:9 + sti], b_fin, float(D))
                S.op("dve", P(nc.vector.scalar_tensor_tensor, out=ob[:], in0=ob[:], scalar=stC[:, 8 + sti:9 + sti],
                              in1=fgt[:], op0=ALU.mult, op1=ALU.mult), reads=[b_fin, b_wres, bob], writes=[bob])
                r0 = T * 512 + blk * 128
                S.dma("sp", P(nc.sync.dma_start, out=out[r0:r0 + 128, :], in_=ob[:]), reads=[bob], chan=bob, is_out=True)
            done(CI[T]["woa"], CI[T]["wob"])

    nwait = S.emit(nc, ES)
    return nc, nwait


ES = None


def make_inputs(c, x, mem, positions, norm_g, w_in, b_gate, conv_w, conv_b, conv_ln_g, conv_ln_b, w_conv_o,
                q_norm_g, w_uq, kv_norm_g, w_ukv, w_mla_o, mem_norm_g, w_mem_kv, w_x_o, w_out, final_norm_g,
                shared):
    b, p = c // 2, c % 2
    own, oth = OWN[p], OTH[p]
    order = own + oth
    xb = x[b]
    x_all = np.concatenate([xb[g * CH:(g + 1) * CH] for g in order], axis=0)
    pos_all = np.concatenate([positions[b, g * CH:(g + 1) * CH] for g in order], axis=0).astype(np.int32)
    x_ext = np.zeros((4, 576, D), np.float32)
    for t in range(4):
        for jj in range(2):
            g = own[2 * t + jj]
            if g > 0:
                x_ext[t, jj * 32:(jj + 1) * 32] = xb[g * CH - 32:g * CH]
            x_ext[t, 64 + jj * 256:64 + (jj + 1) * 256] = xb[g * CH:(g + 1) * CH]
    maskf = np.zeros((128, 8, 256), np.float32)
    for j in range(8):
        if not (oth[j] < own[j]):
            maskf[:, j, :] = NEG
    d = dict(shared)
    d.update(x_all=np.ascontiguousarray(x_all), x_ext=x_ext, pos_all=pos_all,
             mem=np.ascontiguousarray(mem[b]), maskf=maskf)
    return d


def make_shared(norm_g, w_in, b_gate, conv_w, conv_b, conv_ln_g, conv_ln_b, w_conv_o, q_norm_g, w_uq, kv_norm_g,
                w_ukv, w_mla_o, mem_norm_g, w_mem_kv, w_x_o, w_out, final_norm_g):
    f = np.float32
    w_in0 = w_in[0]
    vecs = np.zeros((128, NV), f)
    vecs[:, VC_BG:VC_BG + 24] = b_gate[0].reshape(24, 128).T
    vecs[:, VC_CB:VC_CB + 4] = conv_b[0].reshape(4, 128).T
    vecs[:, VC_LG:VC_LG + 4] = conv_ln_g[0].reshape(4, 128).T
    vecs[:, VC_LB:VC_LB + 4] = conv_ln_b[0].reshape(4, 128).T
    vecs[:, VC_QG:VC_QG + 3] = q_norm_g[0].reshape(3, 128).T
    vecs[:, VC_KG:VC_KG + 2] = kv_norm_g[0].reshape(2, 128).T
    inv_freq = (10000.0 ** (-np.arange(0, 32, 2, dtype=np.float32) / 32)).astype(f)
    vecs[:, VC_IF] = np.tile(inv_freq, 8)
    vecs[:, VC_PC] = np.pi / 2
    vecs[:, VC_PS] = np.tile(np.concatenate([np.full(16, np.pi), np.zeros(16)]), 4)
    vecs[:, VC_NG:VC_NG + 8] = norm_g[0].reshape(8, 128).T
    vecs[:, VC_MG:VC_MG + 8] = mem_norm_g[0].reshape(8, 128).T
    vecs[:, VC_CW:VC_CW + 124] = conv_w[0].T.reshape(4, 128, 31).transpose(1, 0, 2).reshape(128, 124)
    rope = np.arange(2176, 2208)
    rope_sw = np.concatenate([rope[16:], rope[:16]])
    junk = np.arange(1920, 1984)
    cols_a = np.concatenate([np.arange(1536, 1920), np.arange(1920, 2176), junk, rope, junk, rope_sw])
    w_a = np.ascontiguousarray(w_in0[:, cols_a])
    uq = w_uq[0]
    cols_b = []
    for h in range(8):
        base = h * 96
        cols_b += list(range(base, base + 64)) + list(range(base + 80, base + 96)) + list(range(base + 64, base + 80))
    w_uqb = np.ascontiguousarray(uq[:, cols_b])
    ukv = w_ukv[0].reshape(256, 8, 128)
    w_uk = np.ascontiguousarray(ukv[:, :, :64].reshape(256, 512))
    w_uv = np.ascontiguousarray(ukv[:, :, 64:].reshape(256, 512))
    seg = lambda a, n: np.arange(a, a + n)
    cols_c = np.concatenate([seg(0, 512), seg(512, 512), seg(1024, 512), seg(3744, 1024), seg(2720, 512),
                             seg(3232, 512), seg(5792, 1024), seg(2208, 512), seg(4768, 1024)])
    w_c = np.ascontiguousarray(np.concatenate([w_in0[:, cols_c], w_out[0]], axis=1))
    maskd = np.zeros((128, 2, 256), f)
    pp = np.arange(128)[:, None]
    qq = np.arange(256)[None, :]
    for kb in range(2):
        maskd[:, kb, :] = np.where(qq >= kb * 128 + pp, 0.0, NEG)
    return dict(vecs=vecs, norm_g=np.ascontiguousarray(norm_g[0]), final_g=np.ascontiguousarray(final_norm_g),
                mem_g=np.ascontiguousarray(mem_norm_g[0]), ident=np.eye(128, dtype=f), maskd=maskd,
                w_a=w_a, w_uqa=np.ascontiguousarray(uq), w_uqb=w_uqb, w_uk=w_uk, w_uv=w_uv, w_c=w_c,
                w_conv_o=np.ascontiguousarray(w_conv_o[0]), w_mla_o=np.ascontiguousarray(w_mla_o[0]),
                w_x_o=np.ascontiguousarray(w_x_o[0]), w_out=np.ascontiguousarray(w_out[0]),
                w_mkv=np.ascontiguousarray(w_mem_kv[0]))


def run(inputs, debug=None, phases="ABC", cores=None, trace=False):
    global ES
    inputs = {k: np.asarray(v) for k, v in inputs.items()}
    with contextlib.ExitStack() as es:
        ES = es
        nc, nwait = build_program(debug=debug, phases=phases)
        wkeys = ["norm_g", "w_in", "b_gate", "conv_w", "conv_b", "conv_ln_g", "conv_ln_b", "w_conv_o", "q_norm_g",
                 "w_uq", "kv_norm_g", "w_ukv", "w_mla_o", "mem_norm_g", "w_mem_kv", "w_x_o", "w_out", "final_norm_g"]
        shared = make_shared(**{k: inputs[k] for k in wkeys})
        cores = list(range(NCORES)) if cores is None else cores
        in_maps = [make_inputs(c, shared=shared, **inputs) for c in cores]
        res = run_bass_kernel_spmd(nc, in_maps, core_ids=list(range(len(cores))), **({"trace": True} if trace else {}))
    return res


def kernel(**inputs):
    res = run(inputs)
    x = np.asarray(inputs["x"])
    outp = np.zeros(x.shape, np.float32)
    for c in range(NCORES):
        b, p = c // 2, c % 2
        o = res.results[c]["out"]
        for j, g in enumerate(OWN[p]):
            outp[b, g * CH:(g + 1) * CH] = o[j * CH:(j + 1) * CH]
    return outp
```

```python
import contextlib
from functools import partial as P
import numpy as np
import concourse.bass as bass
import concourse.mybir as mybir
from concourse.bass_utils import run_bass_kernel_spmd

F32 = mybir.dt.float32
BF16 = mybir.dt.bfloat16
I32 = mybir.dt.int32
AF = mybir.ActivationFunctionType
ALU = mybir.AluOpType

NCORES = 8
D = 1024
SEQ = 4096
CH = 256
OWN = {0: [0, 3, 4, 7, 8, 11, 12, 15], 1: [1, 2, 5, 6, 9, 10, 13, 14]}
OTH = {0: OWN[1], 1: OWN[0]}
NEG = -30000.0
EPS = 1e-6
SCALE_MLA = 96.0 ** -0.5
SCALE_X = 128.0 ** -0.5
SB_BASE = 16512
TWO_PI = float(2 * np.pi)
CW1 = 6.28125
CW2 = float(2 * np.pi - 6.28125)
PI_LO = 3.1415925

VC_BG = 0
VC_CB = 24
VC_LG = 28
VC_LB = 32
VC_QG = 36
VC_KG = 39
VC_IF = 41
VC_PC = 42
VC_PS = 43
VC_CW = 44
VC_NG = 168
VC_MG = 176
VC_WH = 184
NV = 184


STRICT_SAME_ENGINE = True


class Buf:
    __slots__ = ("name", "writers", "readers", "sem", "dcount")

    def __init__(self, name):
        self.name = name
        self.writers = []
        self.readers = []
        self.sem = None
        self.dcount = 0


class Op:
    __slots__ = ("eng", "fn", "deps", "is_dma", "chan", "chan_count", "signal", "count")

    def __init__(self, eng, fn, is_dma=False):
        self.eng = eng
        self.fn = fn
        self.deps = []
        self.is_dma = is_dma
        self.chan = None
        self.chan_count = 0
        self.signal = False
        self.count = 0


class Sched:
    def __init__(self):
        self.ops = []
        self.last = {}
        self.barrier_deps = []
        self.chans = []
        self.chanmap = {}
        self.out_chans = []

    def _dep(self, x, y, kind):
        if y is x:
            return
        if (not y.is_dma) and (not x.is_dma) and y.eng == x.eng:
            if x.eng == "pe":
                return
            if kind != "RAW" and not STRICT_SAME_ENGINE:
                return
        x.deps.append(y)
        if not y.is_dma:
            y.signal = True

    def _add(self, x, reads, writes, shared):
        for y in self.barrier_deps:
            self._dep(x, y, "RAW")
        for b in reads:
            for w in b.writers:
                self._dep(x, w, "RAW")
        for b in writes:
            for w in b.writers:
                self._dep(x, w, "WAW")
            for r in b.readers:
                self._dep(x, r, "WAR")
        for b in shared:
            if b.readers:
                for w in b.writers:
                    self._dep(x, w, "WAW")
                for r in b.readers:
                    self._dep(x, r, "WAR")
        for b in reads:
            b.readers.append(x)
        for b in writes:
            b.writers = [x]
            b.readers = []
        for b in shared:
            if b.readers:
                b.writers = [x]
                b.readers = []
            else:
                b.writers.append(x)
        self.ops.append(x)
        self.last[x.eng if not x.is_dma else ("dma", id(x.chan))] = x

    def op(self, eng, fn, reads=(), writes=(), shared=()):
        x = Op(eng, fn)
        self._add(x, reads, writes, shared)
        return x

    def dma(self, eng, fn, reads=(), writes=(), shared=(), chan=None, is_out=False):
        x = Op(eng, fn, is_dma=True)
        if chan is None:
            chan = (list(writes) + list(shared) + list(reads))[0]
        key = (id(chan), eng)
        if key not in self.chanmap:
            self.chanmap[key] = Buf("chan_%s_%s" % (chan.name, eng))
            self.chans.append(self.chanmap[key])
        chan = self.chanmap[key]
        x.chan = chan
        chan.dcount += 1
        x.chan_count = chan.dcount
        if is_out and chan not in self.out_chans:
            self.out_chans.append(chan)
        self._add(x, reads, writes, shared)
        return x

    def barrier(self):
        self.barrier_deps = list(self.last.values())
        for y in self.barrier_deps:
            if not y.is_dma:
                y.signal = True

    def emit(self, nc, es):
        engs = {"pe": nc.tensor, "act": nc.scalar, "dve": nc.vector, "pool": nc.gpsimd, "sp": nc.sync}
        sems = {}
        for e in ("pe", "act", "dve", "pool"):
            sems[e] = es.enter_context(nc.semaphore("s_" + e))
        for i, c in enumerate(self.chans):
            c.sem = es.enter_context(nc.semaphore("d%d" % i))
        cnt = {e: 0 for e in sems}
        for x in self.ops:
            if not x.is_dma and x.signal:
                cnt[x.eng] += 1
                x.count = cnt[x.eng]
        known = {e: {} for e in engs}
        nwait = 0
        for x in self.ops:
            e = engs[x.eng]
            need = {}
            for y in x.deps:
                if y.is_dma:
                    key, val = y.chan.sem, 16 * y.chan_count
                else:
                    key, val = sems[y.eng], y.count
                k = id(key)
                if k not in need or need[k][1] < val:
                    need[k] = (key, val)
            kn = known[x.eng]
            for k, (key, val) in need.items():
                if kn.get(k, 0) < val:
                    e.wait_ge(key, val)
                    kn[k] = val
                    nwait += 1
            ins = x.fn()
            if x.is_dma:
                ins.then_inc(x.chan.sem, 16)
            elif x.signal:
                ins.then_inc(sems[x.eng], 1)
        for c in self.out_chans:
            nc.sync.wait_ge(c.sem, 16 * c.dcount)
        return nwait


class Arena:
    def __init__(self, nc, lo, hi, tag):
        self.nc, self.lo, self.hi, self.tag = nc, lo, hi, tag
        self.off = lo
        self.n = 0

    def alloc(self, shape, dtype, name="t"):
        nbytes = int(np.prod(shape[1:])) * (2 if dtype == BF16 else 4)
        nbytes = (nbytes + 63) // 64 * 64
        assert self.off + nbytes <= self.hi, (self.tag, name, self.off, nbytes, self.hi)
        t = self.nc.alloc_sbuf_tensor_at("%s_%s%d" % (self.tag, name, self.n), list(shape), dtype,
                                         offset=SB_BASE + self.off)
        self.off += nbytes
        self.n += 1
        return t


class PsumRing:
    def __init__(self, ps, banks):
        self.ps = ps
        self.banks = list(banks)
        self.bufs = {b: Buf("ps%d" % b) for b in self.banks}
        self.i = 0

    def next(self):
        b = self.banks[self.i % len(self.banks)]
        self.i += 1
        return self.ps[:, b, :], self.bufs[b]


CSTOP = 9


def build_program(debug=None, phases="ABC"):
    nc = bass.Bass("TRN2", target_bir_lowering=False)
    S = Sched()

    def din(name, shape, dt=F32):
        return nc.dram_tensor(name, list(shape), dt, kind="ExternalInput").ap()

    x_all = din("x_all", [4096, D])
    x_ext = din("x_ext", [4, 576, D])
    pos_all = din("pos_all", [4096], I32)
    mem_in = din("mem", [256, D])
    vecs_in = din("vecs", [128, NV])
    gb_in = din("norm_g", [D])
    fg_in = din("final_g", [D])
    mg_in = din("mem_g", [D])
    ident_in = din("ident", [128, 128])
    maskd_in = din("maskd", [128, 2, 256])
    maskf_in = din("maskf", [128, 8, 256])
    w_a_in = din("w_a", [D, 832])
    w_uqa_in = din("w_uqa", [384, 768])
    w_uqb_in = din("w_uqb", [384, 768])
    w_uk_in = din("w_uk", [256, 512])
    w_uv_in = din("w_uv", [256, 512])
    w_c_in = din("w_c", [D, 7168])
    w_conv_o_in = din("w_conv_o", [512, D])
    w_mla_o_in = din("w_mla_o", [512, D])
    w_x_o_in = din("w_x_o", [512, D])
    w_out_in = din("w_out", [D, D])
    w_mkv_in = din("w_mkv", [D, D])
    out = nc.dram_tensor("out", [2048, D], F32, kind="ExternalOutput").ap()
    dbg = {}
    if debug:
        for nm, shp in debug.items():
            dbg[nm] = nc.dram_tensor("dbg_" + nm, list(shp), F32, kind="ExternalOutput").ap()

    ps = nc.alloc_psum_tensor("ps", [128, 8, 512], F32)

    P_ = Arena(nc, 0, 164352, "P")
    KT = P_.alloc([128, 8, 4096], BF16, "KT")
    VA = P_.alloc([128, 32, 8, 65], BF16, "VA")
    QT = P_.alloc([128, 8, 2048], BF16, "QT")
    OG = P_.alloc([128, 4, 8, 512], BF16, "OG")
    CST = Arena(nc, 208768, 212864, "C")
    ident = CST.alloc([128, 128], BF16, "ident")
    ones_f = CST.alloc([128, 128], F32, "ones_f")
    ones_b = CST.alloc([128, 128], BF16, "ones_b")
    vecs = CST.alloc([128, NV], F32, "vecs")
    hb = CST.alloc([128, 24], F32, "hb")
    zero_c = CST.alloc([128, 1], F32, "zero")
    eps_c = CST.alloc([128, 1], F32, "eps")
    b_const = Buf("consts")
    b_KT = [Buf("KT%d" % t) for t in range(8)]
    b_VA = [Buf("VA%d" % t) for t in range(8)]
    b_QT = [Buf("QT%d" % t) for t in range(4)]
    b_OG = [Buf("OG%d" % t) for t in range(4)]

    def dump(name, ap_sb, bufs, dst=None, cast=False):
        if name in dbg and cast:
            S.dma("pool", P(nc.gpsimd.dma_start, out=(dst if dst is not None else dbg[name]), in_=ap_sb),
                  reads=bufs, chan=Buf("dbg_" + name), is_out=True)
        elif name in dbg:
            S.dma("sp", P(nc.sync.dma_start, out=(dst if dst is not None else dbg[name]), in_=ap_sb),
                  reads=bufs, chan=Buf("dbg_" + name), is_out=True)

    S.dma("pool", P(nc.gpsimd.dma_start, out=ident[:], in_=ident_in[:, :]), shared=[b_const])
    S.dma("sp", P(nc.sync.dma_start, out=vecs[:], in_=vecs_in[:, :]), shared=[b_const])
    S.op("dve", P(nc.vector.memset, ones_f[:], 1.0), shared=[b_const])
    S.op("dve", P(nc.vector.memset, ones_b[:], 1.0), shared=[b_const])
    S.op("dve", P(nc.vector.memset, zero_c[:], 0.0), shared=[b_const])
    S.op("dve", P(nc.vector.memset, eps_c[:], EPS), shared=[b_const])
    b_hb = Buf("hb")
    S.op("dve", P(nc.vector.tensor_scalar, out=hb[:], in0=vecs[:, VC_BG:VC_BG + 24], scalar1=0.5,
                                                scalar2=None, op0=ALU.mult), reads=[b_const], writes=[b_hb])

    wc_bf = nc.dram_tensor("wc_bf", [14, 128, 8, 512], BF16, kind="Internal").ap()
    wo_bf = nc.dram_tensor("wo_bf", [3, 128, 4, D], BF16, kind="Internal").ap()
    wmkv_bf = nc.dram_tensor("wmkv_bf", [2, 128, 8, 512], BF16, kind="Internal").ap()
    b_wcv = {("c", ck): Buf("wcv_c%d" % ck) for ck in range(14)}
    b_wcv.update({("o", i): Buf("wcv_o%d" % i) for i in range(3)})
    b_wcv.update({("m", ck): Buf("wcv_m%d" % ck) for ck in range(2)})

    def convert_weights():
        def cv_c(ck):
            S.dma("pool", P(nc.gpsimd.dma_start, out=wc_bf[ck],
                            in_=w_c_in[:, ck * 512:(ck + 1) * 512].rearrange("(kc p) n -> p kc n", p=128)),
                  writes=[b_wcv[("c", ck)]])

        def cv_o(i):
            win_ = (w_x_o_in, w_mla_o_in, w_conv_o_in)[i]
            S.dma("pool", P(nc.gpsimd.dma_start, out=wo_bf[i], in_=win_.rearrange("(kc p) n -> p kc n", p=128)),
                  writes=[b_wcv[("o", i)]])
        for ck in range(2):
            S.dma("pool", P(nc.gpsimd.dma_start, out=wmkv_bf[ck],
                            in_=w_mkv_in[:, ck * 512:(ck + 1) * 512].rearrange("(kc p) n -> p kc n", p=128)),
                  writes=[b_wcv[("m", ck)]])
        for ck in (0, 1, 5, 6, 2, 9):
            cv_c(ck)
        cv_o(0)
        cv_c(7)
        cv_c(8)
        cv_o(1)
        cv_c(10)
        cv_c(11)
        cv_o(2)
        for ck in (3, 4, 12, 13):
            cv_c(ck)

    if "A" in phases:
        A_ = Arena(nc, 131584, 208768, "A")
        ckv = A_.alloc([128, 2, 4096], BF16, "ckv")
        cq = A_.alloc([128, 3, 2048], BF16, "cq")
        cs = A_.alloc([128, 4096], BF16, "cs")
        sn = A_.alloc([128, 4096], BF16, "sn")
        w_uqa = A_.alloc([128, 3, 768], BF16, "w_uqa")
        w_uqb = A_.alloc([128, 3, 768], BF16, "w_uqb")
        w_uk = A_.alloc([128, 2, 512], BF16, "w_uk")
        w_uv = A_.alloc([128, 2, 512], BF16, "w_uv")
        t1q = [A_.alloc([128, 512], F32, "t1q") for _ in range(2)]
        t2q = [A_.alloc([128, 512], F32, "t2q") for _ in range(2)]
        A1_ = Arena(nc, 65536, 131584, "A1")
        w_a = A1_.alloc([128, 8, 832], BF16, "w_a")
        xb = [A1_.alloc([128, D], F32, "xb") for _ in range(4)]
        xn = [A1_.alloc([128, D], BF16, "xn") for _ in range(2)]
        xnT = [A1_.alloc([128, 8, 512], BF16, "xnT") for _ in range(2)]
        a1_mark = A1_.off
        sqb = [A1_.alloc([128, 3, 512], BF16, "sqb") for _ in range(2)]
        rsb = [A1_.alloc([128, 512], F32, "rsb") for _ in range(2)]
        t1 = A1_.alloc([128, 512], F32, "t1")
        t2 = A1_.alloc([128, 512], F32, "t2")
        kr = [A_.alloc([128, 512], BF16, "kr") for _ in range(2)]
        st = A_.alloc([128, 16], F32, "st")
        A2_ = Arena(nc, a1_mark, 131584, "A2")
        rp_i = A2_.alloc([128, 1024], I32, "rp_i")
        rp_a = A2_.alloc([128, 1024], F32, "rp_a")
        rp_n = A2_.alloc([128, 1024], F32, "rp_n")
        rp_o = A2_.alloc([128, 1024], BF16, "rp_o")
        b_w, b_w0, b_w2 = Buf("wA"), Buf("w_a_raw"), Buf("wA2")
        b_cs = Buf("cs")
        b_xb = [Buf("xb%d" % i) for i in range(4)]
        b_xn = [Buf("xn0"), Buf("xn1")]
        b_xnT = [Buf("xnT0"), Buf("xnT1")]
        b_sq, b_rs = [Buf("sq0"), Buf("sq1")], [Buf("rs0"), Buf("rs1")]
        b_t1, b_t2, b_kr = Buf("t1"), Buf("t2"), [Buf("kr0"), Buf("kr1")]
        b_stl = [Buf("st%d" % i) for i in range(8)]
        b_rpi, b_rpa, b_rpn, b_rpo = Buf("rpi"), Buf("rpa"), Buf("rpn"), Buf("rpo")
        b_ckv = [Buf("ckv%d" % t) for t in range(8)]
        b_cq = [Buf("cq%d" % t) for t in range(4)]
        b_t1q, b_t2q = [Buf("t1q0"), Buf("t1q1")], [Buf("t2q0"), Buf("t2q1")]

        S.dma("pool", P(nc.gpsimd.dma_start, out=w_a[:], in_=w_a_in.rearrange("(kc p) n -> p kc n", p=128)),
              writes=[b_w0])
        for wt_, win_ in ((w_uk, w_uk_in), (w_uv, w_uv_in), (w_uqa, w_uqa_in), (w_uqb, w_uqb_in)):
            S.dma("pool", P(nc.gpsimd.dma_start, out=wt_[:], in_=win_.rearrange("(kc p) n -> p kc n", p=128)),
                  shared=[b_w2])
        for g in range(4):
            S.dma("sp", P(nc.sync.dma_start, out=rp_i[32 * g:32 * g + 32, :],
                          in_=pos_all[g * 1024:(g + 1) * 1024].partition_broadcast(32)), shared=[b_rpi])
        for tbl, pcol in ((cs, VC_PC), (sn, VC_PS)):
            S.op("dve", P(nc.vector.tensor_copy, rp_a[:], rp_i[:]), reads=[b_rpi], writes=[b_rpa])
            S.op("dve", P(nc.vector.tensor_scalar, out=rp_a[:], in0=rp_a[:], scalar1=vecs[:, VC_IF:VC_IF + 1],
                          scalar2=vecs[:, pcol:pcol + 1], op0=ALU.mult, op1=ALU.add),
                 reads=[b_rpa, b_const], writes=[b_rpa])
            S.op("dve", P(nc.vector.tensor_scalar, out=rp_n[:], in0=rp_a[:], scalar1=1.0 / TWO_PI,
                          scalar2=None, op0=ALU.mult), reads=[b_rpa], writes=[b_rpn])
            S.op("dve", P(nc.vector.tensor_copy, rp_i[:], rp_n[:]), reads=[b_rpn], writes=[b_rpi])
            S.op("dve", P(nc.vector.tensor_copy, rp_n[:], rp_i[:]), reads=[b_rpi], writes=[b_rpn])
            if tbl is cs:
                for g in range(4):
                    S.dma("sp", P(nc.sync.dma_start, out=rp_i[32 * g:32 * g + 32, :],
                                  in_=pos_all[g * 1024:(g + 1) * 1024].partition_broadcast(32)), shared=[b_rpi])
            S.op("dve", P(nc.vector.scalar_tensor_tensor, out=rp_a[:], in0=rp_n[:], scalar=-CW1,
                          in1=rp_a[:], op0=ALU.mult, op1=ALU.add), reads=[b_rpn, b_rpa], writes=[b_rpa])
            S.op("dve", P(nc.vector.scalar_tensor_tensor, out=rp_a[:], in0=rp_n[:], scalar=-CW2,
                          in1=rp_a[:], op0=ALU.mult, op1=ALU.add), reads=[b_rpn, b_rpa], writes=[b_rpa])
            S.op("dve", P(nc.vector.tensor_scalar, out=rp_a[:], in0=rp_a[:], scalar1=-PI_LO, scalar2=PI_LO,
                          op0=ALU.max, op1=ALU.min), reads=[b_rpa], writes=[b_rpa])
            S.op("act", P(nc.scalar.activation, out=rp_o[:], in_=rp_a[:], func=AF.Sin, bias=zero_c[:, :], scale=1.0),
                 reads=[b_rpa, b_const], writes=[b_rpo])
            for g in range(4):
                S.dma("sp", P(nc.sync.dma_start, out=tbl[64:96, g * 1024:(g + 1) * 1024], in_=rp_o[32 * g:32 * g + 32, :]),
                      reads=[b_rpo], shared=[b_cs], chan=b_cs)
        for kc in range(8):
            S.op("dve", P(nc.vector.tensor_scalar, out=w_a[:, kc, :], in0=w_a[:, kc, :],
                          scalar1=vecs[:, VC_NG + kc:VC_NG + kc + 1], scalar2=None, op0=ALU.mult),
                 reads=[b_w0, b_const], shared=[b_w])
        S.barrier()

        R = slice(64, 96)
        ringT = PsumRing(ps, [0, 1])
        ringP = PsumRing(ps, [2, 3, 4, 5, 6, 7])
        evac_i = [0]

        def evac_copy(out_ap, in_ap, reads, writes=(), shared=(), scale=None):
            evac_i[0] += 1
            if evac_i[0] % 2 == 0:
                if scale is None:
                    S.op("act", P(nc.scalar.copy, out=out_ap, in_=in_ap), reads=reads, writes=writes, shared=shared)
                else:
                    S.op("act", P(nc.scalar.mul, out=out_ap, in_=in_ap, mul=scale), reads=reads, writes=writes,
                         shared=shared)
            else:
                if scale is None:
                    S.op("dve", P(nc.vector.tensor_copy, out_ap, in_ap), reads=reads, writes=writes, shared=shared)
                else:
                    S.op("dve", P(nc.vector.tensor_scalar, out=out_ap, in0=in_ap, scalar1=scale, scalar2=None,
                                  op0=ALU.mult), reads=reads, writes=writes, shared=shared)

        ngrp = [0]

        def front1(t):
            tok0 = t * 512
            b_st = b_stl[t % 2]
            c0_ = (t % 2) * 4
            for blk in range(4):
                r0 = tok0 + blk * 128
                S.dma("sp", P(nc.sync.dma_start, out=xb[blk][:], in_=x_all[r0:r0 + 128, :]), writes=[b_xb[blk]])
            for blk in range(4):
                xni, bxn = xn[blk % 2], b_xn[blk % 2]
                S.op("act", P(nc.scalar.activation, out=xni[:], in_=xb[blk][:], func=AF.Square,
                              accum_out=st[:, c0_ + blk:c0_ + blk + 1]), reads=[b_xb[blk]], writes=[bxn], shared=[b_st])
            S.op("act", P(nc.scalar.activation, out=st[:, 8 + c0_:12 + c0_], in_=st[:, c0_:c0_ + 4], func=AF.Ln,
                          bias=eps_c[:, 0:1], scale=1.0 / D), reads=[b_st, b_const], writes=[b_st])
            S.op("act", P(nc.scalar.activation, out=st[:, 8 + c0_:12 + c0_], in_=st[:, 8 + c0_:12 + c0_], func=AF.Exp,
                          scale=-0.5), reads=[b_st], writes=[b_st])

        def front2(t):
            xT, bxT = xnT[t % 2], b_xnT[t % 2]
            b_st = b_stl[t % 2]
            c0_ = (t % 2) * 4
            for blk in range(4):
                xni, bxn = xn[blk % 2], b_xn[blk % 2]
                S.op("dve", P(nc.vector.tensor_scalar, out=xni[:], in0=xb[blk][:],
                              scalar1=st[:, 8 + c0_ + blk:9 + c0_ + blk], scalar2=None, op0=ALU.mult),
                     reads=[b_xb[blk], b_st], writes=[bxn])
                pT, bT = ringT.next()
                pTb = pT.bitcast(BF16)
                for kc in range(8):
                    S.op("pe", P(nc.tensor.transpose, out=pTb[:, kc * 128:(kc + 1) * 128],
                                 in_=xni[:, kc * 128:(kc + 1) * 128], identity=ident[:]),
                         reads=[bxn, b_const], shared=[bT])
                evac_copy(xT[:, :, blk * 128:(blk + 1) * 128], pTb.rearrange("p (k n) -> p k n", k=8),
                          reads=[bT], shared=[bxT])

        front1(0)
        front2(0)
        for t in range(8):
            own = t < 4
            tok0 = t * 512
            xT, bxT = xnT[t % 2], b_xnT[t % 2]
            if t + 1 < 8:
                front1(t + 1)

            def proj(c0, m, xT=xT, bxT=bxT):
                p_, b_ = ringP.next()
                for kc in range(8):
                    S.op("pe", P(nc.tensor.matmul, p_[0:m, :], w_a[:, kc, c0:c0 + m], xT[:, kc, :],
                                 start=(kc == 0), stop=(kc == 7)), reads=[b_w, bxT], shared=[b_])
                return p_, b_

            def norm_group(pbs, nch, gcol, div, dst, bdst):
                gi = ngrp[0] % 2
                ngrp[0] += 1
                sq_, bsq, rs_, brs = sqb[gi], b_sq[gi], rsb[gi], b_rs[gi]
                for i, (p_, b_) in enumerate(pbs):
                    S.op("act", P(nc.scalar.activation, out=sq_[:, i, :], in_=p_, func=AF.Square),
                         reads=[b_], shared=[bsq])
                pss, bss = ringP.next()
                for i in range(nch):
                    S.op("pe", P(nc.tensor.matmul, pss, ones_b[:], sq_[:, i, :], start=(i == 0), stop=(i == nch - 1)),
                         reads=[bsq, b_const], shared=[bss])
                S.op("act", P(nc.scalar.activation, out=rs_[:], in_=pss, func=AF.Ln, bias=eps_c[:, 0:1],
                              scale=1.0 / div), reads=[bss, b_const], writes=[brs])
                S.op("act", P(nc.scalar.activation, out=rs_[:], in_=rs_[:], func=AF.Exp, scale=-0.5),
                     reads=[brs], writes=[brs])
                for i, (p_, b_) in enumerate(pbs):
                    S.op("dve", P(nc.vector.scalar_tensor_tensor, out=dst(i), in0=p_,
                                  scalar=vecs[:, gcol + i:gcol + i + 1], in1=rs_[:], op0=ALU.mult, op1=ALU.mult),
                         reads=[b_, brs, b_const], shared=[bdst])

            kvp = [proj(384 + 128 * i, 128) for i in range(2)]
            pA, bA = proj(640, 96)
            pB, bB = proj(736, 96)
            if t + 1 < 8:
                front2(t + 1)
            S.op("dve", P(nc.vector.tensor_tensor, out=t1[R, :], in0=pA[R, :], in1=cs[R, tok0:tok0 + 512],
                          op=ALU.mult), reads=[bA, b_cs], writes=[b_t1])
            S.op("dve", P(nc.vector.tensor_tensor, out=t2[R, :], in0=pB[R, :], in1=sn[R, tok0:tok0 + 512],
                          op=ALU.mult), reads=[bB, b_cs], writes=[b_t2])
            kr_, bkr = kr[t % 2], b_kr[t % 2]
            S.op("dve", P(nc.vector.tensor_tensor, out=kr_[R, :], in0=t1[R, :], in1=t2[R, :], op=ALU.add),
                 reads=[b_t1, b_t2], writes=[bkr])
            for h in range(8):
                if h % 2 == 0:
                    S.op("act", P(nc.scalar.copy, out=KT[R, h, tok0:tok0 + 512], in_=kr_[R, :]),
                         reads=[bkr], shared=[b_KT[t]])
                else:
                    S.op("dve", P(nc.vector.tensor_copy, KT[R, h, tok0:tok0 + 512], kr_[R, :]),
                         reads=[bkr], shared=[b_KT[t]])
            norm_group(kvp, 2, VC_KG, 256.0, lambda i, tok0=tok0: ckv[:, i, tok0:tok0 + 512], b_ckv[t])
            if own:
                qp = [proj(128 * i, 128) for i in range(3)]
                norm_group(qp, 3, VC_QG, 384.0, lambda i, tok0=tok0: cq[:, i, tok0:tok0 + 512], b_cq[t])
        S.barrier()
        S.op("pool", P(nc.gpsimd.memset, VA[:, :, :, 64:65], 1.0), shared=b_VA)
        ringQ = PsumRing(ps, [0, 1, 2, 3, 4, 5, 6, 7])
        nq = [0]
        for t in range(8):
            own = t < 4
            tok0 = t * 512
            for h in range(8):
                p_, b_ = ringQ.next()
                for kc in range(2):
                    S.op("pe", P(nc.tensor.matmul, p_[0:64, :], w_uk[:, kc, h * 64:(h + 1) * 64],
                                 ckv[:, kc, tok0:tok0 + 512], start=(kc == 0), stop=(kc == 1)),
                         reads=[b_w2, b_ckv[t]], shared=[b_])
                evac_copy(KT[0:64, h, tok0:tok0 + 512], p_[0:64, :], reads=[b_], shared=[b_KT[t]])
            for blk in range(4):
                p_, b_ = ringQ.next()
                for kc in range(2):
                    S.op("pe", P(nc.tensor.matmul, p_, ckv[:, kc, tok0 + blk * 128:tok0 + (blk + 1) * 128],
                                 w_uv[:, kc, :], start=(kc == 0), stop=(kc == 1)),
                         reads=[b_w2, b_ckv[t]], shared=[b_])
                evac_copy(VA[:, t * 4 + blk, :, 0:64], p_.rearrange("p (h d) -> p h d", h=8), reads=[b_],
                          shared=[b_VA[t]])
            if own:
                for h in range(8):
                    pa, ba = ringQ.next()
                    for kc in range(3):
                        S.op("pe", P(nc.tensor.matmul, pa[0:96, :], w_uqa[:, kc, h * 96:(h + 1) * 96],
                                     cq[:, kc, tok0:tok0 + 512], start=(kc == 0), stop=(kc == 2)),
                             reads=[b_w2, b_cq[t]], shared=[ba])
                    pb, bb = ringQ.next()
                    for kc in range(3):
                        S.op("pe", P(nc.tensor.matmul, pb[0:96, :], w_uqb[:, kc, h * 96:(h + 1) * 96],
                                     cq[:, kc, tok0:tok0 + 512], start=(kc == 0), stop=(kc == 2)),
                             reads=[b_w2, b_cq[t]], shared=[bb])
                    qi = nq[0] % 2
                    nq[0] += 1
                    S.op("act", P(nc.scalar.mul, out=QT[0:64, h, tok0:tok0 + 512], in_=pa[0:64, :],
                                  mul=SCALE_MLA), reads=[ba], shared=[b_QT[t]])
                    S.op("dve", P(nc.vector.scalar_tensor_tensor, out=t1q[qi][R, :], in0=pa[R, :], scalar=SCALE_MLA,
                                  in1=cs[R, tok0:tok0 + 512], op0=ALU.mult, op1=ALU.mult),
                         reads=[ba, b_cs], writes=[b_t1q[qi]])
                    S.op("dve", P(nc.vector.scalar_tensor_tensor, out=t2q[qi][R, :], in0=pb[R, :], scalar=SCALE_MLA,
                                  in1=sn[R, tok0:tok0 + 512], op0=ALU.mult, op1=ALU.mult),
                         reads=[bb, b_cs], writes=[b_t2q[qi]])
                    S.op("pool", P(nc.gpsimd.tensor_tensor, out=QT[R, h, tok0:tok0 + 512], in0=t1q[qi][R, :],
                                   in1=t2q[qi][R, :], op=ALU.add),
                         reads=[b_t1q[qi], b_t2q[qi]], shared=[b_QT[t]])
        S.barrier()
        if debug and ("kt" in dbg or "va" in dbg or "qt" in dbg):
            dstage = Arena(nc, 131584, 208768, "DBG").alloc([128, 2080], F32, "dstage")
            b_ds = Buf("dstage")
            if "kt" in dbg:
                for i, h in enumerate((0, 5)):
                    for half in range(2):
                        S.op("dve", P(nc.vector.tensor_copy, dstage[0:96, 0:2048], KT[0:96, h, half * 2048:(half + 1) * 2048]),
                             reads=b_KT, writes=[b_ds])
                        dump("kt", dstage[0:96, 0:2048], [b_ds], dst=dbg["kt"][i, :, half * 2048:(half + 1) * 2048])
            if "va" in dbg:
                S.op("dve", P(nc.vector.tensor_copy, dstage[:, 0:2080], VA[:, 0:4, :, :].rearrange("p a h d -> p (a h d)")),
                     reads=b_VA, writes=[b_ds])
                dump("va", dstage[:, 0:2080], [b_ds])
            if "qt" in dbg:
                S.op("dve", P(nc.vector.tensor_copy, dstage[0:96, 0:2048], QT[0:96, 3, :]), reads=b_QT, writes=[b_ds])
                dump("qt", dstage[0:96, 0:2048], [b_ds])
            S.barrier()

    if "B" in phases:
        B_ = Arena(nc, 164352, 208768, "B")
        maskd = B_.alloc([128, 2, 256], BF16, "maskd")
        maskf = B_.alloc([128, 8, 256], BF16, "maskf")
        NPT = 6
        PT = [B_.alloc([128, 512], BF16, "PT") for _ in range(NPT)]
        b_PT = [Buf("PT%d" % i) for i in range(NPT)]
        o_sb = [B_.alloc([128, 512], F32, "o_sb") for _ in range(2)]
        rden = [B_.alloc([128, 512], F32, "rden") for _ in range(2)]
        b_osb = [Buf("osb0"), Buf("osb1")]
        b_rden = [Buf("rden0"), Buf("rden1")]
        b_mask = Buf("masks")
        S.dma("pool", P(nc.gpsimd.dma_start, out=maskd[:], in_=maskd_in[:, :, :]), shared=[b_mask])
        S.dma("pool", P(nc.gpsimd.dma_start, out=maskf[:], in_=maskf_in[:, :, :]), shared=[b_mask])
        convert_weights()
        ringS = PsumRing(ps, [3, 4, 5, 6, 7])
        b_O = [Buf("O0"), Buf("O1")]
        b_bc = Buf("bc")
        items = []
        for T in range(4):
            for h in range(8):
                blocks = []
                for grp in range(2):
                    for j in range(2 * T + 2):
                        c = j + 8 * grp
                        for kb in range(2):
                            if j < 2 * T:
                                blocks.append((c, kb, 0, 512, None))
                            elif j == 2 * T:
                                m = maskd[:, kb, :] if grp == 0 else maskf[:, j, :]
                                blocks.append((c, kb, 0, 512, (m, 0)))
                            else:
                                m = maskd[:, kb, :] if grp == 0 else maskf[:, j, :]
                                blocks.append((c, kb, 256, 512, (m, 256)))
                for bi, blk in enumerate(blocks):
                    items.append((T, h, blk, bi == 0, bi == len(blocks) - 1))
        N = len(items)
        LOOK = 3
        st_ = {}

        def emit_qk(i):
            T, h, (c, kb, qlo, qhi, mask), first, last = items[i]
            sp_, sb_ = ringS.next()
            kcol = c * 256 + kb * 128
            S.op("pe", P(nc.tensor.matmul, sp_[:, qlo:qhi], KT[0:96, h, kcol:kcol + 128],
                         QT[0:96, h, T * 512 + qlo:T * 512 + qhi], start=True, stop=(mask is None)),
                 reads=[b_KT[c // 2], b_QT[T]], shared=[sb_])
            if mask is not None:
                m, mlo = mask
                S.op("pe", P(nc.tensor.matmul, sp_[:, mlo:mlo + 256], ident[:], m, start=False, stop=True),
                     reads=[b_mask, b_const], shared=[sb_])
            pt = PT[i % NPT]
            S.op("act", P(nc.scalar.activation, out=pt[:, qlo:qhi], in_=sp_[:, qlo:qhi], func=AF.Exp),
                 reads=[sb_], writes=[b_PT[i % NPT]])

        def emit_pv(i):
            T, h, (c, kb, qlo, qhi, mask), first, last = items[i]
            ob = (T * 8 + h) % 2
            O = ps[:, ob, :]
            pt = PT[i % NPT]
            S.op("pe", P(nc.tensor.matmul, O[0:65, qlo:qhi], VA[:, c * 2 + kb, h, :], pt[:, qlo:qhi],
                         start=first, stop=last),
                 reads=[b_VA[c // 2], b_PT[i % NPT]], shared=[b_O[ob]])
            if last:
                S.op("dve", P(nc.vector.tensor_copy, o_sb[ob][0:65, :], O[0:65, :]), reads=[b_O[ob]],
                     writes=[b_osb[ob]])
                S.op("dve", P(nc.vector.reciprocal, out=rden[ob][64:65, :], in_=o_sb[ob][64:65, :]),
                     reads=[b_osb[ob]], writes=[b_rden[ob]])
                pending.append((i + FIN_DELAY, T, h, ob))

        def emit_fin(T, h, ob):
            bc = ps[:, 2, :]
            S.op("pe", P(nc.tensor.matmul, bc[0:64, :], ones_f[64:65, 0:64], rden[ob][64:65, :],
                         start=True, stop=True), reads=[b_rden[ob], b_const], writes=[b_bc])
            S.op("dve", P(nc.vector.tensor_tensor, out=OG[0:64, T, h, :], in0=o_sb[ob][0:64, :],
                          in1=bc[0:64, :], op=ALU.mult), reads=[b_osb[ob], b_bc], shared=[b_OG[T]])

        pending = []
        FIN_DELAY = 7
        for i in range(N + LOOK):
            if i < N:
                emit_qk(i)
            if i >= LOOK:
                emit_pv(i - LOOK)
            while pending and pending[0][0] <= i - LOOK:
                _, T_, h_, ob_ = pending.pop(0)
                emit_fin(T_, h_, ob_)
        for _, T_, h_, ob_ in pending:
            emit_fin(T_, h_, ob_)
        S.barrier()
        if debug and "og" in dbg:
            dstage2 = Arena(nc, 0, 65536, "DBG2").alloc([128, 2048], F32, "dstage2")
            b_ds2 = Buf("dstage2")
            for i, h in enumerate((0, 6)):
                for T_ in range(4):
                    S.op("dve", P(nc.vector.tensor_copy, dstage2[0:64, T_ * 512:(T_ + 1) * 512], OG[0:64, T_, h, :]),
                         reads=b_OG, shared=[b_ds2])
                dump("og", dstage2[0:64, :], [b_ds2], dst=dbg["og"][i])
            S.barrier()

    if "C" in phases:
        C_ = Arena(nc, 0, 131584, "C1")
        C2_ = Arena(nc, 164352, 208768, "C2")
        fgt = C_.alloc([128, D], F32, "fgt")
        KM = C_.alloc([128, 4, 256], BF16, "KM")
        VM = C_.alloc([128, 2, 512], BF16, "VM")
        wh = C_.alloc([128, 124], F32, "wh")
        epsC = C_.alloc([128, 1], F32, "epsC")
        diag = C_.alloc([128, 124, 128], BF16, "diag")
        NWC = 4
        wc, wc_o = [], []
        for i in range(NWC):
            off_i = C_.off
            wc.append(C_.alloc([128, 8, 512], BF16, "wc"))
            wc_o.append(Arena(nc, off_i, off_i + 8192, "WO%d" % i).alloc([128, 4, 1024], BF16, "wo"))
        b_wc = [Buf("wc%d" % i) for i in range(NWC)]
        xbC = [C_.alloc([128, D], F32, "xbC") for _ in range(2)]
        b_xbC = [Buf("xbC0"), Buf("xbC1")]
        xnC = C_.alloc([128, D], BF16, "xnC")
        xo_ = [C_.alloc([128, 8, 512], BF16, "xo0"),
               Arena(nc, 131584, 131584 + 8192, "XO1").alloc([128, 8, 512], BF16, "xo1")]
        xh_ = [C_.alloc([128, 8, 64], BF16, "xh0"), C_.alloc([128, 8, 64], BF16, "xh1")]
        stC = C_.alloc([128, 16], F32, "stC")
        u2 = C_.alloc([128, 4, 2, 288], BF16, "u2")
        accd_off = C_.off
        acc_d = C_.alloc([128, 4, 512], F32, "acc_d")
        mean_sb = C_.alloc([128, 512], F32, "mean")
        rstd_sb = C_.alloc([128, 512], F32, "rstd")
        ug = C_.alloc([128, 4, 512], BF16, "ug")
        merged2 = C_.alloc([128, 8, 512], F32, "merged2")
        tg = [C2_.alloc([128, 512], F32, "tg") for _ in range(2)]
        sgb = [C2_.alloc([128, 512], BF16, "sg") for _ in range(4)]
        sgc = [C2_.alloc([128, 512], BF16, "sgc") for _ in range(4)]
        mbf_off = C2_.off
        xq = C2_.alloc([128, 4, 512], BF16, "xq")
        oxg = C2_.alloc([128, 4, 512], BF16, "oxg")
        merged_bf = Arena(nc, mbf_off, mbf_off + 8192, "MBF").alloc([128, 8, 512], BF16, "merged_bf")
        PTx = [C2_.alloc([128, 512], BF16, "PTx") for _ in range(2)]
        rr = C2_.alloc([128, 512], F32, "rr")
        tmpf = [C2_.alloc([128, 512], F32, "tmpf") for _ in range(2)]
        m2_sb = tmpf[0]
        ogp = C2_.alloc([128, 4, 512], BF16, "ogp")
        outb = [C2_.alloc([128, D], F32, "outb") for _ in range(2)]
        b_outb = [Buf("outb0"), Buf("outb1")]
        xnC2 = C2_.alloc([128, D], BF16, "xnC2")
        b_wres = Buf("wres")
        b_diag = Buf("diag")
        b_kvm = Buf("kvm")
        b_xnC = Buf("xnC")
        xnL, b_xnL = [xnC, xnC2], [b_xnC, Buf("xnC2")]
        b_xn_ = [Buf("xnTC0"), b_OG[0]]
        cur = {}

        def set_cur(T):
            cur["xo"], cur["xh"], cur["b"] = xo_[T % 2], xh_[T % 2], b_xn_[T % 2]
        b_stC = [Buf("stC%d" % i) for i in range(5)]
        b_u2 = Buf("u2")
        b_tg = [Buf("tg0"), Buf("tg1")]
        b_accd = [Buf("accd%d" % i) for i in range(4)]
        b_mean, b_rstd = Buf("mean"), Buf("rstd")
        b_sg = [Buf("sg%d" % i) for i in range(4)]
        b_sgc = [Buf("sgc%d" % i) for i in range(4)]
        b_ug, b_mg2, b_mbf = Buf("ug"), [Buf("mg2_%d" % i) for i in range(8)], Buf("mbf")
        b_xq, b_PTx, b_rr = Buf("xq"), [Buf("PTx0"), Buf("PTx1")], Buf("rr")
        b_tmpf = [Buf("tmpf0"), Buf("tmpf1")]
        b_m2 = b_tmpf[0]
        b_oxg, b_ogp, b_res = Buf("oxg"), Buf("ogp"), Buf("res")
        b_fin = Buf("fin_st")

        S.dma("sp", P(nc.sync.dma_start, out=fgt[:], in_=fg_in.partition_broadcast(128)), shared=[b_wres])
        S.op("dve", P(nc.vector.memset, epsC[:], EPS), shared=[b_wres])
        b_wh = Buf("wh")
        S.op("dve", P(nc.vector.tensor_scalar, out=wh[:], in0=vecs[:, VC_CW:VC_CW + 124], scalar1=0.5, scalar2=None,
                      op0=ALU.mult), reads=[b_const], writes=[b_wh])
        b_diagm = [Buf("diag%d" % i) for i in range(4)]

        def build_diag(mc):
            for i in range(mc * 31, (mc + 1) * 31):
                if i % 2 == 0:
                    S.op("dve", P(nc.vector.tensor_scalar, out=diag[:, i, :], in0=ident[:], scalar1=wh[:, i:i + 1],
                                  scalar2=None, op0=ALU.mult), reads=[b_wh, b_const], shared=[b_diagm[mc]])
                else:
                    S.op("act", P(nc.scalar.mul, out=diag[:, i, :], in_=ident[:], mul=wh[:, i:i + 1]),
                         reads=[b_wh, b_const], shared=[b_diagm[mc]])

        ringC = PsumRing(ps, [0, 1, 2, 3, 4, 5, 6, 7])

        def wsrc(ap):
            return ap.rearrange("(kc p) n -> p kc n", p=128)

        def WC(a):
            return ("w", wc_bf[a // 512], b_wcv[("c", a // 512)])
        names = ["xq", "xg", "cg", "mg", "wxo", "g2a", "g2b", "wmo", "g1a", "g1b", "wco", "g0a", "g0b"]
        srcs = {"val": WC(0), "glu": WC(512), "xq": WC(2560), "xg": WC(3072), "cg": WC(1024), "mg": WC(4608),
                "wxo": ("o", wo_bf[0], b_wcv[("o", 0)]), "g2a": WC(3584), "g2b": WC(4096), "wmo": ("o", wo_bf[1], b_wcv[("o", 1)]),
                "g1a": WC(5120), "g1b": WC(5632), "wco": ("o", wo_bf[2], b_wcv[("o", 2)]), "g0a": WC(1536), "g0b": WC(2048),
                "woa": WC(6144), "wob": WC(6656)}
        chunk_src = [("w", wmkv_bf[0], b_wcv[("m", 0)]), ("w", wmkv_bf[1], b_wcv[("m", 1)])]
        CI = [dict() for _ in range(4)]

        def add_chunk(T, nm):
            CI[T][nm] = len(chunk_src)
            chunk_src.append(srcs[nm])
        add_chunk(0, "val")
        add_chunk(0, "glu")
        for T in range(4):
            for nm in names:
                add_chunk(T, nm)
            if T + 1 < 4:
                add_chunk(T + 1, "val")
                add_chunk(T + 1, "glu")
            add_chunk(T, "woa")
            add_chunk(T, "wob")
        issued = [0]
        consumed = set()

        def prefetch():
            low = 0
            while low in consumed:
                low += 1
            while issued[0] < min(low + NWC, len(chunk_src)):
                i = issued[0]
                kind, src, bsrc = chunk_src[i]
                dst = wc[i % NWC] if kind == "w" else wc_o[i % NWC]
                S.dma("pool", P(nc.gpsimd.dma_start, out=dst[:], in_=src), reads=[bsrc], writes=[b_wc[i % NWC]],
                      chan=b_wc[i % NWC])
                issued[0] += 1

        def chunk(i):
            assert i < issued[0], (i, issued[0])
            kind = chunk_src[i][0]
            return (wc[i % NWC] if kind == "w" else wc_o[i % NWC]), b_wc[i % NWC]

        def done(*idx):
            for i in idx:
                consumed.add(i)
            prefetch()

        prefetch()

        def rstd_chain(ss_ap, out_ap, buf, div):
            S.op("dve", P(nc.vector.tensor_scalar, out=out_ap, in0=ss_ap, scalar1=1.0 / div, scalar2=EPS, op0=ALU.mult,
                          op1=ALU.add), reads=[buf], shared=[buf])
            S.op("act", P(nc.scalar.sqrt, out=out_ap, in_=out_ap), reads=[buf], shared=[buf])
            S.op("dve", P(nc.vector.reciprocal, out=out_ap, in_=out_ap), reads=[buf], shared=[buf])

        b_dst = [None]
        nT_i = [0]

        def norm_T(x_sb, bx, np_, stb, sti, g_col, dst_cols):
            xnC, b_xnC = xnL[sti % 2], b_xnL[sti % 2]
            S.op("act", P(nc.scalar.activation, out=acc_d[:, 2:4, :].rearrange("p a n -> p (a n)")[0:np_, :],
                          in_=x_sb[0:np_, :], func=AF.Square, accum_out=stC[0:np_, sti:sti + 1]),
                 reads=[bx], writes=[b_accd[2], b_accd[3]], shared=[stb])
            rstd_chain(stC[0:np_, sti:sti + 1], stC[0:np_, 8 + sti:9 + sti], stb, float(D))
            S.op("dve", P(nc.vector.tensor_scalar, out=xnC[0:np_, :], in0=x_sb[0:np_, :],
                          scalar1=stC[0:np_, 8 + sti:9 + sti], scalar2=None, op0=ALU.mult), reads=[bx, stb], writes=[b_xnC])
            pT, bT = ringC.next()
            pTb = pT.bitcast(BF16)
            for kc in range(8):
                S.op("pe", P(nc.tensor.transpose, out=pTb[:, kc * np_:(kc + 1) * np_], in_=xnC[0:np_, kc * 128:(kc + 1) * 128],
                             identity=ident[0:np_, 0:np_]), reads=[b_xnC, b_const], shared=[bT])
            nT_i[0] += 1
            for kc in range(8):
                if nT_i[0] % 2 == 0:
                    S.op("dve", P(nc.vector.tensor_scalar, out=dst_cols(kc), in0=pTb[:, kc * np_:(kc + 1) * np_],
                                  scalar1=vecs[:, g_col + kc:g_col + kc + 1], scalar2=None, op0=ALU.mult),
                         reads=[bT, b_const], shared=[b_dst[0]])
                else:
                    S.op("act", P(nc.scalar.mul, out=dst_cols(kc), in_=pTb[:, kc * np_:(kc + 1) * np_],
                                  mul=vecs[:, g_col + kc:g_col + kc + 1]), reads=[bT, b_const], shared=[b_dst[0]])

        def proj(wt, wb, c0, m):
            p_, b_ = ringC.next()
            for kc in range(8):
                S.op("pe", P(nc.tensor.matmul, p_[0:m, :], wt[:, kc, c0:c0 + m], cur["xo"][:, kc, :],
                             start=(kc == 0), stop=(kc == 7)), reads=[wb, cur["b"]], shared=[b_])
            return p_, b_

        memT = Arena(nc, accd_off, accd_off + 4096, "MEMT").alloc([128, 8, 256], BF16, "memT")
        b_memT = Buf("memT")
        b_dst[0] = b_memT
        for mb in range(2):
            S.dma("sp", P(nc.sync.dma_start, out=xbC[mb][:], in_=mem_in[mb * 128:(mb + 1) * 128, :]), writes=[b_xbC[mb]])
            norm_T(xbC[mb], b_xbC[mb], 128, b_stC[mb], mb, VC_MG,
                   lambda kc, mb=mb: memT[:, kc, mb * 128:(mb + 1) * 128])
        wk_t, wk_b = chunk(0)
        for hx in range(4):
            p_, b_ = ringC.next()
            for kc in range(8):
                S.op("pe", P(nc.tensor.matmul, p_[:, 0:256], wk_t[:, kc, hx * 128:(hx + 1) * 128], memT[:, kc, :],
                             start=(kc == 0), stop=(kc == 7)), reads=[wk_b, b_memT, b_accd[0], b_accd[1]], shared=[b_])
            S.op("dve", P(nc.vector.tensor_copy, KM[:, hx, :], p_[:, 0:256]), reads=[b_], shared=[b_kvm])
        done(0)
        wv_t, wv_b = chunk(1)
        for mb in range(2):
            p_, b_ = ringC.next()
            for kc in range(8):
                S.op("pe", P(nc.tensor.matmul, p_, memT[:, kc, mb * 128:(mb + 1) * 128], wv_t[:, kc, :],
                             start=(kc == 0), stop=(kc == 7)), reads=[wv_b, b_memT, b_accd[0], b_accd[1]], shared=[b_])
            S.op("act", P(nc.scalar.copy, out=VM[:, mb, :], in_=p_), reads=[b_], shared=[b_kvm])
        done(1)

        def gate_merge_steps(cg, co, bias_col, rhs_t, rhs_b, mode):
            def step(fo):
                wo_t, wo_b = chunk(co)
                wt, wb = chunk(cg + fo // 4)
                py, by = ringC.next()
                for kc in range(4):
                    S.op("pe", P(nc.tensor.matmul, py, wo_t[:, kc, fo * 128:(fo + 1) * 128], rhs_t[:, kc, :],
                                 start=(kc == 0), stop=(kc == 3)), reads=[wo_b, rhs_b], shared=[by])
                pz, bz = proj(wt, wb, (fo % 4) * 128, 128)
                tt, tb = tg[fo % 2], b_tg[fo % 2]
                S.op("act", P(nc.scalar.activation, out=tt[:], in_=pz, func=AF.Tanh,
                              bias=hb[:, bias_col + fo:bias_col + fo + 1], scale=0.5), reads=[bz, b_hb], writes=[tb])
                if mode == "first":
                    S.op("dve", P(nc.vector.scalar_tensor_tensor, out=merged2[:, fo, :], in0=tt[:], scalar=1.0, in1=py,
                                  op0=ALU.add, op1=ALU.mult), reads=[tb, by], writes=[b_mg2[fo]])
                else:
                    tf, tfb = tmpf[fo % 2], b_tmpf[fo % 2]
                    S.op("dve", P(nc.vector.scalar_tensor_tensor, out=tf[:], in0=tt[:], scalar=1.0, in1=py,
                                  op0=ALU.add, op1=ALU.mult), reads=[tb, by], writes=[tfb])
                    if mode == "mid":
                        S.op("dve", P(nc.vector.tensor_tensor, out=merged2[:, fo, :], in0=merged2[:, fo, :], in1=tf[:],
                                       op=ALU.add), reads=[tfb, b_mg2[fo]], writes=[b_mg2[fo]])
                    else:
                        S.op("dve", P(nc.vector.tensor_tensor, out=merged_bf[:, fo, :], in0=merged2[:, fo, :], in1=tf[:],
                                       op=ALU.add), reads=[tfb, b_mg2[fo]], shared=[b_mbf, b_xq, b_oxg])
                if fo == 3:
                    done(cg)
                if fo == 7:
                    done(cg + 1, co)
            return [P(step, fo) for fo in range(8)]

        def run_steps(a, b=(), ratio=1):
            a, b = list(a), list(b)
            ia = ib = 0
            while ia < len(a) or ib < len(b):
                for _ in range(ratio):
                    if ia < len(a):
                        a[ia]()
                        ia += 1
                if ib < len(b):
                    b[ib]()
                    ib += 1

        def c0_blk(T, b):
            if b == 0:
                return 64, x_ext[T, 0:64, :], (lambda kc, T=T: xh_[T % 2][:, kc, :])
            return 128, x_ext[T, 64 + (b - 1) * 128:64 + b * 128, :], \
                (lambda kc, T=T, b=b: xo_[T % 2][:, kc, (b - 1) * 128:b * 128])

        junk_f = acc_d[:, 2:4, :].rearrange("p a n -> p (a n)")

        def c0_p1(T, b):
            np_, src, _ = c0_blk(T, b)
            xs, bxs, stb, sti = xbC[b % 2], b_xbC[b % 2], b_stC[b], b
            xnC, b_xnC = xnL[b % 2], b_xnL[b % 2]
            S.dma("sp", P(nc.sync.dma_start, out=xs[0:np_, :], in_=src), writes=[bxs])
            if T == 0:
                S.op("act", P(nc.scalar.activation, out=junk_f[0:np_, :], in_=xs[0:np_, :], func=AF.Square,
                              accum_out=stC[0:np_, sti:sti + 1]), reads=[bxs], writes=[b_accd[2], b_accd[3]], shared=[stb])
            else:
                S.op("act", P(nc.scalar.activation, out=xnC[0:np_, :], in_=xs[0:np_, :], func=AF.Square,
                              accum_out=stC[0:np_, sti:sti + 1]), reads=[bxs], writes=[b_xnC], shared=[stb])
            rstd_chain(stC[0:np_, sti:sti + 1], stC[0:np_, 8 + sti:9 + sti], stb, float(D))
            S.op("dve", P(nc.vector.tensor_scalar, out=xnC[0:np_, :], in0=xs[0:np_, :],
                          scalar1=stC[0:np_, 8 + sti:9 + sti], scalar2=None, op0=ALU.mult), reads=[bxs, stb], writes=[b_xnC])

        def c0_p2(T, b):
            np_, _, dst_cols = c0_blk(T, b)
            bdst = b_xn_[T % 2]
            xnC, b_xnC = xnL[b % 2], b_xnL[b % 2]
            pT, bT = ringC.next()
            pTb = pT.bitcast(BF16)
            for kc in range(8):
                S.op("pe", P(nc.tensor.transpose, out=pTb[:, kc * np_:(kc + 1) * np_], in_=xnC[0:np_, kc * 128:(kc + 1) * 128],
                             identity=ident[0:np_, 0:np_]), reads=[b_xnC, b_const], shared=[bT])
            nT_i[0] += 1
            for kc in range(8):
                if nT_i[0] % 2 == 0:
                    S.op("dve", P(nc.vector.tensor_scalar, out=dst_cols(kc), in0=pTb[:, kc * np_:(kc + 1) * np_],
                                  scalar1=vecs[:, VC_NG + kc:VC_NG + kc + 1], scalar2=None, op0=ALU.mult),
                         reads=[bT, b_const], shared=[bdst])
                else:
                    S.op("act", P(nc.scalar.mul, out=dst_cols(kc), in_=pTb[:, kc * np_:(kc + 1) * np_],
                                  mul=vecs[:, VC_NG + kc:VC_NG + kc + 1]), reads=[bT, b_const], shared=[bdst])

        def hoist(T, step):
            if T + 1 >= 4 or CSTOP < 5:
                return
            if step >= 1:
                c0_p2(T + 1, step - 1)
            if step <= 4:
                c0_p1(T + 1, step)

        def x_reload(T, blk):
            xs, bxs = xbC[blk % 2], b_xbC[blk % 2]
            S.dma("sp", P(nc.sync.dma_start, out=xs[:], in_=x_ext[T, 64 + blk * 128:64 + (blk + 1) * 128, :]),
                  writes=[bxs])

        def s2(T):
            set_cur(T)
            wval, bval = chunk(CI[T]["val"])
            wglu, bglu = chunk(CI[T]["glu"])
            ph, bh = ringC.next()
            for gi, (wt, wb) in enumerate(((wval, bval), (wglu, bglu))):
                for mc in range(4):
                    for kc in range(8):
                        S.op("pe", P(nc.tensor.matmul, ph[:, gi * 256 + mc * 64:gi * 256 + (mc + 1) * 64],
                                     wt[:, kc, mc * 128:(mc + 1) * 128], cur["xh"][:, kc, :], start=(kc == 0), stop=(kc == 7)),
                             reads=[wb, cur["b"]], shared=[bh])
            S.op("act", P(nc.scalar.activation, out=tg[0][:, 0:256], in_=ph[:, 256:512], func=AF.Tanh, scale=0.5),
                 reads=[bh], writes=[b_tg[0]])
            for mc in range(4):
                S.op("dve", P(nc.vector.scalar_tensor_tensor, out=u2[:, mc, :, 0:32],
                              in0=tg[0][:, mc * 64:(mc + 1) * 64].rearrange("p (c i) -> p c i", c=2), scalar=1.0,
                              in1=ph[:, mc * 64:(mc + 1) * 64].rearrange("p (c i) -> p c i", c=2),
                              op0=ALU.add, op1=ALU.mult), reads=[b_tg[0], bh], shared=[b_u2])
            for mc in range(4):
                pv, bv = proj(wval, bval, mc * 128, 128)
                pg, bg = proj(wglu, bglu, mc * 128, 128)
                tt, tb = tg[(mc + 1) % 2], b_tg[(mc + 1) % 2]
                S.op("act", P(nc.scalar.activation, out=tt[:], in_=pg, func=AF.Tanh, scale=0.5), reads=[bg], writes=[tb])
                S.op("dve", P(nc.vector.scalar_tensor_tensor, out=u2[:, mc, :, 32:288],
                              in0=tt[:].rearrange("p (c i) -> p c i", c=2), scalar=1.0,
                              in1=pv.rearrange("p (c i) -> p c i", c=2), op0=ALU.add, op1=ALU.mult),
                     reads=[tb, bv], shared=[b_u2])
            done(CI[T]["val"], CI[T]["glu"])


        for T in range(4 if CSTOP >= 5 else (1 if CSTOP >= 1 else 0)):
            set_cur(T)
            if T == 0:
                c0_p1(0, 0)
                for b in range(5):
                    if b + 1 < 5:
                        c0_p1(0, b + 1)
                    c0_p2(0, b)
            S.dma("sp", P(nc.sync.dma_start, out=ogp[0:64, :, :], in_=OG[0:64, T, 0:8:2, :]),
                  reads=[b_OG[T]], shared=[b_ogp], chan=b_ogp)
            S.dma("sp", P(nc.sync.dma_start, out=ogp[64:128, :, :], in_=OG[0:64, T, 1:8:2, :]),
                  reads=[b_OG[T]], shared=[b_ogp], chan=b_ogp)
            if CSTOP < 2:
                continue
            if T == 0:
                s2(0)
            set_cur(T)
            wxq, bxq = chunk(CI[T]["xq"])
            for hx in range(4):
                p_, b_ = proj(wxq, bxq, hx * 128, 128)
                S.op("act", P(nc.scalar.mul, out=xq[:, hx, :], in_=p_, mul=SCALE_X), reads=[b_], shared=[b_xq])
            done(CI[T]["xq"])
            wxg, bxg = chunk(CI[T]["xg"])
            for hx in range(4):
                p_, b_ = proj(wxg, bxg, hx * 128, 128)
                S.op("act", P(nc.scalar.activation, out=sgb[hx][:], in_=p_, func=AF.Silu), reads=[b_], writes=[b_sg[hx]])
            done(CI[T]["xg"])
            hoist(T, 0)
            wcg, bcg = chunk(CI[T]["cg"])
            for mc in range(4):
                pc, bc_ = proj(wcg, bcg, mc * 128, 128)
                S.op("act", P(nc.scalar.activation, out=sgc[mc][:], in_=pc, func=AF.Silu), reads=[bc_], writes=[b_sgc[mc]])
            done(CI[T]["cg"])
            wmg, bmg = chunk(CI[T]["mg"])
            for mc in range(4):
                p_, b_ = proj(wmg, bmg, mc * 128, 128)
                tf, tfb = tmpf[mc % 2], b_tmpf[mc % 2]
                S.op("act", P(nc.scalar.activation, out=tf[:], in_=p_, func=AF.Silu), reads=[b_], writes=[tfb])
                S.op("dve", P(nc.vector.tensor_tensor, out=ogp[:, mc, :], in0=ogp[:, mc, :], in1=tf[:], op=ALU.mult),
                     reads=[b_ogp, tfb], writes=[b_ogp])
            done(CI[T]["mg"])
            hoist(T, 1)

            def conv_step(mc, T=T):
                if T == 0 and mc == 0:
                    build_diag(0)
                if T == 0 and mc + 1 < 4:
                    build_diag(mc + 1)
                pcv, bcv = ringC.next()
                for k in range(31):
                    S.op("pe", P(nc.tensor.matmul, pcv.rearrange("p (c i) -> p c i", c=2), diag[:, mc * 31 + k, :],
                                 u2[:, mc, :, 2 + k:2 + k + 256], start=(k == 0), stop=(k == 30)),
                         reads=[b_diagm[mc], b_u2], shared=[bcv])
                S.op("act", P(nc.scalar.activation, out=acc_d[:, mc, :], in_=pcv, func=AF.Identity,
                              bias=vecs[:, VC_CB + mc:VC_CB + mc + 1], scale=1.0), reads=[bcv, b_const],
                     writes=[b_accd[mc]])

            def cross_step(hx):
                po, bo = ringC.next()
                pd, bd = ringC.next()
                for mb in range(2):
                    psx, bsx = ringC.next()
                    S.op("pe", P(nc.tensor.matmul, psx, KM[:, hx, mb * 128:(mb + 1) * 128], xq[:, hx, :], start=True,
                                 stop=True), reads=[b_kvm, b_xq], shared=[bsx])
                    S.op("act", P(nc.scalar.activation, out=PTx[mb][:], in_=psx, func=AF.Exp), reads=[bsx],
                         writes=[b_PTx[mb]])
                    S.op("pe", P(nc.tensor.matmul, po, VM[:, mb, hx * 128:(hx + 1) * 128], PTx[mb][:], start=(mb == 0),
                                 stop=(mb == 1)), reads=[b_kvm, b_PTx[mb]], shared=[bo])
                    S.op("pe", P(nc.tensor.matmul, pd, ones_b[:], PTx[mb][:], start=(mb == 0), stop=(mb == 1)),
                         reads=[b_const, b_PTx[mb]], shared=[bd])
                S.op("dve", P(nc.vector.reciprocal, out=rr[:], in_=pd), reads=[bd], writes=[b_rr])
                tf, tfb = tmpf[hx % 2], b_tmpf[hx % 2]
                S.op("dve", P(nc.vector.tensor_tensor, out=tf[:], in0=po, in1=rr[:], op=ALU.mult), reads=[bo, b_rr],
                     writes=[tfb])
                S.op("dve", P(nc.vector.tensor_tensor, out=oxg[:, hx, :], in0=tf[:], in1=sgb[hx][:], op=ALU.mult),
                     reads=[tfb, b_sg[hx]], shared=[b_oxg])

            if T == 0:
                run_steps([P(cross_step, hx) for hx in range(4)], [P(conv_step, mc) for mc in range(4)])
            else:
                run_steps([P(conv_step, mc) for mc in range(4)], [P(cross_step, hx) for hx in range(4)])
            if T == 0:
                dump("cconv", acc_d[:].rearrange("p a n -> p (a n)"), b_accd)
            hoist(T, 2)
            p1, b1 = ringC.next()
            for mc in range(4):
                S.op("pe", P(nc.tensor.matmul, p1, ones_f[:], acc_d[:, mc, :], start=(mc == 0), stop=(mc == 3)),
                     reads=[b_accd[mc], b_const], shared=[b1])
            p2, b2 = ringC.next()
            for mc in range(4):
                tf, tfb = tmpf[mc % 2], b_tmpf[mc % 2]
                S.op("act", P(nc.scalar.activation, out=tf[:], in_=acc_d[:, mc, :], func=AF.Square),
                     reads=[b_accd[mc]], writes=[tfb])
                S.op("pe", P(nc.tensor.matmul, p2, ones_f[:], tf[:], start=(mc == 0), stop=(mc == 3)),
                     reads=[tfb, b_const], shared=[b2])
            S.op("dve", P(nc.vector.tensor_scalar, out=mean_sb[:], in0=p1, scalar1=1.0 / 512, scalar2=None, op0=ALU.mult),
                 reads=[b1], writes=[b_mean])
            S.op("dve", P(nc.vector.tensor_tensor, out=rr[:], in0=mean_sb[:], in1=mean_sb[:], op=ALU.mult),
                 reads=[b_mean], writes=[b_rr])
            S.op("dve", P(nc.vector.scalar_tensor_tensor, out=rstd_sb[:], in0=p2, scalar=1.0 / 512, in1=rr[:],
                          op0=ALU.mult, op1=ALU.subtract), reads=[b2, b_rr], writes=[b_rstd])
            S.op("act", P(nc.scalar.activation, out=rstd_sb[:], in_=rstd_sb[:], func=AF.Sqrt, bias=epsC[:, 0:1], scale=1.0),
                 reads=[b_rstd, b_wres], writes=[b_rstd])
            S.op("dve", P(nc.vector.reciprocal, out=rstd_sb[:], in_=rstd_sb[:]), reads=[b_rstd], writes=[b_rstd])

            def ln_step(mc):
                S.op("dve", P(nc.vector.tensor_tensor, out=acc_d[:, mc, :], in0=acc_d[:, mc, :], in1=mean_sb[:],
                              op=ALU.subtract), reads=[b_accd[mc], b_mean], writes=[b_accd[mc]])
                S.op("dve", P(nc.vector.tensor_tensor, out=acc_d[:, mc, :], in0=acc_d[:, mc, :], in1=rstd_sb[:],
                               op=ALU.mult), reads=[b_accd[mc], b_rstd], writes=[b_accd[mc]])
                S.op("act", P(nc.scalar.activation, out=acc_d[:, mc, :], in_=acc_d[:, mc, :], func=AF.Silu,
                              bias=vecs[:, VC_LB + mc:VC_LB + mc + 1], scale=vecs[:, VC_LG + mc:VC_LG + mc + 1]),
                     reads=[b_accd[mc], b_const], writes=[b_accd[mc]])
                S.op("dve", P(nc.vector.tensor_tensor, out=ug[:, mc, :], in0=acc_d[:, mc, :], in1=sgc[mc][:], op=ALU.mult),
                     reads=[b_accd[mc], b_sgc[mc]], shared=[b_ug])

            run_steps(gate_merge_steps(CI[T]["g2a"], CI[T]["wxo"], 16, oxg, b_oxg, "first"),
                      [P(ln_step, mc) for mc in range(4)], ratio=2)
            if T == 0:
                dump("m1", merged2[:].rearrange("p a n -> p (a n)"), b_mg2)
            hoist(T, 3)
            st10 = gate_merge_steps(CI[T]["g1a"], CI[T]["wmo"], 8, ogp, b_ogp, "mid")
            run_steps(st10[0:4])
            hoist(T, 4)
            run_steps(st10[4:8])
            hoist(T, 5)
            if CSTOP >= 5:
                x_reload(T, 0)
                x_reload(T, 1)
            run_steps(gate_merge_steps(CI[T]["g0a"], CI[T]["wco"], 0, ug, b_ug, "last"))
            if CSTOP < 5:
                continue
            if T + 1 < 4:
                s2(T + 1)
            wo = [chunk(CI[T]["woa"]), chunk(CI[T]["wob"])]
            for blk in range(4):
                xs, bxs = xbC[blk % 2], b_xbC[blk % 2]
                ob, bob = outb[blk % 2], b_outb[blk % 2]
                for half in range(2):
                    pf, bf = ringC.next()
                    wt, wb = wo[half]
                    for kc in range(8):
                        S.op("pe", P(nc.tensor.matmul, pf, merged_bf[:, kc, blk * 128:(blk + 1) * 128], wt[:, kc, :],
                                     start=(kc == 0), stop=(kc == 7)), reads=[b_mbf, b_xq, b_oxg, wb], shared=[bf])
                    S.op("dve", P(nc.vector.scalar_tensor_tensor, out=ob[:, half * 512:(half + 1) * 512], in0=pf, scalar=0.5,
                                  in1=xs[:, half * 512:(half + 1) * 512], op0=ALU.mult, op1=ALU.add),
                         reads=[bf, bxs], shared=[bob])
                if blk + 2 < 4:
                    x_reload(T, blk + 2)
                sti = 5 + blk % 2
                S.op("act", P(nc.scalar.activation, out=acc_d[:, 0:2, :].rearrange("p a n -> p (a n)"), in_=ob[:],
                              func=AF.Square, accum_out=stC[:, sti:sti + 1]), reads=[bob],
                     writes=[b_accd[0], b_accd[1]], shared=[b_fin])
                rstd_chain(stC[:, sti:sti + 1], stC[:, 8 + sti:9 + sti], b_fin, float(D))
                S.op("dve", P(nc.vector.scalar_tensor_tensor, out=ob[:], in0=ob[:], scalar=stC[:, 8 + sti:9 + sti],
                              in1=fgt[:], op0=ALU.mult, op1=ALU.mult), reads=[b_fin, b_wres, bob], writes=[bob])
                r0 = T * 512 + blk * 128
                S.dma("sp", P(nc.sync.dma_start, out=out[r0:r0 + 128, :], in_=ob[:]), reads=[bob], chan=bob, is_out=True)
            done(CI[T]["woa"], CI[T]["wob"])

    nwait = S.emit(nc, ES)
    return nc, nwait


ES = None


def make_inputs(c, x, mem, positions, norm_g, w_in, b_gate, conv_w, conv_b, conv_ln_g, conv_ln_b, w_conv_o,
                q_norm_g, w_uq, kv_norm_g, w_ukv, w_mla_o, mem_norm_g, w_mem_kv, w_x_o, w_out, final_norm_g,
                shared):
    b, p = c // 2, c % 2
    own, oth = OWN[p], OTH[p]
    order = own + oth
    xb = x[b]
    x_all = np.concatenate([xb[g * CH:(g + 1) * CH] for g in order], axis=0)
    pos_all = np.concatenate([positions[b, g * CH:(g + 1) * CH] for g in order], axis=0).astype(np.int32)
    x_ext = np.zeros((4, 576, D), np.float32)
    for t in range(4):
        for jj in range(2):
            g = own[2 * t + jj]
            if g > 0:
                x_ext[t, jj * 32:(jj + 1) * 32] = xb[g * CH - 32:g * CH]
            x_ext[t, 64 + jj * 256:64 + (jj + 1) * 256] = xb[g * CH:(g + 1) * CH]
    maskf = np.zeros((128, 8, 256), np.float32)
    for j in range(8):
        if not (oth[j] < own[j]):
            maskf[:, j, :] = NEG
    d = dict(shared)
    d.update(x_all=np.ascontiguousarray(x_all), x_ext=x_ext, pos_all=pos_all,
             mem=np.ascontiguousarray(mem[b]), maskf=maskf)
    return d


def make_shared(norm_g, w_in, b_gate, conv_w, conv_b, conv_ln_g, conv_ln_b, w_conv_o, q_norm_g, w_uq, kv_norm_g,
                w_ukv, w_mla_o, mem_norm_g, w_mem_kv, w_x_o, w_out, final_norm_g):
    f = np.float32
    w_in0 = w_in[0]
    vecs = np.zeros((128, NV), f)
    vecs[:, VC_BG:VC_BG + 24] = b_gate[0].reshape(24, 128).T
    vecs[:, VC_CB:VC_CB + 4] = conv_b[0].reshape(4, 128).T
    vecs[:, VC_LG:VC_LG + 4] = conv_ln_g[0].reshape(4, 128).T
    vecs[:, VC_LB:VC_LB + 4] = conv_ln_b[0].reshape(4, 128).T
    vecs[:, VC_QG:VC_QG + 3] = q_norm_g[0].reshape(3, 128).T
    vecs[:, VC_KG:VC_KG + 2] = kv_norm_g[0].reshape(2, 128).T
    inv_freq = (10000.0 ** (-np.arange(0, 32, 2, dtype=np.float32) / 32)).astype(f)
    vecs[:, VC_IF] = np.tile(inv_freq, 8)
    vecs[:, VC_PC] = np.pi / 2
    vecs[:, VC_PS] = np.tile(np.concatenate([np.full(16, np.pi), np.zeros(16)]), 4)
    vecs[:, VC_NG:VC_NG + 8] = norm_g[0].reshape(8, 128).T
    vecs[:, VC_MG:VC_MG + 8] = mem_norm_g[0].reshape(8, 128).T
    vecs[:, VC_CW:VC_CW + 124] = conv_w[0].T.reshape(4, 128, 31).transpose(1, 0, 2).reshape(128, 124)
    rope = np.arange(2176, 2208)
    rope_sw = np.concatenate([rope[16:], rope[:16]])
    junk = np.arange(1920, 1984)
    cols_a = np.concatenate([np.arange(1536, 1920), np.arange(1920, 2176), junk, rope, junk, rope_sw])
    w_a = np.ascontiguousarray(w_in0[:, cols_a])
    uq = w_uq[0]
    cols_b = []
    for h in range(8):
        base = h * 96
        cols_b += list(range(base, base + 64)) + list(range(base + 80, base + 96)) + list(range(base + 64, base + 80))
    w_uqb = np.ascontiguousarray(uq[:, cols_b])
    ukv = w_ukv[0].reshape(256, 8, 128)
    w_uk = np.ascontiguousarray(ukv[:, :, :64].reshape(256, 512))
    w_uv = np.ascontiguousarray(ukv[:, :, 64:].reshape(256, 512))
    seg = lambda a, n: np.arange(a, a + n)
    cols_c = np.concatenate([seg(0, 512), seg(512, 512), seg(1024, 512), seg(3744, 1024), seg(2720, 512),
                             seg(3232, 512), seg(5792, 1024), seg(2208, 512), seg(4768, 1024)])
    w_c = np.ascontiguousarray(np.concatenate([w_in0[:, cols_c], w_out[0]], axis=1))
    maskd = np.zeros((128, 2, 256), f)
    pp = np.arange(128)[:, None]
    qq = np.arange(256)[None, :]
    for kb in range(2):
        maskd[:, kb, :] = np.where(qq >= kb * 128 + pp, 0.0, NEG)
    return dict(vecs=vecs, norm_g=np.ascontiguousarray(norm_g[0]), final_g=np.ascontiguousarray(final_norm_g),
                mem_g=np.ascontiguousarray(mem_norm_g[0]), ident=np.eye(128, dtype=f), maskd=maskd,
                w_a=w_a, w_uqa=np.ascontiguousarray(uq), w_uqb=w_uqb, w_uk=w_uk, w_uv=w_uv, w_c=w_c,
                w_conv_o=np.ascontiguousarray(w_conv_o[0]), w_mla_o=np.ascontiguousarray(w_mla_o[0]),
                w_x_o=np.ascontiguousarray(w_x_o[0]), w_out=np.ascontiguousarray(w_out[0]),
                w_mkv=np.ascontiguousarray(w_mem_kv[0]))


def run(inputs, debug=None, phases="ABC", cores=None, trace=False):
    global ES
    inputs = {k: np.asarray(v) for k, v in inputs.items()}
    with contextlib.ExitStack() as es:
        ES = es
        nc, nwait = build_program(debug=debug, phases=phases)
        wkeys = ["norm_g", "w_in", "b_gate", "conv_w", "conv_b", "conv_ln_g", "conv_ln_b", "w_conv_o", "q_norm_g",
                 "w_uq", "kv_norm_g", "w_ukv", "w_mla_o", "mem_norm_g", "w_mem_kv", "w_x_o", "w_out", "final_norm_g"]
        shared = make_shared(**{k: inputs[k] for k in wkeys})
        cores = list(range(NCORES)) if cores is None else cores
        in_maps = [make_inputs(c, shared=shared, **inputs) for c in cores]
        res = run_bass_kernel_spmd(nc, in_maps, core_ids=list(range(len(cores))), **({"trace": True} if trace else {}))
    return res


def kernel(**inputs):
    res = run(inputs)
    x = np.asarray(inputs["x"])
    outp = np.zeros(x.shape, np.float32)
    for c in range(NCORES):
        b, p = c // 2, c % 2
        o = res.results[c]["out"]
        for j, g in enumerate(OWN[p]):
            outp[b, g * CH:(g + 1) * CH] = o[j * CH:(j + 1) * CH]
    return outp
```

```python
import contextlib
from functools import partial as P
import numpy as np
import concourse.bass as bass
import concourse.mybir as mybir
from concourse.bass_utils import run_bass_kernel_spmd

F32 = mybir.dt.float32
BF16 = mybir.dt.bfloat16
I32 = mybir.dt.int32
AF = mybir.ActivationFunctionType
ALU = mybir.AluOpType

NCORES = 8
D = 1024
SEQ = 4096
CH = 256
OWN = {0: [0, 3, 4, 7, 8, 11, 12, 15], 1: [1, 2, 5, 6, 9, 10, 13, 14]}
OTH = {0: OWN[1], 1: OWN[0]}
NEG = -30000.0
EPS = 1e-6
SCALE_MLA = 96.0 ** -0.5
SCALE_X = 128.0 ** -0.5
SB_BASE = 16512
TWO_PI = float(2 * np.pi)
CW1 = 6.28125
CW2 = float(2 * np.pi - 6.28125)
PI_LO = 3.1415925

VC_BG = 0
VC_CB = 24
VC_LG = 28
VC_LB = 32
VC_QG = 36
VC_KG = 39
VC_IF = 41
VC_PC = 42
VC_PS = 43
VC_CW = 44
VC_NG = 168
VC_MG = 176
VC_WH = 184
NV = 184


STRICT_SAME_ENGINE = True


class Buf:
    __slots__ = ("name", "writers", "readers", "sem", "dcount")

    def __init__(self, name):
        self.name = name
        self.writers = []
        self.readers = []
        self.sem = None
        self.dcount = 0


class Op:
    __slots__ = ("eng", "fn", "deps", "is_dma", "chan", "chan_count", "signal", "count")

    def __init__(self, eng, fn, is_dma=False):
        self.eng = eng
        self.fn = fn
        self.deps = []
        self.is_dma = is_dma
        self.chan = None
        self.chan_count = 0
        self.signal = False
        self.count = 0


class Sched:
    def __init__(self):
        self.ops = []
        self.last = {}
        self.barrier_deps = []
        self.chans = []
        self.chanmap = {}
        self.out_chans = []

    def _dep(self, x, y, kind):
        if y is x:
            return
        if (not y.is_dma) and (not x.is_dma) and y.eng == x.eng:
            if x.eng == "pe":
                return
            if kind != "RAW" and not STRICT_SAME_ENGINE:
                return
        x.deps.append(y)
        if not y.is_dma:
            y.signal = True

    def _add(self, x, reads, writes, shared):
        for y in self.barrier_deps:
            self._dep(x, y, "RAW")
        for b in reads:
            for w in b.writers:
                self._dep(x, w, "RAW")
        for b in writes:
            for w in b.writers:
                self._dep(x, w, "WAW")
            for r in b.readers:
                self._dep(x, r, "WAR")
        for b in shared:
            if b.readers:
                for w in b.writers:
                    self._dep(x, w, "WAW")
                for r in b.readers:
                    self._dep(x, r, "WAR")
        for b in reads:
            b.readers.append(x)
        for b in writes:
            b.writers = [x]
            b.readers = []
        for b in shared:
            if b.readers:
                b.writers = [x]
                b.readers = []
            else:
                b.writers.append(x)
        self.ops.append(x)
        self.last[x.eng if not x.is_dma else ("dma", id(x.chan))] = x

    def op(self, eng, fn, reads=(), writes=(), shared=()):
        x = Op(eng, fn)
        self._add(x, reads, writes, shared)
        return x

    def dma(self, eng, fn, reads=(), writes=(), shared=(), chan=None, is_out=False):
        x = Op(eng, fn, is_dma=True)
        if chan is None:
            chan = (list(writes) + list(shared) + list(reads))[0]
        key = (id(chan), eng)
        if key not in self.chanmap:
            self.chanmap[key] = Buf("chan_%s_%s" % (chan.name, eng))
            self.chans.append(self.chanmap[key])
        chan = self.chanmap[key]
        x.chan = chan
        chan.dcount += 1
        x.chan_count = chan.dcount
        if is_out and chan not in self.out_chans:
            self.out_chans.append(chan)
        self._add(x, reads, writes, shared)
        return x

    def barrier(self):
        self.barrier_deps = list(self.last.values())
        for y in self.barrier_deps:
            if not y.is_dma:
                y.signal = True

    def emit(self, nc, es):
        engs = {"pe": nc.tensor, "act": nc.scalar, "dve": nc.vector, "pool": nc.gpsimd, "sp": nc.sync}
        sems = {}
        for e in ("pe", "act", "dve", "pool"):
            sems[e] = es.enter_context(nc.semaphore("s_" + e))
        for i, c in enumerate(self.chans):
            c.sem = es.enter_context(nc.semaphore("d%d" % i))
        cnt = {e: 0 for e in sems}
        for x in self.ops:
            if not x.is_dma and x.signal:
                cnt[x.eng] += 1
                x.count = cnt[x.eng]
        known = {e: {} for e in engs}
        nwait = 0
        for x in self.ops:
            e = engs[x.eng]
            need = {}
            for y in x.deps:
                if y.is_dma:
                    key, val = y.chan.sem, 16 * y.chan_count
                else:
                    key, val = sems[y.eng], y.count
                k = id(key)
                if k not in need or need[k][1] < val:
                    need[k] = (key, val)
            kn = known[x.eng]
            for k, (key, val) in need.items():
                if kn.get(k, 0) < val:
                    e.wait_ge(key, val)
                    kn[k] = val
                    nwait += 1
            ins = x.fn()
            if x.is_dma:
                ins.then_inc(x.chan.sem, 16)
            elif x.signal:
                ins.then_inc(sems[x.eng], 1)
        for c in self.out_chans:
            nc.sync.wait_ge(c.sem, 16 * c.dcount)
        return nwait


class Arena:
    def __init__(self, nc, lo, hi, tag):
        self.nc, self.lo, self.hi, self.tag = nc, lo, hi, tag
        self.off = lo
        self.n = 0

    def alloc(self, shape, dtype, name="t"):
        nbytes = int(np.prod(shape[1:])) * (2 if dtype == BF16 else 4)
        nbytes = (nbytes + 63) // 64 * 64
        assert self.off + nbytes <= self.hi, (self.tag, name, self.off, nbytes, self.hi)
        t = self.nc.alloc_sbuf_tensor_at("%s_%s%d" % (self.tag, name, self.n), list(shape), dtype,
                                         offset=SB_BASE + self.off)
        self.off += nbytes
        self.n += 1
        return t


class PsumRing:
    def __init__(self, ps, banks):
        self.ps = ps
        self.banks = list(banks)
        self.bufs = {b: Buf("ps%d" % b) for b in self.banks}
        self.i = 0

    def next(self):
        b = self.banks[self.i % len(self.banks)]
        self.i += 1
        return self.ps[:, b, :], self.bufs[b]


CSTOP = 9


def build_program(debug=None, phases="ABC"):
    nc = bass.Bass("TRN2", target_bir_lowering=False)
    S = Sched()

    def din(name, shape, dt=F32):
        return nc.dram_tensor(name, list(shape), dt, kind="ExternalInput").ap()

    x_all = din("x_all", [4096, D])
    x_ext = din("x_ext", [4, 576, D])
    pos_all = din("pos_all", [4096], I32)
    mem_in = din("mem", [256, D])
    vecs_in = din("vecs", [128, NV])
    gb_in = din("norm_g", [D])
    fg_in = din("final_g", [D])
    mg_in = din("mem_g", [D])
    ident_in = din("ident", [128, 128])
    maskd_in = din("maskd", [128, 2, 256])
    maskf_in = din("maskf", [128, 8, 256])
    w_a_in = din("w_a", [D, 832])
    w_uqa_in = din("w_uqa", [384, 768])
    w_uqb_in = din("w_uqb", [384, 768])
    w_uk_in = din("w_uk", [256, 512])
    w_uv_in = din("w_uv", [256, 512])
    w_c_in = din("w_c", [D, 7168])
    w_conv_o_in = din("w_conv_o", [512, D])
    w_mla_o_in = din("w_mla_o", [512, D])
    w_x_o_in = din("w_x_o", [512, D])
    w_out_in = din("w_out", [D, D])
    w_mkv_in = din("w_mkv", [D, D])
    out = nc.dram_tensor("out", [2048, D], F32, kind="ExternalOutput").ap()
    dbg = {}
    if debug:
        for nm, shp in debug.items():
            dbg[nm] = nc.dram_tensor("dbg_" + nm, list(shp), F32, kind="ExternalOutput").ap()

    ps = nc.alloc_psum_tensor("ps", [128, 8, 512], F32)

    P_ = Arena(nc, 0, 164352, "P")
    KT = P_.alloc([128, 8, 4096], BF16, "KT")
    VA = P_.alloc([128, 32, 8, 65], BF16, "VA")
    QT = P_.alloc([128, 8, 2048], BF16, "QT")
    OG = P_.alloc([128, 4, 8, 512], BF16, "OG")
    CST = Arena(nc, 208768, 212864, "C")
    ident = CST.alloc([128, 128], BF16, "ident")
    ones_f = CST.alloc([128, 128], F32, "ones_f")
    ones_b = CST.alloc([128, 128], BF16, "ones_b")
    vecs = CST.alloc([128, NV], F32, "vecs")
    hb = CST.alloc([128, 24], F32, "hb")
    zero_c = CST.alloc([128, 1], F32, "zero")
    eps_c = CST.alloc([128, 1], F32, "eps")
    b_const = Buf("consts")
    b_KT = [Buf("KT%d" % t) for t in range(8)]
    b_VA = [Buf("VA%d" % t) for t in range(8)]
    b_QT = [Buf("QT%d" % t) for t in range(4)]
    b_OG = [Buf("OG%d" % t) for t in range(4)]

    def dump(name, ap_sb, bufs, dst=None, cast=False):
        if name in dbg and cast:
            S.dma("pool", P(nc.gpsimd.dma_start, out=(dst if dst is not None else dbg[name]), in_=ap_sb),
                  reads=bufs, chan=Buf("dbg_" + name), is_out=True)
        elif name in dbg:
            S.dma("sp", P(nc.sync.dma_start, out=(dst if dst is not None else dbg[name]), in_=ap_sb),
                  reads=bufs, chan=Buf("dbg_" + name), is_out=True)

    S.dma("pool", P(nc.gpsimd.dma_start, out=ident[:], in_=ident_in[:, :]), shared=[b_const])
    S.dma("sp", P(nc.sync.dma_start, out=vecs[:], in_=vecs_in[:, :]), shared=[b_const])
    S.op("dve", P(nc.vector.memset, ones_f[:], 1.0), shared=[b_const])
    S.op("dve", P(nc.vector.memset, ones_b[:], 1.0), shared=[b_const])
    S.op("dve", P(nc.vector.memset, zero_c[:], 0.0), shared=[b_const])
    S.op("dve", P(nc.vector.memset, eps_c[:], EPS), shared=[b_const])
    b_hb = Buf("hb")
    S.op("dve", P(nc.vector.tensor_scalar, out=hb[:], in0=vecs[:, VC_BG:VC_BG + 24], scalar1=0.5,
                                                scalar2=None, op0=ALU.mult), reads=[b_const], writes=[b_hb])

    wc_bf = nc.dram_tensor("wc_bf", [14, 128, 8, 512], BF16, kind="Internal").ap()
    wo_bf = nc.dram_tensor("wo_bf", [3, 128, 4, D], BF16, kind="Internal").ap()
    wmkv_bf = nc.dram_tensor("wmkv_bf", [2, 128, 8, 512], BF16, kind="Internal").ap()
    b_wcv = {("c", ck): Buf("wcv_c%d" % ck) for ck in range(14)}
    b_wcv.update({("o", i): Buf("wcv_o%d" % i) for i in range(3)})
    b_wcv.update({("m", ck): Buf("wcv_m%d" % ck) for ck in range(2)})

    def convert_weights():
        def cv_c(ck):
            S.dma("pool", P(nc.gpsimd.dma_start, out=wc_bf[ck],
                            in_=w_c_in[:, ck * 512:(ck + 1) * 512].rearrange("(kc p) n -> p kc n", p=128)),
                  writes=[b_wcv[("c", ck)]])

        def cv_o(i):
            win_ = (w_x_o_in, w_mla_o_in, w_conv_o_in)[i]
            S.dma("pool", P(nc.gpsimd.dma_start, out=wo_bf[i], in_=win_.rearrange("(kc p) n -> p kc n", p=128)),
                  writes=[b_wcv[("o", i)]])
        for ck in range(2):
            S.dma("pool", P(nc.gpsimd.dma_start, out=wmkv_bf[ck],
                            in_=w_mkv_in[:, ck * 512:(ck + 1) * 512].rearrange("(kc p) n -> p kc n", p=128)),
                  writes=[b_wcv[("m", ck)]])
        for ck in (0, 1, 5, 6, 2, 9):
            cv_c(ck)
        cv_o(0)
        cv_c(7)
        cv_c(8)
        cv_o(1)
        cv_c(10)
        cv_c(11)
        cv_o(2)
        for ck in (3, 4, 12, 13):
            cv_c(ck)

    if "A" in phases:
        A_ = Arena(nc, 131584, 208768, "A")
        ckv = A_.alloc([128, 2, 4096], BF16, "ckv")
        cq = A_.alloc([128, 3, 2048], BF16, "cq")
        cs = A_.alloc([128, 4096], BF16, "cs")
        sn = A_.alloc([128, 4096], BF16, "sn")
        w_uqa = A_.alloc([128, 3, 768], BF16, "w_uqa")
        w_uqb = A_.alloc([128, 3, 768], BF16, "w_uqb")
        w_uk = A_.alloc([128, 2, 512], BF16, "w_uk")
        w_uv = A_.alloc([128, 2, 512], BF16, "w_uv")
        t1q = [A_.alloc([128, 512], F32, "t1q") for _ in range(2)]
        t2q = [A_.alloc([128, 512], F32, "t2q") for _ in range(2)]
        A1_ = Arena(nc, 65536, 131584, "A1")
        w_a = A1_.alloc([128, 8, 832], BF16, "w_a")
        xb = [A1_.alloc([128, D], F32, "xb") for _ in range(4)]
        xn = [A1_.alloc([128, D], BF16, "xn") for _ in range(2)]
        xnT = [A1_.alloc([128, 8, 512], BF16, "xnT") for _ in range(2)]
        a1_mark = A1_.off
        sqb = [A1_.alloc([128, 3, 512], BF16, "sqb") for _ in range(2)]
        rsb = [A1_.alloc([128, 512], F32, "rsb") for _ in range(2)]
        t1 = A1_.alloc([128, 512], F32, "t1")
        t2 = A1_.alloc([128, 512], F32, "t2")
        kr = [A_.alloc([128, 512], BF16, "kr") for _ in range(2)]
        st = A_.alloc([128, 16], F32, "st")
        A2_ = Arena(nc, a1_mark, 131584, "A2")
        rp_i = A2_.alloc([128, 1024], I32, "rp_i")
        rp_a = A2_.alloc([128, 1024], F32, "rp_a")
        rp_n = A2_.alloc([128, 1024], F32, "rp_n")
        rp_o = A2_.alloc([128, 1024], BF16, "rp_o")
        b_w, b_w0, b_w2 = Buf("wA"), Buf("w_a_raw"), Buf("wA2")
        b_cs = Buf("cs")
        b_xb = [Buf("xb%d" % i) for i in range(4)]
        b_xn = [Buf("xn0"), Buf("xn1")]
        b_xnT = [Buf("xnT0"), Buf("xnT1")]
        b_sq, b_rs = [Buf("sq0"), Buf("sq1")], [Buf("rs0"), Buf("rs1")]
        b_t1, b_t2, b_kr = Buf("t1"), Buf("t2"), [Buf("kr0"), Buf("kr1")]
        b_stl = [Buf("st%d" % i) for i in range(8)]
        b_rpi, b_rpa, b_rpn, b_rpo = Buf("rpi"), Buf("rpa"), Buf("rpn"), Buf("rpo")
        b_ckv = [Buf("ckv%d" % t) for t in range(8)]
        b_cq = [Buf("cq%d" % t) for t in range(4)]
        b_t1q, b_t2q = [Buf("t1q0"), Buf("t1q1")], [Buf("t2q0"), Buf("t2q1")]

        S.dma("pool", P(nc.gpsimd.dma_start, out=w_a[:], in_=w_a_in.rearrange("(kc p) n -> p kc n", p=128)),
              writes=[b_w0])
        for wt_, win_ in ((w_uk, w_uk_in), (w_uv, w_uv_in), (w_uqa, w_uqa_in), (w_uqb, w_uqb_in)):
            S.dma("pool", P(nc.gpsimd.dma_start, out=wt_[:], in_=win_.rearrange("(kc p) n -> p kc n", p=128)),
                  shared=[b_w2])
        for g in range(4):
            S.dma("sp", P(nc.sync.dma_start, out=rp_i[32 * g:32 * g + 32, :],
                          in_=pos_all[g * 1024:(g + 1) * 1024].partition_broadcast(32)), shared=[b_rpi])
        for tbl, pcol in ((cs, VC_PC), (sn, VC_PS)):
            S.op("dve", P(nc.vector.tensor_copy, rp_a[:], rp_i[:]), reads=[b_rpi], writes=[b_rpa])
            S.op("dve", P(nc.vector.tensor_scalar, out=rp_a[:], in0=rp_a[:], scalar1=vecs[:, VC_IF:VC_IF + 1],
                          scalar2=vecs[:, pcol:pcol + 1], op0=ALU.mult, op1=ALU.add),
                 reads=[b_rpa, b_const], writes=[b_rpa])
            S.op("dve", P(nc.vector.tensor_scalar, out=rp_n[:], in0=rp_a[:], scalar1=1.0 / TWO_PI,
                          scalar2=None, op0=ALU.mult), reads=[b_rpa], writes=[b_rpn])
            S.op("dve", P(nc.vector.tensor_copy, rp_i[:], rp_n[:]), reads=[b_rpn], writes=[b_rpi])
            S.op("dve", P(nc.vector.tensor_copy, rp_n[:], rp_i[:]), reads=[b_rpi], writes=[b_rpn])
            if tbl is cs:
                for g in range(4):
                    S.dma("sp", P(nc.sync.dma_start, out=rp_i[32 * g:32 * g + 32, :],
                                  in_=pos_all[g * 1024:(g + 1) * 1024].partition_broadcast(32)), shared=[b_rpi])
            S.op("dve", P(nc.vector.scalar_tensor_tensor, out=rp_a[:], in0=rp_n[:], scalar=-CW1,
                          in1=rp_a[:], op0=ALU.mult, op1=ALU.add), reads=[b_rpn, b_rpa], writes=[b_rpa])
            S.op("dve", P(nc.vector.scalar_tensor_tensor, out=rp_a[:], in0=rp_n[:], scalar=-CW2,
                          in1=rp_a[:], op0=ALU.mult, op1=ALU.add), reads=[b_rpn, b_rpa], writes=[b_rpa])
            S.op("dve", P(nc.vector.tensor_scalar, out=rp_a[:], in0=rp_a[:], scalar1=-PI_LO, scalar2=PI_LO,
                          op0=ALU.max, op1=ALU.min), reads=[b_rpa], writes=[b_rpa])
            S.op("act", P(nc.scalar.activation, out=rp_o[:], in_=rp_a[:], func=AF.Sin, bias=zero_c[:, :], scale=1.0),
                 reads=[b_rpa, b_const], writes=[b_rpo])
            for g in range(4):
                S.dma("sp", P(nc.sync.dma_start, out=tbl[64:96, g * 1024:(g + 1) * 1024], in_=rp_o[32 * g:32 * g + 32, :]),
                      reads=[b_rpo], shared=[b_cs], chan=b_cs)
        for kc in range(8):
            S.op("dve", P(nc.vector.tensor_scalar, out=w_a[:, kc, :], in0=w_a[:, kc, :],
                          scalar1=vecs[:, VC_NG + kc:VC_NG + kc + 1], scalar2=None, op0=ALU.mult),
                 reads=[b_w0, b_const], shared=[b_w])
        S.barrier()

        R = slice(64, 96)
        ringT = PsumRing(ps, [0, 1])
        ringP = PsumRing(ps, [2, 3, 4, 5, 6, 7])
        evac_i = [0]

        def evac_copy(out_ap, in_ap, reads, writes=(), shared=(), scale=None):
            evac_i[0] += 1
            if evac_i[0] % 2 == 0:
                if scale is None:
                    S.op("act", P(nc.scalar.copy, out=out_ap, in_=in_ap), reads=reads, writes=writes, shared=shared)
                else:
                    S.op("act", P(nc.scalar.mul, out=out_ap, in_=in_ap, mul=scale), reads=reads, writes=writes,
                         shared=shared)
            else:
                if scale is None:
                    S.op("dve", P(nc.vector.tensor_copy, out_ap, in_ap), reads=reads, writes=writes, shared=shared)
                else:
                    S.op("dve", P(nc.vector.tensor_scalar, out=out_ap, in0=in_ap, scalar1=scale, scalar2=None,
                                  op0=ALU.mult), reads=reads, writes=writes, shared=shared)

        ngrp = [0]

        def front1(t):
            tok0 = t * 512
            b_st = b_stl[t % 2]
            c0_ = (t % 2) * 4
            for blk in range(4):
                r0 = tok0 + blk * 128
                S.dma("sp", P(nc.sync.dma_start, out=xb[blk][:], in_=x_all[r0:r0 + 128, :]), writes=[b_xb[blk]])
            for blk in range(4):
                xni, bxn = xn[blk % 2], b_xn[blk % 2]
                S.op("act", P(nc.scalar.activation, out=xni[:], in_=xb[blk][:], func=AF.Square,
                              accum_out=st[:, c0_ + blk:c0_ + blk + 1]), reads=[b_xb[blk]], writes=[bxn], shared=[b_st])
            S.op("act", P(nc.scalar.activation, out=st[:, 8 + c0_:12 + c0_], in_=st[:, c0_:c0_ + 4], func=AF.Ln,
                          bias=eps_c[:, 0:1], scale=1.0 / D), reads=[b_st, b_const], writes=[b_st])
            S.op("act", P(nc.scalar.activation, out=st[:, 8 + c0_:12 + c0_], in_=st[:, 8 + c0_:12 + c0_], func=AF.Exp,
                          scale=-0.5), reads=[b_st], writes=[b_st])

        def front2(t):
            xT, bxT = xnT[t % 2], b_xnT[t % 2]
            b_st = b_stl[t % 2]
            c0_ = (t % 2) * 4
            for blk in range(4):
                xni, bxn = xn[blk % 2], b_xn[blk % 2]
                S.op("dve", P(nc.vector.tensor_scalar, out=xni[:], in0=xb[blk][:],
                              scalar1=st[:, 8 + c0_ + blk:9 + c0_ + blk], scalar2=None, op0=ALU.mult),
                     reads=[b_xb[blk], b_st], writes=[bxn])
                pT, bT = ringT.next()
                pTb = pT.bitcast(BF16)
                for kc in range(8):
                    S.op("pe", P(nc.tensor.transpose, out=pTb[:, kc * 128:(kc + 1) * 128],
                                 in_=xni[:, kc * 128:(kc + 1) * 128], identity=ident[:]),
                         reads=[bxn, b_const], shared=[bT])
                evac_copy(xT[:, :, blk * 128:(blk + 1) * 128], pTb.rearrange("p (k n) -> p k n", k=8),
                          reads=[bT], shared=[bxT])

        front1(0)
        front2(0)
        for t in range(8):
            own = t < 4
            tok0 = t * 512
            xT, bxT = xnT[t % 2], b_xnT[t % 2]
            if t + 1 < 8:
                front1(t + 1)

            def proj(c0, m, xT=xT, bxT=bxT):
                p_, b_ = ringP.next()
                for kc in range(8):
                    S.op("pe", P(nc.tensor.matmul, p_[0:m, :], w_a[:, kc, c0:c0 + m], xT[:, kc, :],
                                 start=(kc == 0), stop=(kc == 7)), reads=[b_w, bxT], shared=[b_])
                return p_, b_

            def norm_group(pbs, nch, gcol, div, dst, bdst):
                gi = ngrp[0] % 2
                ngrp[0] += 1
                sq_, bsq, rs_, brs = sqb[gi], b_sq[gi], rsb[gi], b_rs[gi]
                for i, (p_, b_) in enumerate(pbs):
                    S.op("act", P(nc.scalar.activation, out=sq_[:, i, :], in_=p_, func=AF.Square),
                         reads=[b_], shared=[bsq])
                pss, bss = ringP.next()
                for i in range(nch):
                    S.op("pe", P(nc.tensor.matmul, pss, ones_b[:], sq_[:, i, :], start=(i == 0), stop=(i == nch - 1)),
                         reads=[bsq, b_const], shared=[bss])
                S.op("act", P(nc.scalar.activation, out=rs_[:], in_=pss, func=AF.Ln, bias=eps_c[:, 0:1],
                              scale=1.0 / div), reads=[bss, b_const], writes=[brs])
                S.op("act", P(nc.scalar.activation, out=rs_[:], in_=rs_[:], func=AF.Exp, scale=-0.5),
                     reads=[brs], writes=[brs])
                for i, (p_, b_) in enumerate(pbs):
                    S.op("dve", P(nc.vector.scalar_tensor_tensor, out=dst(i), in0=p_,
                                  scalar=vecs[:, gcol + i:gcol + i + 1], in1=rs_[:], op0=ALU.mult, op1=ALU.mult),
                         reads=[b_, brs, b_const], shared=[bdst])

            kvp = [proj(384 + 128 * i, 128) for i in range(2)]
            pA, bA = proj(640, 96)
            pB, bB = proj(736, 96)
            if t + 1 < 8:
                front2(t + 1)
            S.op("dve", P(nc.vector.tensor_tensor, out=t1[R, :], in0=pA[R, :], in1=cs[R, tok0:tok0 + 512],
                          op=ALU.mult), reads=[bA, b_cs], writes=[b_t1])
            S.op("dve", P(nc.vector.tensor_tensor, out=t2[R, :], in0=pB[R, :], in1=sn[R, tok0:tok0 + 512],
                          op=ALU.mult), reads=[bB, b_cs], writes=[b_t2])
            kr_, bkr = kr[t % 2], b_kr[t % 2]
            S.op("dve", P(nc.vector.tensor_tensor, out=kr_[R, :], in0=t1[R, :], in1=t2[R, :], op=ALU.add),
                 reads=[b_t1, b_t2], writes=[bkr])
            for h in range(8):
                if h % 2 == 0:
                    S.op("act", P(nc.scalar.copy, out=KT[R, h, tok0:tok0 + 512], in_=kr_[R, :]),
                         reads=[bkr], shared=[b_KT[t]])
                else:
                    S.op("dve", P(nc.vector.tensor_copy, KT[R, h, tok0:tok0 + 512], kr_[R, :]),
                         reads=[bkr], shared=[b_KT[t]])
            norm_group(kvp, 2, VC_KG, 256.0, lambda i, tok0=tok0: ckv[:, i, tok0:tok0 + 512], b_ckv[t])
            if own:
                qp = [proj(128 * i, 128) for i in range(3)]
                norm_group(qp, 3, VC_QG, 384.0, lambda i, tok0=tok0: cq[:, i, tok0:tok0 + 512], b_cq[t])
        S.barrier()
        S.op("pool", P(nc.gpsimd.memset, VA[:, :, :, 64:65], 1.0), shared=b_VA)
        ringQ = PsumRing(ps, [0, 1, 2, 3, 4, 5, 6, 7])
        nq = [0]
        for t in range(8):
            own = t < 4
            tok0 = t * 512
            for h in range(8):
                p_, b_ = ringQ.next()
                for kc in range(2):
                    S.op("pe", P(nc.tensor.matmul, p_[0:64, :], w_uk[:, kc, h * 64:(h + 1) * 64],
                                 ckv[:, kc, tok0:tok0 + 512], start=(kc == 0), stop=(kc == 1)),
                         reads=[b_w2, b_ckv[t]], shared=[b_])
                evac_copy(KT[0:64, h, tok0:tok0 + 512], p_[0:64, :], reads=[b_], shared=[b_KT[t]])
            for blk in range(4):
                p_, b_ = ringQ.next()
                for kc in range(2):
                    S.op("pe", P(nc.tensor.matmul, p_, ckv[:, kc, tok0 + blk * 128:tok0 + (blk + 1) * 128],
                                 w_uv[:, kc, :], start=(kc == 0), stop=(kc == 1)),
                         reads=[b_w2, b_ckv[t]], shared=[b_])
                evac_copy(VA[:, t * 4 + blk, :, 0:64], p_.rearrange("p (h d) -> p h d", h=8), reads=[b_],
                          shared=[b_VA[t]])
            if own:
                for h in range(8):
                    pa, ba = ringQ.next()
                    for kc in range(3):
                        S.op("pe", P(nc.tensor.matmul, pa[0:96, :], w_uqa[:, kc, h * 96:(h + 1) * 96],
                                     cq[:, kc, tok0:tok0 + 512], start=(kc == 0), stop=(kc == 2)),
                             reads=[b_w2, b_cq[t]], shared=[ba])
                    pb, bb = ringQ.next()
                    for kc in range(3):
                        S.op("pe", P(nc.tensor.matmul, pb[0:96, :], w_uqb[:, kc, h * 96:(h + 1) * 96],
                                     cq[:, kc, tok0:tok0 + 512], start=(kc == 0), stop=(kc == 2)),
                             reads=[b_w2, b_cq[t]], shared=[bb])
                    qi = nq[0] % 2
                    nq[0] += 1
                    S.op("act", P(nc.scalar.mul, out=QT[0:64, h, tok0:tok0 + 512], in_=pa[0:64, :],
                                  mul=SCALE_MLA), reads=[ba], shared=[b_QT[t]])
                    S.op("dve", P(nc.vector.scalar_tensor_tensor, out=t1q[qi][R, :], in0=pa[R, :], scalar=SCALE_MLA,
                                  in1=cs[R, tok0:tok0 + 512], op0=ALU.mult, op1=ALU.mult),
                         reads=[ba, b_cs], writes=[b_t1q[qi]])
                    S.op("dve", P(nc.vector.scalar_tensor_tensor, out=t2q[qi][R, :], in0=pb[R, :], scalar=SCALE_MLA,
                                  in1=sn[R, tok0:tok0 + 512], op0=ALU.mult, op1=ALU.mult),
                         reads=[bb, b_cs], writes=[b_t2q[qi]])
                    S.op("pool", P(nc.gpsimd.tensor_tensor, out=QT[R, h, tok0:tok0 + 512], in0=t1q[qi][R, :],
                                   in1=t2q[qi][R, :], op=ALU.add),
                         reads=[b_t1q[qi], b_t2q[qi]], shared=[b_QT[t]])
        S.barrier()
        if debug and ("kt" in dbg or "va" in dbg or "qt" in dbg):
            dstage = Arena(nc, 131584, 208768, "DBG").alloc([128, 2080], F32, "dstage")
            b_ds = Buf("dstage")
            if "kt" in dbg:
                for i, h in enumerate((0, 5)):
                    for half in range(2):
                        S.op("dve", P(nc.vector.tensor_copy, dstage[0:96, 0:2048], KT[0:96, h, half * 2048:(half + 1) * 2048]),
                             reads=b_KT, writes=[b_ds])
                        dump("kt", dstage[0:96, 0:2048], [b_ds], dst=dbg["kt"][i, :, half * 2048:(half + 1) * 2048])
            if "va" in dbg:
                S.op("dve", P(nc.vector.tensor_copy, dstage[:, 0:2080], VA[:, 0:4, :, :].rearrange("p a h d -> p (a h d)")),
                     reads=b_VA, writes=[b_ds])
                dump("va", dstage[:, 0:2080], [b_ds])
            if "qt" in dbg:
                S.op("dve", P(nc.vector.tensor_copy, dstage[0:96, 0:2048], QT[0:96, 3, :]), reads=b_QT, writes=[b_ds])
                dump("qt", dstage[0:96, 0:2048], [b_ds])
            S.barrier()

    if "B" in phases:
        B_ = Arena(nc, 164352, 208768, "B")
        maskd = B_.alloc([128, 2, 256], BF16, "maskd")
        maskf = B_.alloc([128, 8, 256], BF16, "maskf")
        NPT = 6
        PT = [B_.alloc([128, 512], BF16, "PT") for _ in range(NPT)]
        b_PT = [Buf("PT%d" % i) for i in range(NPT)]
        o_sb = [B_.alloc([128, 512], F32, "o_sb") for _ in range(2)]
        rden = [B_.alloc([128, 512], F32, "rden") for _ in range(2)]
        b_osb = [Buf("osb0"), Buf("osb1")]
        b_rden = [Buf("rden0"), Buf("rden1")]
        b_mask = Buf("masks")
        S.dma("pool", P(nc.gpsimd.dma_start, out=maskd[:], in_=maskd_in[:, :, :]), shared=[b_mask])
        S.dma("pool", P(nc.gpsimd.dma_start, out=maskf[:], in_=maskf_in[:, :, :]), shared=[b_mask])
        convert_weights()
        ringS = PsumRing(ps, [3, 4, 5, 6, 7])
        b_O = [Buf("O0"), Buf("O1")]
        b_bc = Buf("bc")
        items = []
        for T in range(4):
            for h in range(8):
                blocks = []
                for grp in range(2):
                    for j in range(2 * T + 2):
                        c = j + 8 * grp
                        for kb in range(2):
                            if j < 2 * T:
                                blocks.append((c, kb, 0, 512, None))
                            elif j == 2 * T:
                                m = maskd[:, kb, :] if grp == 0 else maskf[:, j, :]
                                blocks.append((c, kb, 0, 512, (m, 0)))
                            else:
                                m = maskd[:, kb, :] if grp == 0 else maskf[:, j, :]
                                blocks.append((c, kb, 256, 512, (m, 256)))
                for bi, blk in enumerate(blocks):
                    items.append((T, h, blk, bi == 0, bi == len(blocks) - 1))
        N = len(items)
        LOOK = 3
        st_ = {}

        def emit_qk(i):
            T, h, (c, kb, qlo, qhi, mask), first, last = items[i]
            sp_, sb_ = ringS.next()
            kcol = c * 256 + kb * 128
            S.op("pe", P(nc.tensor.matmul, sp_[:, qlo:qhi], KT[0:96, h, kcol:kcol + 128],
                         QT[0:96, h, T * 512 + qlo:T * 512 + qhi], start=True, stop=(mask is None)),
                 reads=[b_KT[c // 2], b_QT[T]], shared=[sb_])
            if mask is not None:
                m, mlo = mask
                S.op("pe", P(nc.tensor.matmul, sp_[:, mlo:mlo + 256], ident[:], m, start=False, stop=True),
                     reads=[b_mask, b_const], shared=[sb_])
            pt = PT[i % NPT]
            S.op("act", P(nc.scalar.activation, out=pt[:, qlo:qhi], in_=sp_[:, qlo:qhi], func=AF.Exp),
                 reads=[sb_], writes=[b_PT[i % NPT]])

        def emit_pv(i):
            T, h, (c, kb, qlo, qhi, mask), first, last = items[i]
            ob = (T * 8 + h) % 2
            O = ps[:, ob, :]
            pt = PT[i % NPT]
            S.op("pe", P(nc.tensor.matmul, O[0:65, qlo:qhi], VA[:, c * 2 + kb, h, :], pt[:, qlo:qhi],
                         start=first, stop=last),
                 reads=[b_VA[c // 2], b_PT[i % NPT]], shared=[b_O[ob]])
            if last:
                S.op("dve", P(nc.vector.tensor_copy, o_sb[ob][0:65, :], O[0:65, :]), reads=[b_O[ob]],
                     writes=[b_osb[ob]])
                S.op("dve", P(nc.vector.reciprocal, out=rden[ob][64:65, :], in_=o_sb[ob][64:65, :]),
                     reads=[b_osb[ob]], writes=[b_rden[ob]])
                pending.append((i + FIN_DELAY, T, h, ob))

        def emit_fin(T, h, ob):
            bc = ps[:, 2, :]
            S.op("pe", P(nc.tensor.matmul, bc[0:64, :], ones_f[64:65, 0:64], rden[ob][64:65, :],
                         start=True, stop=True), reads=[b_rden[ob], b_const], writes=[b_bc])
            S.op("dve", P(nc.vector.tensor_tensor, out=OG[0:64, T, h, :], in0=o_sb[ob][0:64, :],
                          in1=bc[0:64, :], op=ALU.mult), reads=[b_osb[ob], b_bc], shared=[b_OG[T]])

        pending = []
        FIN_DELAY = 7
        for i in range(N + LOOK):
            if i < N:
                emit_qk(i)
            if i >= LOOK:
                emit_pv(i - LOOK)
            while pending and pending[0][0] <= i - LOOK:
                _, T_, h_, ob_ = pending.pop(0)
                emit_fin(T_, h_, ob_)
        for _, T_, h_, ob_ in pending:
            emit_fin(T_, h_, ob_)
        S.barrier()
        if debug and "og" in dbg:
            dstage2 = Arena(nc, 0, 65536, "DBG2").alloc([128, 2048], F32, "dstage2")
            b_ds2 = Buf("dstage2")
            for i, h in enumerate((0, 6)):
                for T_ in range(4):
                    S.op("dve", P(nc.vector.tensor_copy, dstage2[0:64, T_ * 512:(T_ + 1) * 512], OG[0:64, T_, h, :]),
                         reads=b_OG, shared=[b_ds2])
                dump("og", dstage2[0:64, :], [b_ds2], dst=dbg["og"][i])
            S.barrier()

    if "C" in phases:
        C_ = Arena(nc, 0, 131584, "C1")
        C2_ = Arena(nc, 164352, 208768, "C2")
        fgt = C_.alloc([128, D], F32, "fgt")
        KM = C_.alloc([128, 4, 256], BF16, "KM")
        VM = C_.alloc([128, 2, 512], BF16, "VM")
        wh = C_.alloc([128, 124], F32, "wh")
        epsC = C_.alloc([128, 1], F32, "epsC")
        diag = C_.alloc([128, 124, 128], BF16, "diag")
        NWC = 4
        wc, wc_o = [], []
        for i in range(NWC):
            off_i = C_.off
            wc.append(C_.alloc([128, 8, 512], BF16, "wc"))
            wc_o.append(Arena(nc, off_i, off_i + 8192, "WO%d" % i).alloc([128, 4, 1024], BF16, "wo"))
        b_wc = [Buf("wc%d" % i) for i in range(NWC)]
        xbC = [C_.alloc([128, D], F32, "xbC") for _ in range(2)]
        b_xbC = [Buf("xbC0"), Buf("xbC1")]
        xnC = C_.alloc([128, D], BF16, "xnC")
        xo_ = [C_.alloc([128, 8, 512], BF16, "xo0"),
               Arena(nc, 131584, 131584 + 8192, "XO1").alloc([128, 8, 512], BF16, "xo1")]
        xh_ = [C_.alloc([128, 8, 64], BF16, "xh0"), C_.alloc([128, 8, 64], BF16, "xh1")]
        stC = C_.alloc([128, 16], F32, "stC")
        u2 = C_.alloc([128, 4, 2, 288], BF16, "u2")
        accd_off = C_.off
        acc_d = C_.alloc([128, 4, 512], F32, "acc_d")
        mean_sb = C_.alloc([128, 512], F32, "mean")
        rstd_sb = C_.alloc([128, 512], F32, "rstd")
        ug = C_.alloc([128, 4, 512], BF16, "ug")
        merged2 = C_.alloc([128, 8, 512], F32, "merged2")
        tg = [C2_.alloc([128, 512], F32, "tg") for _ in range(2)]
        sgb = [C2_.alloc([128, 512], BF16, "sg") for _ in range(4)]
        sgc = [C2_.alloc([128, 512], BF16, "sgc") for _ in range(4)]
        mbf_off = C2_.off
        xq = C2_.alloc([128, 4, 512], BF16, "xq")
        oxg = C2_.alloc([128, 4, 512], BF16, "oxg")
        merged_bf = Arena(nc, mbf_off, mbf_off + 8192, "MBF").alloc([128, 8, 512], BF16, "merged_bf")
        PTx = [C2_.alloc([128, 512], BF16, "PTx") for _ in range(2)]
        rr = C2_.alloc([128, 512], F32, "rr")
        tmpf = [C2_.alloc([128, 512], F32, "tmpf") for _ in range(2)]
        m2_sb = tmpf[0]
        ogp = C2_.alloc([128, 4, 512], BF16, "ogp")
        outb = [C2_.alloc([128, D], F32, "outb") for _ in range(2)]
        b_outb = [Buf("outb0"), Buf("outb1")]
        xnC2 = C2_.alloc([128, D], BF16, "xnC2")
        b_wres = Buf("wres")
        b_diag = Buf("diag")
        b_kvm = Buf("kvm")
        b_xnC = Buf("xnC")
        xnL, b_xnL = [xnC, xnC2], [b_xnC, Buf("xnC2")]
        b_xn_ = [Buf("xnTC0"), b_OG[0]]
        cur = {}

        def set_cur(T):
            cur["xo"], cur["xh"], cur["b"] = xo_[T % 2], xh_[T % 2], b_xn_[T % 2]
        b_stC = [Buf("stC%d" % i) for i in range(5)]
        b_u2 = Buf("u2")
        b_tg = [Buf("tg0"), Buf("tg1")]
        b_accd = [Buf("accd%d" % i) for i in range(4)]
        b_mean, b_rstd = Buf("mean"), Buf("rstd")
        b_sg = [Buf("sg%d" % i) for i in range(4)]
        b_sgc = [Buf("sgc%d" % i) for i in range(4)]
        b_ug, b_mg2, b_mbf = Buf("ug"), [Buf("mg2_%d" % i) for i in range(8)], Buf("mbf")
        b_xq, b_PTx, b_rr = Buf("xq"), [Buf("PTx0"), Buf("PTx1")], Buf("rr")
        b_tmpf = [Buf("tmpf0"), Buf("tmpf1")]
        b_m2 = b_tmpf[0]
        b_oxg, b_ogp, b_res = Buf("oxg"), Buf("ogp"), Buf("res")
        b_fin = Buf("fin_st")

        S.dma("sp", P(nc.sync.dma_start, out=fgt[:], in_=fg_in.partition_broadcast(128)), shared=[b_wres])
        S.op("dve", P(nc.vector.memset, epsC[:], EPS), shared=[b_wres])
        b_wh = Buf("wh")
        S.op("dve", P(nc.vector.tensor_scalar, out=wh[:], in0=vecs[:, VC_CW:VC_CW + 124], scalar1=0.5, scalar2=None,
                      op0=ALU.mult), reads=[b_const], writes=[b_wh])
        b_diagm = [Buf("diag%d" % i) for i in range(4)]

        def build_diag(mc):
            for i in range(mc * 31, (mc + 1) * 31):
                if i % 2 == 0:
                    S.op("dve", P(nc.vector.tensor_scalar, out=diag[:, i, :], in0=ident[:], scalar1=wh[:, i:i + 1],
                                  scalar2=None, op0=ALU.mult), reads=[b_wh, b_const], shared=[b_diagm[mc]])
                else:
                    S.op("act", P(nc.scalar.mul, out=diag[:, i, :], in_=ident[:], mul=wh[:, i:i + 1]),
                         reads=[b_wh, b_const], shared=[b_diagm[mc]])

        ringC = PsumRing(ps, [0, 1, 2, 3, 4, 5, 6, 7])

        def wsrc(ap):
            return ap.rearrange("(kc p) n -> p kc n", p=128)

        def WC(a):
            return ("w", wc_bf[a // 512], b_wcv[("c", a // 512)])
        names = ["xq", "xg", "cg", "mg", "wxo", "g2a", "g2b", "wmo", "g1a", "g1b", "wco", "g0a", "g0b"]
        srcs = {"val": WC(0), "glu": WC(512), "xq": WC(2560), "xg": WC(3072), "cg": WC(1024), "mg": WC(4608),
                "wxo": ("o", wo_bf[0], b_wcv[("o", 0)]), "g2a": WC(3584), "g2b": WC(4096), "wmo": ("o", wo_bf[1], b_wcv[("o", 1)]),
                "g1a": WC(5120), "g1b": WC(5632), "wco": ("o", wo_bf[2], b_wcv[("o", 2)]), "g0a": WC(1536), "g0b": WC(2048),
                "woa": WC(6144), "wob": WC(6656)}
        chunk_src = [("w", wmkv_bf[0], b_wcv[("m", 0)]), ("w", wmkv_bf[1], b_wcv[("m", 1)])]
        CI = [dict() for _ in range(4)]

        def add_chunk(T, nm):
            CI[T][nm] = len(chunk_src)
            chunk_src.append(srcs[nm])
        add_chunk(0, "val")
        add_chunk(0, "glu")
        for T in range(4):
            for nm in names:
                add_chunk(T, nm)
            if T + 1 < 4:
                add_chunk(T + 1, "val")
                add_chunk(T + 1, "glu")
            add_chunk(T, "woa")
            add_chunk(T, "wob")
        issued = [0]
        consumed = set()

        def prefetch():
            low = 0
            while low in consumed:
                low += 1
            while issued[0] < min(low + NWC, len(chunk_src)):
                i = issued[0]
                kind, src, bsrc = chunk_src[i]
                dst = wc[i % NWC] if kind == "w" else wc_o[i % NWC]
                S.dma("pool", P(nc.gpsimd.dma_start, out=dst[:], in_=src), reads=[bsrc], writes=[b_wc[i % NWC]],
                      chan=b_wc[i % NWC])
                issued[0] += 1

        def chunk(i):
            assert i < issued[0], (i, issued[0])
            kind = chunk_src[i][0]
            return (wc[i % NWC] if kind == "w" else wc_o[i % NWC]), b_wc[i % NWC]

        def done(*idx):
            for i in idx:
                consumed.add(i)
            prefetch()

        prefetch()

        def rstd_chain(ss_ap, out_ap, buf, div):
            S.op("dve", P(nc.vector.tensor_scalar, out=out_ap, in0=ss_ap, scalar1=1.0 / div, scalar2=EPS, op0=ALU.mult,
                          op1=ALU.add), reads=[buf], shared=[buf])
            S.op("act", P(nc.scalar.sqrt, out=out_ap, in_=out_ap), reads=[buf], shared=[buf])
            S.op("dve", P(nc.vector.reciprocal, out=out_ap, in_=out_ap), reads=[buf], shared=[buf])

        b_dst = [None]
        nT_i = [0]

        def norm_T(x_sb, bx, np_, stb, sti, g_col, dst_cols):
            xnC, b_xnC = xnL[sti % 2], b_xnL[sti % 2]
            S.op("act", P(nc.scalar.activation, out=acc_d[:, 2:4, :].rearrange("p a n -> p (a n)")[0:np_, :],
                          in_=x_sb[0:np_, :], func=AF.Square, accum_out=stC[0:np_, sti:sti + 1]),
                 reads=[bx], writes=[b_accd[2], b_accd[3]], shared=[stb])
            rstd_chain(stC[0:np_, sti:sti + 1], stC[0:np_, 8 + sti:9 + sti], stb, float(D))
            S.op("dve", P(nc.vector.tensor_scalar, out=xnC[0:np_, :], in0=x_sb[0:np_, :],
                          scalar1=stC[0:np_, 8 + sti:9 + sti], scalar2=None, op0=ALU.mult), reads=[bx, stb], writes=[b_xnC])
            pT, bT = ringC.next()
            pTb = pT.bitcast(BF16)
            for kc in range(8):
                S.op("pe", P(nc.tensor.transpose, out=pTb[:, kc * np_:(kc + 1) * np_], in_=xnC[0:np_, kc * 128:(kc + 1) * 128],
                             identity=ident[0:np_, 0:np_]), reads=[b_xnC, b_const], shared=[bT])
            nT_i[0] += 1
            for kc in range(8):
                if nT_i[0] % 2 == 0:
                    S.op("dve", P(nc.vector.tensor_scalar, out=dst_cols(kc), in0=pTb[:, kc * np_:(kc + 1) * np_],
                                  scalar1=vecs[:, g_col + kc:g_col + kc + 1], scalar2=None, op0=ALU.mult),
                         reads=[bT, b_const], shared=[b_dst[0]])
                else:
                    S.op("act", P(nc.scalar.mul, out=dst_cols(kc), in_=pTb[:, kc * np_:(kc + 1) * np_],
                                  mul=vecs[:, g_col + kc:g_col + kc + 1]), reads=[bT, b_const], shared=[b_dst[0]])

        def proj(wt, wb, c0, m):
            p_, b_ = ringC.next()
            for kc in range(8):
                S.op("pe", P(nc.tensor.matmul, p_[0:m, :], wt[:, kc, c0:c0 + m], cur["xo"][:, kc, :],
                             start=(kc == 0), stop=(kc == 7)), reads=[wb, cur["b"]], shared=[b_])
            return p_, b_

        memT = Arena(nc, accd_off, accd_off + 4096, "MEMT").alloc([128, 8, 256], BF16, "memT")
        b_memT = Buf("memT")
        b_dst[0] = b_memT
        for mb in range(2):
            S.dma("sp", P(nc.sync.dma_start, out=xbC[mb][:], in_=mem_in[mb * 128:(mb + 1) * 128, :]), writes=[b_xbC[mb]])
            norm_T(xbC[mb], b_xbC[mb], 128, b_stC[mb], mb, VC_MG,
                   lambda kc, mb=mb: memT[:, kc, mb * 128:(mb + 1) * 128])
        wk_t, wk_b = chunk(0)
        for hx in range(4):
            p_, b_ = ringC.next()
            for kc in range(8):
                S.op("pe", P(nc.tensor.matmul, p_[:, 0:256], wk_t[:, kc, hx * 128:(hx + 1) * 128], memT[:, kc, :],
                             start=(kc == 0), stop=(kc == 7)), reads=[wk_b, b_memT, b_accd[0], b_accd[1]], shared=[b_])
            S.op("dve", P(nc.vector.tensor_copy, KM[:, hx, :], p_[:, 0:256]), reads=[b_], shared=[b_kvm])
        done(0)
        wv_t, wv_b = chunk(1)
        for mb in range(2):
            p_, b_ = ringC.next()
            for kc in range(8):
                S.op("pe", P(nc.tensor.matmul, p_, memT[:, kc, mb * 128:(mb + 1) * 128], wv_t[:, kc, :],
                             start=(kc == 0), stop=(kc == 7)), reads=[wv_b, b_memT, b_accd[0], b_accd[1]], shared=[b_])
            S.op("act", P(nc.scalar.copy, out=VM[:, mb, :], in_=p_), reads=[b_], shared=[b_kvm])
        done(1)

        def gate_merge_steps(cg, co, bias_col, rhs_t, rhs_b, mode):
            def step(fo):
                wo_t, wo_b = chunk(co)
                wt, wb = chunk(cg + fo // 4)
                py, by = ringC.next()
                for kc in range(4):
                    S.op("pe", P(nc.tensor.matmul, py, wo_t[:, kc, fo * 128:(fo + 1) * 128], rhs_t[:, kc, :],
                                 start=(kc == 0), stop=(kc == 3)), reads=[wo_b, rhs_b], shared=[by])
                pz, bz = proj(wt, wb, (fo % 4) * 128, 128)
                tt, tb = tg[fo % 2], b_tg[fo % 2]
                S.op("act", P(nc.scalar.activation, out=tt[:], in_=pz, func=AF.Tanh,
                              bias=hb[:, bias_col + fo:bias_col + fo + 1], scale=0.5), reads=[bz, b_hb], writes=[tb])
                if mode == "first":
                    S.op("dve", P(nc.vector.scalar_tensor_tensor, out=merged2[:, fo, :], in0=tt[:], scalar=1.0, in1=py,
                                  op0=ALU.add, op1=ALU.mult), reads=[tb, by], writes=[b_mg2[fo]])
                else:
                    tf, tfb = tmpf[fo % 2], b_tmpf[fo % 2]
                    S.op("dve", P(nc.vector.scalar_tensor_tensor, out=tf[:], in0=tt[:], scalar=1.0, in1=py,
                                  op0=ALU.add, op1=ALU.mult), reads=[tb, by], writes=[tfb])
                    if mode == "mid":
                        S.op("dve", P(nc.vector.tensor_tensor, out=merged2[:, fo, :], in0=merged2[:, fo, :], in1=tf[:],
                                       op=ALU.add), reads=[tfb, b_mg2[fo]], writes=[b_mg2[fo]])
                    else:
                        S.op("dve", P(nc.vector.tensor_tensor, out=merged_bf[:, fo, :], in0=merged2[:, fo, :], in1=tf[:],
                                       op=ALU.add), reads=[tfb, b_mg2[fo]], shared=[b_mbf, b_xq, b_oxg])
                if fo == 3:
                    done(cg)
                if fo == 7:
                    done(cg + 1, co)
            return [P(step, fo) for fo in range(8)]

        def run_steps(a, b=(), ratio=1):
            a, b = list(a), list(b)
            ia = ib = 0
            while ia < len(a) or ib < len(b):
                for _ in range(ratio):
                    if ia < len(a):
                        a[ia]()
                        ia += 1
                if ib < len(b):
                    b[ib]()
                    ib += 1

        def c0_blk(T, b):
            if b == 0:
                return 64, x_ext[T, 0:64, :], (lambda kc, T=T: xh_[T % 2][:, kc, :])
            return 128, x_ext[T, 64 + (b - 1) * 128:64 + b * 128, :], \
                (lambda kc, T=T, b=b: xo_[T % 2][:, kc, (b - 1) * 128:b * 128])

        junk_f = acc_d[:, 2:4, :].rearrange("p a n -> p (a n)")

        def c0_p1(T, b):
            np_, src, _ = c0_blk(T, b)
            xs, bxs, stb, sti = xbC[b % 2], b_xbC[b % 2], b_stC[b], b
            xnC, b_xnC = xnL[b % 2], b_xnL[b % 2]
            S.dma("sp", P(nc.sync.dma_start, out=xs[0:np_, :], in_=src), writes=[bxs])
            if T == 0:
                S.op("act", P(nc.scalar.activation, out=junk_f[0:np_, :], in_=xs[0:np_, :], func=AF.Square,
                              accum_out=stC[0:np_, sti:sti + 1]), reads=[bxs], writes=[b_accd[2], b_accd[3]], shared=[stb])
            else:
                S.op("act", P(nc.scalar.activation, out=xnC[0:np_, :], in_=xs[0:np_, :], func=AF.Square,
                              accum_out=stC[0:np_, sti:sti + 1]), reads=[bxs], writes=[b_xnC], shared=[stb])
            rstd_chain(stC[0:np_, sti:sti + 1], stC[0:np_, 8 + sti:9 + sti], stb, float(D))
            S.op("dve", P(nc.vector.tensor_scalar, out=xnC[0:np_, :], in0=xs[0:np_, :],
                          scalar1=stC[0:np_, 8 + sti:9 + sti], scalar2=None, op0=ALU.mult), reads=[bxs, stb], writes=[b_xnC])

        def c0_p2(T, b):
            np_, _, dst_cols = c0_blk(T, b)
            bdst = b_xn_[T % 2]
            xnC, b_xnC = xnL[b % 2], b_xnL[b % 2]
            pT, bT = ringC.next()
            pTb = pT.bitcast(BF16)
            for kc in range(8):
                S.op("pe", P(nc.tensor.transpose, out=pTb[:, kc * np_:(kc + 1) * np_], in_=xnC[0:np_, kc * 128:(kc + 1) * 128],
                             identity=ident[0:np_, 0:np_]), reads=[b_xnC, b_const], shared=[bT])
            nT_i[0] += 1
            for kc in range(8):
                if nT_i[0] % 2 == 0:
                    S.op("dve", P(nc.vector.tensor_scalar, out=dst_cols(kc), in0=pTb[:, kc * np_:(kc + 1) * np_],
                                  scalar1=vecs[:, VC_NG + kc:VC_NG + kc + 1], scalar2=None, op0=ALU.mult),
                         reads=[bT, b_const], shared=[bdst])
                else:
                    S.op("act", P(nc.scalar.mul, out=dst_cols(kc), in_=pTb[:, kc * np_:(kc + 1) * np_],
                                  mul=vecs[:, VC_NG + kc:VC_NG + kc + 1]), reads=[bT, b_const], shared=[bdst])

        def hoist(T, step):
            if T + 1 >= 4 or CSTOP < 5:
                return
            if 2 <= step <= 6:
                c0_p2(T + 1, step - 2)
            if step <= 4:
                c0_p1(T + 1, step)

        def x_reload(T, blk):
            xs, bxs = xbC[blk % 2], b_xbC[blk % 2]
            S.dma("sp", P(nc.sync.dma_start, out=xs[:], in_=x_ext[T, 64 + blk * 128:64 + (blk + 1) * 128, :]),
                  writes=[bxs])

        def s2(T):
            set_cur(T)
            wval, bval = chunk(CI[T]["val"])
            wglu, bglu = chunk(CI[T]["glu"])
            ph, bh = ringC.next()
            for gi, (wt, wb) in enumerate(((wval, bval), (wglu, bglu))):
                for mc in range(4):
                    for kc in range(8):
                        S.op("pe", P(nc.tensor.matmul, ph[:, gi * 256 + mc * 64:gi * 256 + (mc + 1) * 64],
                                     wt[:, kc, mc * 128:(mc + 1) * 128], cur["xh"][:, kc, :], start=(kc == 0), stop=(kc == 7)),
                             reads=[wb, cur["b"]], shared=[bh])
            S.op("act", P(nc.scalar.activation, out=tg[0][:, 0:256], in_=ph[:, 256:512], func=AF.Tanh, scale=0.5),
                 reads=[bh], writes=[b_tg[0]])
            for mc in range(4):
                S.op("dve", P(nc.vector.scalar_tensor_tensor, out=u2[:, mc, :, 0:32],
                              in0=tg[0][:, mc * 64:(mc + 1) * 64].rearrange("p (c i) -> p c i", c=2), scalar=1.0,
                              in1=ph[:, mc * 64:(mc + 1) * 64].rearrange("p (c i) -> p c i", c=2),
                              op0=ALU.add, op1=ALU.mult), reads=[b_tg[0], bh], shared=[b_u2])
            for mc in range(4):
                pv, bv = proj(wval, bval, mc * 128, 128)
                pg, bg = proj(wglu, bglu, mc * 128, 128)
                tt, tb = tg[(mc + 1) % 2], b_tg[(mc + 1) % 2]
                S.op("act", P(nc.scalar.activation, out=tt[:], in_=pg, func=AF.Tanh, scale=0.5), reads=[bg], writes=[tb])
                S.op("dve", P(nc.vector.scalar_tensor_tensor, out=u2[:, mc, :, 32:288],
                              in0=tt[:].rearrange("p (c i) -> p c i", c=2), scalar=1.0,
                              in1=pv.rearrange("p (c i) -> p c i", c=2), op0=ALU.add, op1=ALU.mult),
                     reads=[tb, bv], shared=[b_u2])
            done(CI[T]["val"], CI[T]["glu"])


        for T in range(4 if CSTOP >= 5 else (1 if CSTOP >= 1 else 0)):
            set_cur(T)
            if T == 0:
                c0_p1(0, 0)
                for b in range(5):
                    if b + 1 < 5:
                        c0_p1(0, b + 1)
                    c0_p2(0, b)
            S.dma("sp", P(nc.sync.dma_start, out=ogp[0:64, :, :], in_=OG[0:64, T, 0:8:2, :]),
                  reads=[b_OG[T]], shared=[b_ogp], chan=b_ogp)
            S.dma("sp", P(nc.sync.dma_start, out=ogp[64:128, :, :], in_=OG[0:64, T, 1:8:2, :]),
                  reads=[b_OG[T]], shared=[b_ogp], chan=b_ogp)
            if CSTOP < 2:
                continue
            if T == 0:
                s2(0)
            set_cur(T)
            wxq, bxq = chunk(CI[T]["xq"])
            for hx in range(4):
                p_, b_ = proj(wxq, bxq, hx * 128, 128)
                S.op("act", P(nc.scalar.mul, out=xq[:, hx, :], in_=p_, mul=SCALE_X), reads=[b_], shared=[b_xq])
            done(CI[T]["xq"])
            wxg, bxg = chunk(CI[T]["xg"])
            for hx in range(4):
                p_, b_ = proj(wxg, bxg, hx * 128, 128)
                S.op("act", P(nc.scalar.activation, out=sgb[hx][:], in_=p_, func=AF.Silu), reads=[b_], writes=[b_sg[hx]])
            done(CI[T]["xg"])
            hoist(T, 0)
            wcg, bcg = chunk(CI[T]["cg"])
            for mc in range(4):
                pc, bc_ = proj(wcg, bcg, mc * 128, 128)
                S.op("act", P(nc.scalar.activation, out=sgc[mc][:], in_=pc, func=AF.Silu), reads=[bc_], writes=[b_sgc[mc]])
            done(CI[T]["cg"])
            wmg, bmg = chunk(CI[T]["mg"])
            for mc in range(4):
                p_, b_ = proj(wmg, bmg, mc * 128, 128)
                tf, tfb = tmpf[mc % 2], b_tmpf[mc % 2]
                S.op("act", P(nc.scalar.activation, out=tf[:], in_=p_, func=AF.Silu), reads=[b_], writes=[tfb])
                S.op("dve", P(nc.vector.tensor_tensor, out=ogp[:, mc, :], in0=ogp[:, mc, :], in1=tf[:], op=ALU.mult),
                     reads=[b_ogp, tfb], writes=[b_ogp])
            done(CI[T]["mg"])
            hoist(T, 1)

            def conv_step(mc, T=T):
                if T == 0 and mc == 0:
                    build_diag(0)
                if T == 0 and mc + 1 < 4:
                    build_diag(mc + 1)
                pcv, bcv = ringC.next()
                for k in range(31):
                    S.op("pe", P(nc.tensor.matmul, pcv.rearrange("p (c i) -> p c i", c=2), diag[:, mc * 31 + k, :],
                                 u2[:, mc, :, 2 + k:2 + k + 256], start=(k == 0), stop=(k == 30)),
                         reads=[b_diagm[mc], b_u2], shared=[bcv])
                S.op("act", P(nc.scalar.activation, out=acc_d[:, mc, :], in_=pcv, func=AF.Identity,
                              bias=vecs[:, VC_CB + mc:VC_CB + mc + 1], scale=1.0), reads=[bcv, b_const],
                     writes=[b_accd[mc]])

            def cross_step(hx):
                po, bo = ringC.next()
                pd, bd = ringC.next()
                for mb in range(2):
                    psx, bsx = ringC.next()
                    S.op("pe", P(nc.tensor.matmul, psx, KM[:, hx, mb * 128:(mb + 1) * 128], xq[:, hx, :], start=True,
                                 stop=True), reads=[b_kvm, b_xq], shared=[bsx])
                    S.op("act", P(nc.scalar.activation, out=PTx[mb][:], in_=psx, func=AF.Exp), reads=[bsx],
                         writes=[b_PTx[mb]])
                    S.op("pe", P(nc.tensor.matmul, po, VM[:, mb, hx * 128:(hx + 1) * 128], PTx[mb][:], start=(mb == 0),
                                 stop=(mb == 1)), reads=[b_kvm, b_PTx[mb]], shared=[bo])
                    S.op("pe", P(nc.tensor.matmul, pd, ones_b[:], PTx[mb][:], start=(mb == 0), stop=(mb == 1)),
                         reads=[b_const, b_PTx[mb]], shared=[bd])
                S.op("dve", P(nc.vector.reciprocal, out=rr[:], in_=pd), reads=[bd], writes=[b_rr])
                tf, tfb = tmpf[hx % 2], b_tmpf[hx % 2]
                S.op("dve", P(nc.vector.tensor_tensor, out=tf[:], in0=po, in1=rr[:], op=ALU.mult), reads=[bo, b_rr],
                     writes=[tfb])
                S.op("dve", P(nc.vector.tensor_tensor, out=oxg[:, hx, :], in0=tf[:], in1=sgb[hx][:], op=ALU.mult),
                     reads=[tfb, b_sg[hx]], shared=[b_oxg])

            if T == 0:
                run_steps([P(cross_step, hx) for hx in range(4)], [P(conv_step, mc) for mc in range(4)])
            else:
                run_steps([P(conv_step, mc) for mc in range(4)], [P(cross_step, hx) for hx in range(4)])
            if T == 0:
                dump("cconv", acc_d[:].rearrange("p a n -> p (a n)"), b_accd)
            hoist(T, 2)
            p1, b1 = ringC.next()
            for mc in range(4):
                S.op("pe", P(nc.tensor.matmul, p1, ones_f[:], acc_d[:, mc, :], start=(mc == 0), stop=(mc == 3)),
                     reads=[b_accd[mc], b_const], shared=[b1])
            p2, b2 = ringC.next()
            for mc in range(4):
                tf, tfb = tmpf[mc % 2], b_tmpf[mc % 2]
                S.op("act", P(nc.scalar.activation, out=tf[:], in_=acc_d[:, mc, :], func=AF.Square),
                     reads=[b_accd[mc]], writes=[tfb])
                S.op("pe", P(nc.tensor.matmul, p2, ones_f[:], tf[:], start=(mc == 0), stop=(mc == 3)),
                     reads=[tfb, b_const], shared=[b2])
            S.op("dve", P(nc.vector.tensor_scalar, out=mean_sb[:], in0=p1, scalar1=1.0 / 512, scalar2=None, op0=ALU.mult),
                 reads=[b1], writes=[b_mean])
            S.op("dve", P(nc.vector.tensor_tensor, out=rr[:], in0=mean_sb[:], in1=mean_sb[:], op=ALU.mult),
                 reads=[b_mean], writes=[b_rr])
            S.op("dve", P(nc.vector.scalar_tensor_tensor, out=rstd_sb[:], in0=p2, scalar=1.0 / 512, in1=rr[:],
                          op0=ALU.mult, op1=ALU.subtract), reads=[b2, b_rr], writes=[b_rstd])
            S.op("act", P(nc.scalar.activation, out=rstd_sb[:], in_=rstd_sb[:], func=AF.Sqrt, bias=epsC[:, 0:1], scale=1.0),
                 reads=[b_rstd, b_wres], writes=[b_rstd])
            S.op("dve", P(nc.vector.reciprocal, out=rstd_sb[:], in_=rstd_sb[:]), reads=[b_rstd], writes=[b_rstd])

            def ln_step(mc):
                S.op("dve", P(nc.vector.tensor_tensor, out=acc_d[:, mc, :], in0=acc_d[:, mc, :], in1=mean_sb[:],
                              op=ALU.subtract), reads=[b_accd[mc], b_mean], writes=[b_accd[mc]])
                S.op("dve", P(nc.vector.tensor_tensor, out=acc_d[:, mc, :], in0=acc_d[:, mc, :], in1=rstd_sb[:],
                               op=ALU.mult), reads=[b_accd[mc], b_rstd], writes=[b_accd[mc]])
                S.op("act", P(nc.scalar.activation, out=acc_d[:, mc, :], in_=acc_d[:, mc, :], func=AF.Silu,
                              bias=vecs[:, VC_LB + mc:VC_LB + mc + 1], scale=vecs[:, VC_LG + mc:VC_LG + mc + 1]),
                     reads=[b_accd[mc], b_const], writes=[b_accd[mc]])
                S.op("dve", P(nc.vector.tensor_tensor, out=ug[:, mc, :], in0=acc_d[:, mc, :], in1=sgc[mc][:], op=ALU.mult),
                     reads=[b_accd[mc], b_sgc[mc]], shared=[b_ug])

            run_steps(gate_merge_steps(CI[T]["g2a"], CI[T]["wxo"], 16, oxg, b_oxg, "first"),
                      [P(ln_step, mc) for mc in range(4)], ratio=2)
            if T == 0:
                dump("m1", merged2[:].rearrange("p a n -> p (a n)"), b_mg2)
            hoist(T, 3)
            st10 = gate_merge_steps(CI[T]["g1a"], CI[T]["wmo"], 8, ogp, b_ogp, "mid")
            run_steps(st10[0:4])
            hoist(T, 4)
            run_steps(st10[4:8])
            hoist(T, 5)
            if CSTOP >= 5:
                x_reload(T, 0)
                x_reload(T, 1)
            st11 = gate_merge_steps(CI[T]["g0a"], CI[T]["wco"], 0, ug, b_ug, "last")
            run_steps(st11[0:4])
            hoist(T, 6)
            run_steps(st11[4:8])
            if CSTOP < 5:
                continue
            if T + 1 < 4:
                s2(T + 1)
            wo = [chunk(CI[T]["woa"]), chunk(CI[T]["wob"])]
            for blk in range(4):
                xs, bxs = xbC[blk % 2], b_xbC[blk % 2]
                ob, bob = outb[blk % 2], b_outb[blk % 2]
                for half in range(2):
                    pf, bf = ringC.next()
                    wt, wb = wo[half]
                    for kc in range(8):
                        S.op("pe", P(nc.tensor.matmul, pf, merged_bf[:, kc, blk * 128:(blk + 1) * 128], wt[:, kc, :],
                                     start=(kc == 0), stop=(kc == 7)), reads=[b_mbf, b_xq, b_oxg, wb], shared=[bf])
                    S.op("dve", P(nc.vector.scalar_tensor_tensor, out=ob[:, half * 512:(half + 1) * 512], in0=pf, scalar=0.5,
                                  in1=xs[:, half * 512:(half + 1) * 512], op0=ALU.mult, op1=ALU.add),
                         reads=[bf, bxs], shared=[bob])
                if blk + 2 < 4:
                    x_reload(T, blk + 2)
                sti = 5 + blk % 2
                S.op("act", P(nc.scalar.activation, out=acc_d[:, 0:2, :].rearrange("p a n -> p (a n)"), in_=ob[:],
                              func=AF.Square, accum_out=stC[:, sti:sti + 1]), reads=[bob],
                     writes=[b_accd[0], b_accd[1]], shared=[b_fin])
                rstd_chain(stC[:, sti:sti + 1], stC[:, 8 + sti:9 + sti], b_fin, float(D))
                S.op("dve", P(nc.vector.scalar_tensor_tensor, out=ob[:], in0=ob[:], scalar=stC[:, 8 + sti:9 + sti],
                              in1=fgt[:], op0=ALU.mult, op1=ALU.mult), reads=[b_fin, b_wres, bob], writes=[bob])
                r0 = T * 512 + blk * 128
                S.dma("sp", P(nc.sync.dma_start, out=out[r0:r0 + 128, :], in_=ob[:]), reads=[bob], chan=bob, is_out=True)
            done(CI[T]["woa"], CI[T]["wob"])

    nwait = S.emit(nc, ES)
    return nc, nwait


ES = None


def make_inputs(c, x, mem, positions, norm_g, w_in, b_gate, conv_w, conv_b, conv_ln_g, conv_ln_b, w_conv_o,
                q_norm_g, w_uq, kv_norm_g, w_ukv, w_mla_o, mem_norm_g, w_mem_kv, w_x_o, w_out, final_norm_g,
                shared):
    b, p = c // 2, c % 2
    own, oth = OWN[p], OTH[p]
    order = own + oth
    xb = x[b]
    x_all = np.concatenate([xb[g * CH:(g + 1) * CH] for g in order], axis=0)
    pos_all = np.concatenate([positions[b, g * CH:(g + 1) * CH] for g in order], axis=0).astype(np.int32)
    x_ext = np.zeros((4, 576, D), np.float32)
    for t in range(4):
        for jj in range(2):
            g = own[2 * t + jj]
            if g > 0:
                x_ext[t, jj * 32:(jj + 1) * 32] = xb[g * CH - 32:g * CH]
            x_ext[t, 64 + jj * 256:64 + (jj + 1) * 256] = xb[g * CH:(g + 1) * CH]
    maskf = np.zeros((128, 8, 256), np.float32)
    for j in range(8):
        if not (oth[j] < own[j]):
            maskf[:, j, :] = NEG
    d = dict(shared)
    d.update(x_all=np.ascontiguousarray(x_all), x_ext=x_ext, pos_all=pos_all,
             mem=np.ascontiguousarray(mem[b]), maskf=maskf)
    return d


def make_shared(norm_g, w_in, b_gate, conv_w, conv_b, conv_ln_g, conv_ln_b, w_conv_o, q_norm_g, w_uq, kv_norm_g,
                w_ukv, w_mla_o, mem_norm_g, w_mem_kv, w_x_o, w_out, final_norm_g):
    f = np.float32
    w_in0 = w_in[0]
    vecs = np.zeros((128, NV), f)
    vecs[:, VC_BG:VC_BG + 24] = b_gate[0].reshape(24, 128).T
    vecs[:, VC_CB:VC_CB + 4] = conv_b[0].reshape(4, 128).T
    vecs[:, VC_LG:VC_LG + 4] = conv_ln_g[0].reshape(4, 128).T
    vecs[:, VC_LB:VC_LB + 4] = conv_ln_b[0].reshape(4, 128).T
    vecs[:, VC_QG:VC_QG + 3] = q_norm_g[0].reshape(3, 128).T
    vecs[:, VC_KG:VC_KG + 2] = kv_norm_g[0].reshape(2, 128).T
    inv_freq = (10000.0 ** (-np.arange(0, 32, 2, dtype=np.float32) / 32)).astype(f)
    vecs[:, VC_IF] = np.tile(inv_freq, 8)
    vecs[:, VC_PC] = np.pi / 2
    vecs[:, VC_PS] = np.tile(np.concatenate([np.full(16, np.pi), np.zeros(16)]), 4)
    vecs[:, VC_NG:VC_NG + 8] = norm_g[0].reshape(8, 128).T
    vecs[:, VC_MG:VC_MG + 8] = mem_norm_g[0].reshape(8, 128).T
    vecs[:, VC_CW:VC_CW + 124] = conv_w[0].T.reshape(4, 128, 31).transpose(1, 0, 2).reshape(128, 124)
    rope = np.arange(2176, 2208)
    rope_sw = np.concatenate([rope[16:], rope[:16]])
    junk = np.arange(1920, 1984)
    cols_a = np.concatenate([np.arange(1536, 1920), np.arange(1920, 2176), junk, rope, junk, rope_sw])
    w_a = np.ascontiguousarray(w_in0[:, cols_a])
    uq = w_uq[0]
    cols_b = []
    for h in range(8):
        base = h * 96
        cols_b += list(range(base, base + 64)) + list(range(base + 80, base + 96)) + list(range(base + 64, base + 80))
    w_uqb = np.ascontiguousarray(uq[:, cols_b])
    ukv = w_ukv[0].reshape(256, 8, 128)
    w_uk = np.ascontiguousarray(ukv[:, :, :64].reshape(256, 512))
    w_uv = np.ascontiguousarray(ukv[:, :, 64:].reshape(256, 512))
    seg = lambda a, n: np.arange(a, a + n)
    cols_c = np.concatenate([seg(0, 512), seg(512, 512), seg(1024, 512), seg(3744, 1024), seg(2720, 512),
                             seg(3232, 512), seg(5792, 1024), seg(2208, 512), seg(4768, 1024)])
    w_c = np.ascontiguousarray(np.concatenate([w_in0[:, cols_c], w_out[0]], axis=1))
    maskd = np.zeros((128, 2, 256), f)
    pp = np.arange(128)[:, None]
    qq = np.arange(256)[None, :]
    for kb in range(2):
        maskd[:, kb, :] = np.where(qq >= kb * 128 + pp, 0.0, NEG)
    return dict(vecs=vecs, norm_g=np.ascontiguousarray(norm_g[0]), final_g=np.ascontiguousarray(final_norm_g),
                mem_g=np.ascontiguousarray(mem_norm_g[0]), ident=np.eye(128, dtype=f), maskd=maskd,
                w_a=w_a, w_uqa=np.ascontiguousarray(uq), w_uqb=w_uqb, w_uk=w_uk, w_uv=w_uv, w_c=w_c,
                w_conv_o=np.ascontiguousarray(w_conv_o[0]), w_mla_o=np.ascontiguousarray(w_mla_o[0]),
                w_x_o=np.ascontiguousarray(w_x_o[0]), w_out=np.ascontiguousarray(w_out[0]),
                w_mkv=np.ascontiguousarray(w_mem_kv[0]))


def run(inputs, debug=None, phases="ABC", cores=None, trace=False):
    global ES
    inputs = {k: np.asarray(v) for k, v in inputs.items()}
    with contextlib.ExitStack() as es:
        ES = es
        nc, nwait = build_program(debug=debug, phases=phases)
        wkeys = ["norm_g", "w_in", "b_gate", "conv_w", "conv_b", "conv_ln_g", "conv_ln_b", "w_conv_o", "q_norm_g",
                 "w_uq", "kv_norm_g", "w_ukv", "w_mla_o", "mem_norm_g", "w_mem_kv", "w_x_o", "w_out", "final_norm_g"]
        shared = make_shared(**{k: inputs[k] for k in wkeys})
        cores = list(range(NCORES)) if cores is None else cores
        in_maps = [make_inputs(c, shared=shared, **inputs) for c in cores]
        res = run_bass_kernel_spmd(nc, in_maps, core_ids=list(range(len(cores))), **({"trace": True} if trace else {}))
    return res


def kernel(**inputs):
    res = run(inputs)
    x = np.asarray(inputs["x"])
    outp = np.zeros(x.shape, np.float32)
    for c in range(NCORES):
        b, p = c // 2, c % 2
        o = res.results[c]["out"]
        for j, g in enumerate(OWN[p]):
            outp[b, g * CH:(g + 1) * CH] = o[j * CH:(j + 1) * CH]
    return outp
```
